# Optimizing a Trainium2 kernel written in Bass

```python
import functools
import jax
import jax.numpy as jnp
from jax import lax
import numpy as np

D_MODEL = 2048
BATCH = 8
SEQ = 2048
DEPTH = 2

CTX_LEN = 256
GRID_W = 64
N_EVEN = (DEPTH + 1) // 2
N_ODD = DEPTH // 2
NORM_EPS = 1e-6
F32 = jnp.float32

ATT_HEADS = 8
ATT_KV_HEADS = 2
ATT_GROUP = ATT_HEADS // ATT_KV_HEADS
HEAD_DIM = 128
ATT_WIDTH = ATT_HEADS * HEAD_DIM
KV_WIDTH = ATT_KV_HEADS * HEAD_DIM
ROPE_BASE = 10000.0
Q_BLOCK = 128
CONV_WIDTH = D_MODEL // 2
CONV_KERNEL = 31
AB_IN = ATT_WIDTH + 2 * KV_WIDTH + 2 * CONV_WIDTH
AB_OUT = ATT_WIDTH + CONV_WIDTH
AB_SPLITS = [ATT_WIDTH, ATT_WIDTH + KV_WIDTH, ATT_WIDTH + 2 * KV_WIDTH]
RWKV_HEADS = 16
RWKV_HEAD_DIM = 64
RWKV_WIDTH = RWKV_HEADS * RWKV_HEAD_DIM
DECAY_LORA = 64
AAA_LORA = 64
GATE_LORA = 160
RWKV_IN = 3 * RWKV_WIDTH + DECAY_LORA + AAA_LORA + GATE_LORA
RWKV_SPLITS = [RWKV_WIDTH, 2 * RWKV_WIDTH, 3 * RWKV_WIDTH, 3 * RWKV_WIDTH + DECAY_LORA,
               3 * RWKV_WIDTH + DECAY_LORA + AAA_LORA]
RWKV_GN_EPS = 64e-5
RET_HEADS = 8
RET_QK_DIM = 128
RET_V_DIM = 256
RET_QK_WIDTH = RET_HEADS * RET_QK_DIM
RET_V_WIDTH = RET_HEADS * RET_V_DIM
RET_CHUNK = 128
CD_IN = RWKV_IN + 2 * RET_QK_WIDTH + 2 * RET_V_WIDTH
CD_OUT = RWKV_WIDTH + RET_V_WIDTH
CD_SPLITS = [RWKV_IN, RWKV_IN + RET_QK_WIDTH, RWKV_IN + 2 * RET_QK_WIDTH,
             RWKV_IN + 2 * RET_QK_WIDTH + RET_V_WIDTH]
FFN_DIM = 5632
FFN_CONV = 3

kernel_name = 'hybrid_diffusion_trunk'


def rms_norm(x, g):
    xf = x.astype(F32)
    y = xf * lax.rsqrt(jnp.mean(xf * xf, axis=-1, keepdims=True) + NORM_EPS)
    return (y * g.astype(F32)).astype(x.dtype)


def layer_norm(x, g, b, eps):
    xf = x.astype(F32)
    xc = xf - jnp.mean(xf, axis=-1, keepdims=True)
    var = jnp.mean(xc * xc, axis=-1, keepdims=True)
    return (xc * lax.rsqrt(var + eps) * g.astype(F32) + b.astype(F32)).astype(x.dtype)


def head_group_norm(x, g, b, eps):
    B, T, H, N = x.shape
    return layer_norm(x, g.reshape(H, N), b.reshape(H, N), eps).reshape(B, T, H * N)


def modulate(x, shift, scale):
    return x * (1 + scale) + shift


def dwconv(x, w, b):
    K = w.shape[0]
    y = lax.conv_general_dilated(x, w[:, None, :].astype(x.dtype), window_strides=(1,),
                                 padding=[(K // 2, K // 2)],
                                 dimension_numbers=('NWC', 'WIO', 'NWC'),
                                 feature_group_count=x.shape[-1])
    return y + b.astype(x.dtype)


def token_shift(z, mu):
    z_prev = jnp.pad(z, ((0, 0), (1, 0), (0, 0)))[:, :-1]
    z_next = jnp.pad(z, ((0, 0), (0, 1), (0, 0)))[:, 1:]
    return z + mu[0] * (z_prev - z) + mu[1] * (z_next - z)


def axial_rope_tables(n_tok, head_dim):
    rows = n_tok // GRID_W
    row = jnp.repeat(jnp.arange(rows, dtype=F32), GRID_W)
    col = jnp.tile(jnp.arange(GRID_W, dtype=F32), rows)
    axis_dim = head_dim // 2
    inv_freq = ROPE_BASE ** (-jnp.arange(0, axis_dim, 2, dtype=F32) / axis_dim)
    ang = jnp.concatenate([row[:, None] * inv_freq, col[:, None] * inv_freq], axis=-1)
    return jnp.cos(ang), jnp.sin(ang)


def apply_axial_rope(x, cos, sin):
    B, T, H, d = x.shape
    q = d // 4
    xf = x.astype(F32).reshape(B, T, H, 2, 2 * q)
    x1, x2 = xf[..., :q], xf[..., q:]
    c = cos.reshape(T, 2, q)[None, :, None]
    s = sin.reshape(T, 2, q)[None, :, None]
    out = jnp.concatenate([x1 * c - x2 * s, x1 * s + x2 * c], axis=-1)
    return out.reshape(B, T, H, d).astype(x.dtype)


def block_attention(q, k, v):
    B, T, Hk, G, d = q.shape
    nb = T // Q_BLOCK
    qb = jnp.moveaxis(q.reshape(B, nb, Q_BLOCK, Hk, G, d), 1, 0)
    scale = d ** -0.5

    def one_block(qi):
        s = jnp.einsum('bqhgd,bnhd->bhgqn', qi, k, preferred_element_type=F32) * scale
        p = jax.nn.softmax(s, axis=-1).astype(v.dtype)
        return jnp.einsum('bhgqn,bnhd->bqhgd', p, v)

    o = lax.map(one_block, qb)
    return jnp.moveaxis(o, 0, 1).reshape(B, T, Hk, G, d)


def conv_ffn(u, w_up, conv_w, conv_b, w_down):
    gate, val = jnp.split(u @ w_up, 2, axis=-1)
    gate = dwconv(gate, conv_w, conv_b)
    return (jax.nn.silu(gate) * val) @ w_down


def mixer_ab(u_lat, u_ctx, w_in, q_norm_g, k_norm_g, conv_w, conv_b, conv_norm_g, conv_norm_b,
             w_out, need_ctx):
    def project(u):
        B, T, _ = u.shape
        q, k, v, glu = jnp.split(u @ w_in, AB_SPLITS, axis=-1)
        q = rms_norm(q.reshape(B, T, ATT_HEADS, HEAD_DIM), q_norm_g)
        k = rms_norm(k.reshape(B, T, ATT_KV_HEADS, HEAD_DIM), k_norm_g)
        return q, k, v.reshape(B, T, ATT_KV_HEADS, HEAD_DIM), glu

    def conformer_conv(glu):
        a, b = jnp.split(glu, 2, axis=-1)
        h = dwconv(a * jax.nn.sigmoid(b), conv_w, conv_b)
        return jax.nn.silu(layer_norm(h, conv_norm_g, conv_norm_b, NORM_EPS))

    def attend(q, k, v):
        B, T = q.shape[:2]
        q = q.reshape(B, T, ATT_KV_HEADS, ATT_GROUP, HEAD_DIM)
        return block_attention(q, k, v).reshape(B, T, ATT_WIDTH)

    q_c, k_c, v_c, glu_c = project(u_ctx)
    q_l, k_l, v_l, glu_l = project(u_lat)
    cos, sin = axial_rope_tables(u_lat.shape[1], HEAD_DIM)
    q_l = apply_axial_rope(q_l, cos, sin)
    k_l = apply_axial_rope(k_l, cos, sin)
    att_l = attend(q_l, jnp.concatenate([k_l, k_c], axis=1), jnp.concatenate([v_l, v_c], axis=1))
    y_l = jnp.concatenate([att_l, conformer_conv(glu_l)], axis=-1) @ w_out
    if not need_ctx:
        return y_l, None
    att_c = attend(q_c, k_c, v_c)
    y_c = jnp.concatenate([att_c, conformer_conv(glu_c)], axis=-1) @ w_out
    return y_l, y_c


def rwkv7_scan(S0, r, w, k, v, kk, a):
    def step(S, inp):
        r_t, w_t, k_t, v_t, kk_t, a_t = inp
        sa = jnp.einsum('bhij,bhj->bhi', S, -kk_t)
        S = (S * w_t[:, :, None, :] + sa[..., None] * (kk_t * a_t)[:, :, None, :]
             + v_t[..., None] * k_t[:, :, None, :])
        return S, jnp.einsum('bhij,bhj->bhi', S, r_t)

    xs = tuple(jnp.moveaxis(t, 1, 0) for t in (r, w, k, v, kk, a))
    S, y = lax.scan(step, S0, xs)
    return jnp.moveaxis(y, 0, 1), S


def retention_chunks(R0, q, k, v, log_gamma):
    B, T, H, dk = q.shape
    dv = v.shape[-1]
    n = T // RET_CHUNK
    idx = jnp.arange(RET_CHUNK, dtype=F32)
    lg = log_gamma.astype(F32)[:, None]
    diff = idx[:, None] - idx[None, :]
    inner_decay = jnp.where(diff >= 0, jnp.exp(lg[:, :, None] * jnp.maximum(diff, 0.0)), 0.0)
    q_decay = jnp.exp(lg * (idx + 1.0))[:, :, None]
    k_decay = jnp.exp(lg * (RET_CHUNK - 1.0 - idx))[:, :, None]
    chunk_decay = jnp.exp(lg * RET_CHUNK)[:, :, None]

    def chunks(t):
        return jnp.moveaxis(t.astype(F32).reshape(B, n, RET_CHUNK, H, t.shape[-1]), (1, 3), (0, 2))

    def step(R, inp):
        qc, kc, vc = inp
        s = jnp.einsum('bhid,bhjd->bhij', qc, kc) * inner_decay
        o = jnp.einsum('bhij,bhjv->bhiv', s, vc) + jnp.einsum('bhid,bhdv->bhiv', qc, R) * q_decay
        R = R * chunk_decay + jnp.einsum('bhjd,bhjv->bhdv', kc * k_decay, vc)
        return R, o

    R, o = lax.scan(step, R0, (chunks(q), chunks(k), chunks(v)))
    return jnp.moveaxis(o, (0, 2), (1, 3)).reshape(B, T, H, dv), R


def bidirectional(scan_f, scan_b, ctx_f, lat_f, ctx_b, lat_b, state0):
    flip = lambda ts: tuple(jnp.flip(t, axis=1) for t in ts)
    yc_f, sc_f = scan_f(state0, *ctx_f)
    yl_f, _ = scan_f(sc_f, *lat_f)
    yc_b, sc_b = scan_b(state0, *flip(ctx_b))
    yl_b, _ = scan_b(sc_b, *flip(lat_b))
    return yl_f + jnp.flip(yl_b, axis=1), yc_f + jnp.flip(yc_b, axis=1)


def mixer_cd(u_lat, u_ctx, w_in, shift_mu, w0, w2, a0, a2, g2, k_k, k_a, r_k, ln_g, ln_b,
             decay_logit, gn_g, gn_b, w_out, need_ctx):
    def features(u, rotary):
        B, T, _ = u.shape
        z, rq, rk, rv, rg = jnp.split(u @ w_in, CD_SPLITS, axis=-1)
        r, k, v, dw, da, dg = jnp.split(token_shift(z, shift_mu), RWKV_SPLITS, axis=-1)
        hd = lambda t: t.astype(F32).reshape(B, T, RWKV_HEADS, RWKV_HEAD_DIM)
        kk = hd(k * k_k)
        kk = kk * lax.rsqrt(jnp.sum(kk * kk, axis=-1, keepdims=True) + 1e-12)
        dirs = []
        for d in range(2):
            wl = -jax.nn.softplus(-(w0[d] + jnp.tanh(dw) @ w2[d])) - 0.5
            decay = jnp.exp(-jnp.exp(wl.astype(F32)))
            a = jax.nn.sigmoid(a0[d] + da @ a2[d])
            kd = k * (1 + (a - 1) * k_a)
            dirs.append((hd(r), hd(decay), hd(kd), hd(v), kk, hd(a)))
        gate = jax.nn.sigmoid(dg) @ g2
        q_r = rq.reshape(B, T, RET_HEADS, RET_QK_DIM)
        k_r = rk.reshape(B, T, RET_HEADS, RET_QK_DIM)
        if rotary:
            cos, sin = axial_rope_tables(T, RET_QK_DIM)
            q_r = apply_axial_rope(q_r, cos, sin)
            k_r = apply_axial_rope(k_r, cos, sin)
        k_r = k_r * RET_QK_DIM ** -0.5
        ret = (q_r, k_r, rv.reshape(B, T, RET_HEADS, RET_V_DIM))
        return dirs, gate, ret, rg

    c_dirs, c_gate, c_ret, c_rg = features(u_ctx, False)
    l_dirs, l_gate, l_ret, l_rg = features(u_lat, True)
    B = u_lat.shape[0]
    S0 = jnp.zeros((B, RWKV_HEADS, RWKV_HEAD_DIM, RWKV_HEAD_DIM), F32)
    wkv_l, wkv_c = bidirectional(rwkv7_scan, rwkv7_scan, c_dirs[0], l_dirs[0], c_dirs[1], l_dirs[1], S0)
    lg = jax.nn.log_sigmoid(decay_logit.astype(F32))
    R0 = jnp.zeros((B, RET_HEADS, RET_QK_DIM, RET_V_DIM), F32)
    ret_l, ret_c = bidirectional(functools.partial(retention_chunks, log_gamma=lg[0]),
                                 functools.partial(retention_chunks, log_gamma=lg[1]),
                                 c_ret, l_ret, c_ret, l_ret, R0)

    def output(wkv, dirs, gate, ret, rg, dtype):
        Bq, T = wkv.shape[:2]
        bonus = sum(jnp.sum(r * kd * r_k.astype(F32), axis=-1, keepdims=True) * v
                    for (r, _, kd, v, _, _) in dirs)
        y_c = (head_group_norm(wkv, ln_g, ln_b, RWKV_GN_EPS) + bonus.reshape(Bq, T, RWKV_WIDTH)) * gate
        y_d = head_group_norm(ret, gn_g, gn_b, NORM_EPS) * jax.nn.silu(rg)
        return jnp.concatenate([y_c.astype(dtype), y_d.astype(dtype)], axis=-1) @ w_out

    y_l = output(wkv_l, l_dirs, l_gate, ret_l, l_rg, u_lat.dtype)
    if not need_ctx:
        return y_l, None
    return y_l, output(wkv_c, c_dirs, c_gate, ret_c, c_rg, u_ctx.dtype)


def setup_inputs(seed: int = 0) -> dict:
    key = jax.random.key(seed)
    ks = iter(jax.random.split(key, 48))
    nrm = lambda shape, scale: jax.random.normal(next(ks), shape, F32) * scale
    uni = lambda shape, lo, hi: jax.random.uniform(next(ks), shape, F32, lo, hi)
    D = D_MODEL
    base = 1.0 - 2.0 ** (-5.0 - np.arange(RET_HEADS, dtype=np.float32))
    base_logit = jnp.asarray(np.log(base) - np.log1p(-base), F32)
    return {
        'x': nrm((BATCH, SEQ, D), 1.0),
        'c': nrm((BATCH, D), 1.0),
        'ctx': nrm((BATCH, CTX_LEN, D), 1.0),
        'c_ctx': nrm((D,), 1.0),
        'mod_w': nrm((DEPTH, D, 6 * D), 0.5 * D ** -0.5),
        'mod_b': nrm((DEPTH, 6 * D), 0.02),
        'norm_mix_g': 1.0 + nrm((DEPTH, D), 0.02),
        'norm_ffn_g': 1.0 + nrm((DEPTH, D), 0.02),
        'ffn_w_up': nrm((DEPTH, D, 2 * FFN_DIM), D ** -0.5),
        'ffn_conv_w': nrm((DEPTH, FFN_CONV, FFN_DIM), FFN_CONV ** -0.5),
        'ffn_conv_b': nrm((DEPTH, FFN_DIM), 0.02),
        'ffn_w_down': nrm((DEPTH, FFN_DIM, D), FFN_DIM ** -0.5),
        'ab_w_in': nrm((N_EVEN, D, AB_IN), D ** -0.5),
        'ab_q_norm': 1.0 + nrm((N_EVEN, HEAD_DIM), 0.02),
        'ab_k_norm': 1.0 + nrm((N_EVEN, HEAD_DIM), 0.02),
        'ab_conv_w': nrm((N_EVEN, CONV_KERNEL, CONV_WIDTH), CONV_KERNEL ** -0.5),
        'ab_conv_b': nrm((N_EVEN, CONV_WIDTH), 0.02),
        'ab_conv_norm_g': 1.0 + nrm((N_EVEN, CONV_WIDTH), 0.02),
        'ab_conv_norm_b': nrm((N_EVEN, CONV_WIDTH), 0.02),
        'ab_w_out': nrm((N_EVEN, AB_OUT, D), AB_OUT ** -0.5),
        'cd_w_in': nrm((N_ODD, D, CD_IN), D ** -0.5),
        'cd_shift_mu': uni((N_ODD, 2, RWKV_IN), 0.0, 0.5),
        'rwkv_w0': uni((N_ODD, 2, RWKV_WIDTH), -4.0, 1.0),
        'rwkv_w2': nrm((N_ODD, 2, DECAY_LORA, RWKV_WIDTH), 0.5 * DECAY_LORA ** -0.5),
        'rwkv_a0': nrm((N_ODD, 2, RWKV_WIDTH), 0.1),
        'rwkv_a2': nrm((N_ODD, 2, AAA_LORA, RWKV_WIDTH), 0.5 * AAA_LORA ** -0.5),
        'rwkv_g2': nrm((N_ODD, GATE_LORA, RWKV_WIDTH), GATE_LORA ** -0.5),
        'rwkv_k_k': 0.85 + nrm((N_ODD, RWKV_WIDTH), 0.05),
        'rwkv_k_a': 1.0 + nrm((N_ODD, RWKV_WIDTH), 0.05),
        'rwkv_r_k': nrm((N_ODD, RWKV_HEADS, RWKV_HEAD_DIM), 0.1),
        'rwkv_ln_g': 1.0 + nrm((N_ODD, RWKV_WIDTH), 0.02),
        'rwkv_ln_b': nrm((N_ODD, RWKV_WIDTH), 0.02),
        'ret_decay_logit': base_logit + nrm((N_ODD, 2, RET_HEADS), 0.1),
        'ret_gn_g': 1.0 + nrm((N_ODD, RET_V_WIDTH), 0.02),
        'ret_gn_b': nrm((N_ODD, RET_V_WIDTH), 0.02),
        'cd_w_out': nrm((N_ODD, CD_OUT, D), CD_OUT ** -0.5),
        'final_norm_g': 1.0 + nrm((D,), 0.02),
    }


def reference(x, c, ctx, c_ctx, mod_w, mod_b, norm_mix_g, norm_ffn_g, ffn_w_up, ffn_conv_w,
              ffn_conv_b, ffn_w_down, ab_w_in, ab_q_norm, ab_k_norm, ab_conv_w, ab_conv_b,
              ab_conv_norm_g, ab_conv_norm_b, ab_w_out, cd_w_in, cd_shift_mu, rwkv_w0, rwkv_w2,
              rwkv_a0, rwkv_a2, rwkv_g2, rwkv_k_k, rwkv_k_a, rwkv_r_k, rwkv_ln_g, rwkv_ln_b,
              ret_decay_logit, ret_gn_g, ret_gn_b, cd_w_out, final_norm_g):
    h, hc = x, ctx
    s_lat, s_ctx = jax.nn.silu(c), jax.nn.silu(c_ctx)
    for layer in range(DEPTH):
        need_ctx = layer < DEPTH - 1
        mod_l = (s_lat @ mod_w[layer] + mod_b[layer])[:, None, :]
        mod_c = (s_ctx @ mod_w[layer] + mod_b[layer])[None, None, :]
        sh1, sc1, g1, sh2, sc2, g2 = jnp.split(mod_l, 6, axis=-1)
        csh1, csc1, cg1, csh2, csc2, cg2 = jnp.split(mod_c, 6, axis=-1)
        u = modulate(rms_norm(h, norm_mix_g[layer]), sh1, sc1)
        uc = modulate(rms_norm(hc, norm_mix_g[layer]), csh1, csc1)
        i = layer // 2
        if layer % 2 == 0:
            y, yc = mixer_ab(u, uc, ab_w_in[i], ab_q_norm[i], ab_k_norm[i], ab_conv_w[i], ab_conv_b[i],
                             ab_conv_norm_g[i], ab_conv_norm_b[i], ab_w_out[i], need_ctx)
        else:
            y, yc = mixer_cd(u, uc, cd_w_in[i], cd_shift_mu[i], rwkv_w0[i], rwkv_w2[i], rwkv_a0[i],
                             rwkv_a2[i], rwkv_g2[i], rwkv_k_k[i], rwkv_k_a[i], rwkv_r_k[i],
                             rwkv_ln_g[i], rwkv_ln_b[i], ret_decay_logit[i], ret_gn_g[i], ret_gn_b[i],
                             cd_w_out[i], need_ctx)
        h = h + g1 * y
        f = modulate(rms_norm(h, norm_ffn_g[layer]), sh2, sc2)
        h = h + g2 * conv_ffn(f, ffn_w_up[layer], ffn_conv_w[layer], ffn_conv_b[layer], ffn_w_down[layer])
        if need_ctx:
            hc = hc + cg1 * yc
            fc = modulate(rms_norm(hc, norm_ffn_g[layer]), csh2, csc2)
            hc = hc + cg2 * conv_ffn(fc, ffn_w_up[layer], ffn_conv_w[layer], ffn_conv_b[layer],
                                     ffn_w_down[layer])
    return rms_norm(h, final_norm_g)
```

```python
import numpy as np
import concourse.bass as bass
import concourse.mybir as mybir
from concourse.bass_utils import run_bass_kernel_spmd
from contextlib import ExitStack

F32 = mybir.dt.float32
BF16 = mybir.dt.bfloat16
AF = mybir.ActivationFunctionType
ALU = mybir.AluOpType
AX = mybir.AxisListType

D = 2048
NLAT = 2048
NCTX = 256
TT = NLAT + NCTX
FFN = 5632
TILES = [(0, 512), (512, 512), (1024, 512), (1536, 512), (2048, 256)]
COMPUTE = ('pe', 'act', 'dve', 'pool')
STREAMS = ('pe', 'act', 'dve', 'pool', 'sp')
NDSEM = 8
NRING = 8
RING_CHUNK = 512


class Buf:
    __slots__ = ('name', 'w', 'rs')

    def __init__(self, name=''):
        self.name = name
        self.w = None
        self.rs = {}


class Op:
    __slots__ = ('stream', 'fn', 'pos', 'dma', 'deps', 'waits', 'inc', 'val', 'dsem', 'gid', 'ring')

    def __init__(self, stream, fn, dma):
        self.stream = stream
        self.fn = fn
        self.dma = dma
        self.deps = {}
        self.waits = []
        self.inc = False
        self.val = 0
        self.dsem = None


class Sched:
    def __init__(self, same_engine_sync=True):
        self.streams = {s: [] for s in STREAMS}
        self.all = []
        self.dma_rr = {s: 0 for s in STREAMS}
        self.dma_last = {}
        self.last_c = {}
        self.same = same_engine_sync

    def _adddep(self, o, d):
        if d.dma:
            key = ('d',) + d.dsem + (d.gid,)
        else:
            if d.stream == o.stream and (not self.same or o.stream == 'pe'):
                return
            key = ('e', d.stream)
        cur = o.deps.get(key)
        if cur is None or d.pos > cur.pos:
            o.deps[key] = d

    @staticmethod
    def _flat(xs):
        out = []
        for x in xs:
            if isinstance(x, (list, tuple)):
                out.extend(Sched._flat(x))
            else:
                out.append(x)
        return out

    def op(self, stream, fn, reads=(), writes=(), dma=False):
        reads = self._flat(reads)
        writes = self._flat(writes)
        o = Op(stream, fn, dma)
        o.pos = len(self.streams[stream])
        o.gid = len(self.all)
        for b in reads:
            if b.w is not None:
                self._adddep(o, b.w)
        for b in writes:
            if b.w is not None:
                self._adddep(o, b.w)
            for d in b.rs.values():
                self._adddep(o, d)
        if dma:
            k = self.dma_rr[stream] % NDSEM
            self.dma_rr[stream] += 1
            o.dsem = (stream, k)
            prev = self.dma_last.get(o.dsem)
            if prev is not None:
                self._adddep(o, prev)
            self.dma_last[o.dsem] = o
        for b in writes:
            b.w = o
            b.rs = {}
        rk = ('d', o.gid) if dma else stream
        for b in reads:
            if b.w is not o:
                b.rs[rk] = o
        if not dma:
            self.last_c[stream] = o
        self.streams[stream].append(o)
        self.all.append(o)
        return o

    def barrier(self):
        tails = list(self.last_c.values())
        dl = list(self.dma_last.values())
        for s in STREAMS:
            o = Op(s, None, False)
            o.pos = len(self.streams[s])
            o.gid = len(self.all)
            for d in tails + dl:
                if (not d.dma) and d.stream == s:
                    continue
                if d.dma:
                    key = ('d',) + d.dsem + (d.gid,)
                else:
                    key = ('e', d.stream)
                cur = o.deps.get(key)
                if cur is None or d.pos > cur.pos:
                    o.deps[key] = d
            self.streams[s].append(o)
            self.all.append(o)

    def finalize(self):
        seen = {s: {} for s in STREAMS}
        for o in self.all:
            sn = seen[o.stream]
            for key, d in o.deps.items():
                if d.dma:
                    k2 = ('d',) + d.dsem
                    if sn.get(k2, -1) >= d.gid:
                        continue
                    sn[k2] = d.gid
                else:
                    if sn.get(key, -1) >= d.pos:
                        continue
                    sn[key] = d.pos
                d.inc = True
                o.waits.append(d)
        for s in STREAMS:
            c = 0
            per = [0] * NRING
            for o in self.streams[s]:
                if o.dma or o.fn is None:
                    continue
                if o.inc:
                    r = (c // RING_CHUNK) % NRING
                    c += 1
                    per[r] += 1
                    o.ring = r
                    o.val = per[r]
        cnt = {}
        for o in self.all:
            if o.dma:
                cnt[o.dsem] = cnt.get(o.dsem, 0) + 16
                o.val = cnt[o.dsem]

    def emit(self, nc):
        self.finalize()
        with ExitStack() as es:
            esem = {(s, r): es.enter_context(nc.semaphore('e_%s%d' % (s, r))) for s in COMPUTE for r in range(NRING)}
            dsem = {}
            for s in STREAMS:
                for k in range(min(NDSEM, self.dma_rr[s])):
                    dsem[(s, k)] = es.enter_context(nc.semaphore('d_%s%d' % (s, k)))
            block = es.enter_context(nc.Block())

            def replay(stream, eng):
                for o in self.streams[stream]:
                    for d in o.waits:
                        if d.dma:
                            eng.wait_ge(dsem[d.dsem], d.val)
                        else:
                            eng.wait_ge(esem[(d.stream, d.ring)], d.val)
                    if o.fn is None:
                        continue
                    ins = o.fn(eng)
                    if o.dma:
                        ins.then_inc(dsem[o.dsem], 16)
                    elif o.inc:
                        ins.then_inc(esem[(o.stream, o.ring)], 1)

            @block.sync
            def _(e):
                replay('sp', e)

            @block.tensor
            def _(e):
                replay('pe', e)

            @block.scalar
            def _(e):
                replay('act', e)

            @block.vector
            def _(e):
                replay('dve', e)

            @block.gpsimd
            def _(e):
                replay('pool', e)


class Arena:
    def __init__(self, nc, nbytes=207360):
        self.n32 = nbytes // 4
        self.t = nc.alloc_sbuf_tensor('arena', [128, self.n32], F32)
        self.off = 0
        self.marks = []
        self.peak = 0

    def alloc(self, free_shape, dtype=F32):
        n = 1
        for s in free_shape:
            n *= s
        nb = n * (2 if dtype == BF16 else 4)
        nb = (nb + 63) // 64 * 64
        o32 = self.off // 4
        assert self.off + nb <= self.n32 * 4, ('SBUF arena overflow', self.off, nb)
        self.off += nb
        self.peak = max(self.peak, self.off)
        ap = self.t[:, o32:o32 + nb // 4]
        if dtype == BF16:
            ap = ap.bitcast(BF16)[:, 0:n]
        else:
            ap = ap[:, 0:n]
        if len(free_shape) == 2:
            ap = ap.rearrange('p (a b) -> p a b', b=free_shape[1])
        elif len(free_shape) == 3:
            ap = ap.rearrange('p (a b c) -> p a b c', b=free_shape[1], c=free_shape[2])
        return ap

    def mark(self):
        self.marks.append(self.off)

    def release(self):
        self.off = self.marks.pop()


def fm(v):
    v = np.asarray(v, np.float32).reshape(-1, 128)
    return np.ascontiguousarray(v.T)


class Pack:
    def __init__(self):
        self.cols = {}
        self.parts = []
        self.n = 0

    def add(self, name, arr):
        arr = np.asarray(arr, np.float32)
        assert arr.shape[0] == 128
        arr = arr.reshape(128, -1)
        self.cols[name] = (self.n, arr.shape[1])
        self.parts.append(arr)
        self.n += arr.shape[1]

    def array(self):
        return np.ascontiguousarray(np.concatenate(self.parts, axis=1))


def pack_layout():
    L = []
    for l in range(2):
        L += [('gmix%d' % l, 16), ('gffn%d' % l, 16), ('modb%d' % l, 96), ('fcb%d' % l, 44), ('fcw%d' % l, 132)]
    L += [('qg', 1), ('qgs', 1), ('kg', 1), ('kgs', 1), ('acw', 248), ('acb', 8), ('ang', 8), ('anb', 8)]
    L += [('mu', 54), ('w0', 16), ('a0', 16), ('kk', 8), ('ka', 8), ('rk', 8), ('lng', 8), ('lnb', 8),
          ('gng', 16), ('gnb', 16), ('dlog', 16), ('gfin', 16)]
    return L


def pack_params(inp):
    P = Pack()
    for l in range(2):
        P.add('gmix%d' % l, fm(inp['norm_mix_g'][l]))
        P.add('gffn%d' % l, fm(inp['norm_ffn_g'][l]))
        P.add('modb%d' % l, fm(inp['mod_b'][l]))
        P.add('fcb%d' % l, fm(inp['ffn_conv_b'][l]))
        w = np.asarray(inp['ffn_conv_w'][l], np.float32)
        P.add('fcw%d' % l, np.stack([fm(w[j]) for j in range(3)], axis=2))
    sw = np.arange(128) ^ 32
    qg = np.asarray(inp['ab_q_norm'][0], np.float32)
    kg = np.asarray(inp['ab_k_norm'][0], np.float32)
    P.add('qg', qg.reshape(128, 1))
    P.add('qgs', qg[sw].reshape(128, 1))
    P.add('kg', kg.reshape(128, 1))
    P.add('kgs', kg[sw].reshape(128, 1))
    w = np.asarray(inp['ab_conv_w'][0], np.float32)
    P.add('acw', np.stack([fm(w[j]) for j in range(31)], axis=2))
    P.add('acb', fm(inp['ab_conv_b'][0]))
    P.add('ang', fm(inp['ab_conv_norm_g'][0]))
    P.add('anb', fm(inp['ab_conv_norm_b'][0]))
    mu = np.zeros((2, 27 * 128), np.float32)
    mu[:, :3360] = np.asarray(inp['cd_shift_mu'][0], np.float32)
    P.add('mu', np.stack([fm(mu[j]) for j in range(2)], axis=2))
    P.add('w0', np.stack([fm(inp['rwkv_w0'][0][j]) for j in range(2)], axis=1))
    P.add('a0', np.stack([fm(inp['rwkv_a0'][0][j]) for j in range(2)], axis=1))
    P.add('kk', fm(inp['rwkv_k_k'][0]))
    P.add('ka', fm(inp['rwkv_k_a'][0]))
    P.add('rk', fm(np.asarray(inp['rwkv_r_k'][0]).reshape(-1)))
    P.add('lng', fm(inp['rwkv_ln_g'][0]))
    P.add('lnb', fm(inp['rwkv_ln_b'][0]))
    P.add('gng', fm(inp['ret_gn_g'][0]))
    P.add('gnb', fm(inp['ret_gn_b'][0]))
    P.add('dlog', np.broadcast_to(np.asarray(inp['ret_decay_logit'][0], np.float32).reshape(1, 16), (128, 16)))
    P.add('gfin', fm(inp['final_norm_g']))
    assert [(k, v[1]) for k, v in P.cols.items()] == pack_layout()
    return P


CST_LAYOUT = [('ident', 128), ('ones', 128), ('bones', 128), ('eps6', 1), ('eps12', 1), ('epsgn', 1), ('one', 1),
              ('U', 128), ('L', 128), ('IU', 64), ('IL', 64), ('diffT', 128), ('maskF', 128), ('maskB', 128), ('irow', 128), ('irowb', 128),
              ('jcol', 1), ('jcolb', 1), ('c128', 1)]


def const_pack():
    P = Pack()
    P.add('ident', np.eye(128, dtype=np.float32))
    P.add('ones', np.ones((128, 128), np.float32))
    bo = np.zeros((128, 128), np.float32)
    bo[:64, :64] = 1
    bo[64:, 64:] = 1
    P.add('bones', bo)
    P.add('eps6', np.full((128, 1), 1e-6, np.float32))
    P.add('eps12', np.full((128, 1), 1e-12, np.float32))
    P.add('epsgn', np.full((128, 1), 64e-5, np.float32))
    P.add('one', np.ones((128, 1), np.float32))
    p = np.arange(128)
    hp, sp = p // 64, p % 64
    U = ((hp[:, None] == hp[None, :]) & (sp[None, :] > sp[:, None])).astype(np.float32)
    P.add('U', U)
    P.add('L', np.ascontiguousarray(U.T))
    t64 = np.arange(64)
    P.add('IU', (sp[:, None] <= t64[None, :]).astype(np.float32))
    P.add('IL', (sp[:, None] >= t64[None, :]).astype(np.float32))
    diffT = (p[None, :] - p[:, None]).astype(np.float32)
    P.add('diffT', diffT)
    P.add('maskF', (diffT >= 0).astype(np.float32))
    P.add('maskB', (diffT <= 0).astype(np.float32))
    P.add('irow', np.broadcast_to((p + 1.0).astype(np.float32)[None, :], (128, 128)))
    P.add('irowb', np.broadcast_to((128.0 - p).astype(np.float32)[None, :], (128, 128)))
    P.add('jcol', (127.0 - p).astype(np.float32).reshape(128, 1))
    P.add('jcolb', p.astype(np.float32).reshape(128, 1))
    P.add('c128', np.full((128, 1), 128.0, np.float32))
    assert [(k, v[1]) for k, v in P.cols.items()] == CST_LAYOUT
    return P


def rope_tables():
    rows = NLAT // 64
    row = np.repeat(np.arange(rows, dtype=np.float32), 64)
    col = np.tile(np.arange(64, dtype=np.float32), rows)
    inv = (10000.0 ** (-np.arange(0, 64, 2, dtype=np.float32) / 64)).astype(np.float32)
    ang = np.concatenate([row[:, None] * inv, col[:, None] * inv], axis=-1).astype(np.float32)
    cos, sin = np.cos(ang).astype(np.float32), np.sin(ang).astype(np.float32)
    C = np.zeros((128, NLAT), np.float32)
    S = np.zeros((128, NLAT), np.float32)
    for d in range(128):
        axis, r = d // 64, d % 64
        f = axis * 32 + (r % 32)
        C[d] = cos[:, f]
        S[d] = -sin[:, f] if r < 32 else sin[:, f]
    return np.stack([C, S])


WEIGHT_NAMES = ['mod_w', 'ffn_w_up', 'ffn_w_down', 'ab_w_in', 'ab_w_out', 'cd_w_in', 'rwkv_w2', 'rwkv_a2', 'rwkv_g2',
                'cd_w_out']
WEIGHT_SHAPES = {'mod_w': [2, D, 6 * D], 'ffn_w_up': [2, D, 2 * FFN], 'ffn_w_down': [2, FFN, D], 'ab_w_in': [1, D, 3584],
                 'ab_w_out': [1, D, D], 'cd_w_in': [1, D, 9504], 'rwkv_w2': [1, 2, 64, 1024], 'rwkv_a2': [1, 2, 64, 1024],
                 'rwkv_g2': [1, 160, 1024], 'cd_w_out': [1, 3072, D]}


KDEC = 0.6065306597126334


class MK1:
    def qbank(self):
        i = self.q_rr % 8
        self.q_rr += 1
        return self.banks[i][:, 0:128], self.qbufs[i]

    def phase_cd_proj(self, uT, buT):
        A = self.A
        w_in = self.W['cd_w_in'][0]
        zT = self.scratch('zT', [27, 128, 2306], F32)
        self.bz = [Buf('z%d' % i) for i in range(27)]
        rqk = self.scratch('rqk', [16, 128, TT], BF16)
        self.brqk = [Buf('rqk%d' % i) for i in range(16)]
        rvt = self.scratch('rvt', [18, 128, 2048], BF16)
        self.brv = Buf('rvt')
        rgs = self.scratch('rgs', [16, 128, NLAT], BF16)
        self.brg = [Buf('rg%d' % i) for i in range(16)]
        A.mark()
        self.wpool(2, 16 * 512)
        zpad = [A.alloc([2308], F32) for _ in range(2)]
        bzp = [Buf(), Buf()]
        zo = [A.alloc([2306], F32) for _ in range(2)]
        bzo = [Buf(), Buf()]
        c0 = A.alloc([27], F32)
        bc0 = Buf()
        mu = self.P('mu').rearrange('p (c j) -> p c j', j=2)
        for i in range(2):
            self.pool(lambda e, i=i: e.memset(zpad[i], 0.0), [], [bzp[i]])
        self.dve(lambda e: e.tensor_tensor(out=c0, in0=mu[:, :, 0], in1=mu[:, :, 1], op=ALU.add), [self.bpk], [bc0])
        self.dve(lambda e: e.tensor_scalar(out=c0, in0=c0, scalar1=-1.0, scalar2=1.0, op0=ALU.mult, op1=ALU.add), [bc0], [bc0])
        for g in range(7):
            ncol = 512 if g < 6 else 288
            wt, wb = self.wload(w_in[:, g * 512:g * 512 + ncol], 16, ncol)
            for c4 in range(4 if g < 6 else 3):
                c = g * 4 + c4
                m = 128 if c < 26 else 32
                zp, bzp_ = zpad[c % 2], bzp[c % 2]
                zo_, bzo_ = zo[c % 2], bzo[c % 2]
                for ti, (t0, tl) in enumerate(TILES):
                    off = 1 + t0 if ti < 4 else 2051
                    bk, bb = self.bank()
                    self.mm(bk[0:m, 0:tl], [(wt[:, kc, c4 * 128:c4 * 128 + m], uT[:, kc, t0:t0 + tl]) for kc in range(16)], [wb, buT], bb)
                    self.act(lambda e, bk=bk, zp=zp, m=m, off=off, tl=tl: e.activation(out=zp[0:m, off:off + tl], in_=bk[0:m, 0:tl], func=AF.Copy), [bb], [bzp_])
                self.dve(lambda e, zo_=zo_, zp=zp, m=m, c=c: e.tensor_scalar(out=zo_[0:m, :], in0=zp[0:m, 1:2307], scalar1=c0[0:m, c:c + 1],
                                                                           scalar2=None, op0=ALU.mult), [bzp_, bc0], [bzo_])
                self.dve(lambda e, zo_=zo_, zp=zp, m=m, c=c: e.scalar_tensor_tensor(out=zo_[0:m, :], in0=zp[0:m, 0:2306], scalar=mu[0:m, c, 0:1],
                                                                                  in1=zo_[0:m, :], op0=ALU.mult, op1=ALU.add), [bzp_, self.bpk, bzo_], [bzo_])
                self.dve(lambda e, zo_=zo_, zp=zp, m=m, c=c: e.scalar_tensor_tensor(out=zo_[0:m, :], in0=zp[0:m, 2:2308], scalar=mu[0:m, c, 1:2],
                                                                                  in1=zo_[0:m, :], op0=ALU.mult, op1=ALU.add), [bzp_, self.bpk, bzo_], [bzo_])
                self.dma_sp(zT[c][0:m], zo_[0:m, :], [bzo_], [self.bz[c]])
        A.release()
        self.S.barrier()
        A.mark()
        rope = A.alloc([2, NLAT], F32)
        brope = Buf()
        self.dma_sp(rope, self.roped.rearrange('a p t -> p a t'), [], [brope])
        self.wpool(2, 16 * 256)
        ws_ap = [A.alloc([16 * 256], BF16) for _ in range(2)]
        bws = [Buf(), Buf()]
        t1 = [A.alloc([512], F32) for _ in range(2)]
        bt1 = [Buf(), Buf()]
        t2 = [A.alloc([512], F32) for _ in range(2)]
        bt2 = [Buf(), Buf()]
        row = [A.alloc([TT], BF16) for _ in range(2)]
        brow = [Buf(), Buf()]
        k = 0
        for g in range(8):
            col0 = 3360 + g * 256
            wt, wb = self.wload(w_in[:, col0:col0 + 256], 16, 256)
            ws = ws_ap[g % 2].rearrange('p (c n) -> p c n', n=256)
            bw_ = bws[g % 2]
            wtv = wt.rearrange('p c (g b e) -> p (c g) b e', b=2, e=32)
            wsv = ws.rearrange('p c (g b e) -> p (c g) b e', b=2, e=32)
            for b in range(2):
                self.pool(lambda e, wsv=wsv, wtv=wtv, b=b: e.tensor_copy(out=wsv[:, :, b, :], in_=wtv[:, :, 1 - b, :]), [wb], [bw_])
            for hh in range(2):
                hd = g * 2 + hh
                rw, brw = row[hd % 2], brow[hd % 2]
                tiles = TILES[:4] if hd < 8 else TILES
                for ti, (t0, tl) in enumerate(tiles):
                    i2 = k % 2
                    k += 1
                    bkq, bbq = self.bank()
                    self.mm(bkq[:, 0:tl], [(wt[:, c, hh * 128:(hh + 1) * 128], uT[:, c, t0:t0 + tl]) for c in range(16)], [wb, buT], bbq)
                    if ti < 4:
                        bkw, bbw = self.bank()
                        self.mm(bkw[:, 0:tl], [(ws[:, c, hh * 128:(hh + 1) * 128], uT[:, c, t0:t0 + tl]) for c in range(16)], [bw_, buT], bbw)
                        self.dve(lambda e, bkq=bkq, i2=i2, t0=t0, tl=tl: e.tensor_tensor(out=t1[i2][:, 0:tl], in0=bkq[:, 0:tl], in1=rope[:, 0, t0:t0 + tl],
                                                                                      op=ALU.mult), [bbq, brope], [bt1[i2]])
                        self.dve(lambda e, bkw=bkw, i2=i2, t0=t0, tl=tl: e.tensor_tensor(out=t2[i2][:, 0:tl], in0=bkw[:, 0:tl], in1=rope[:, 1, t0:t0 + tl],
                                                                                      op=ALU.mult), [bbw, brope], [bt2[i2]])
                        self.pool(lambda e, i2=i2, rw=rw, t0=t0, tl=tl: e.tensor_tensor(out=rw[:, t0:t0 + tl], in0=t1[i2][:, 0:tl], in1=t2[i2][:, 0:tl],
                                                                                     op=ALU.add), [bt1[i2], bt2[i2]], [brw])
                    else:
                        self.act(lambda e, bkq=bkq, rw=rw, t0=t0, tl=tl: e.activation(out=rw[:, t0:t0 + tl], in_=bkq[:, 0:tl], func=AF.Copy), [bbq], [brw])
                self.dma_sp(rqk[hd], rw, [brw], [self.brqk[hd]])
        A.release()
        self.S.barrier()
        A.mark()
        wv = A.alloc([16, 2048], BF16)
        bwv = Buf()
        for n4 in range(4):
            self.dma_cast(wv[:, :, n4 * 512:(n4 + 1) * 512], w_in[:, 5408 + n4 * 512:5408 + (n4 + 1) * 512].rearrange('(c p) n -> p c n', p=128), [], [bwv])
        vt = [A.alloc([2048], BF16) for _ in range(2)]
        bvt = [Buf(), Buf()]
        for tb in range(18):
            v_, bv_ = vt[tb % 2], bvt[tb % 2]
            for n4 in range(4):
                bk, bb = self.bank()
                self.mm(bk, [(uT[:, c, tb * 128:(tb + 1) * 128], wv[:, c, n4 * 512:(n4 + 1) * 512]) for c in range(16)], [bwv, buT], bb)
                if n4 % 2 == 0:
                    self.act(lambda e, bk=bk, v_=v_, n4=n4: e.activation(out=v_[:, n4 * 512:(n4 + 1) * 512], in_=bk, func=AF.Copy), [bb], [bv_])
                else:
                    self.dve(lambda e, bk=bk, v_=v_, n4=n4: e.tensor_copy(out=v_[:, n4 * 512:(n4 + 1) * 512], in_=bk), [bb], [bv_])
            self.dma_sp(rvt[tb], v_, [bv_], [self.brv])
        A.release()
        self.S.barrier()
        A.mark()
        self.wpool(2, 16 * 512)
        row = [A.alloc([NLAT], BF16) for _ in range(2)]
        brow = [Buf(), Buf()]
        for g in range(4):
            wt, wb = self.wload(w_in[:, 7456 + g * 512:7456 + (g + 1) * 512], 16, 512)
            for c4 in range(4):
                c = g * 4 + c4
                rw, brw = row[c % 2], brow[c % 2]
                for ti, (t0, tl) in enumerate(TILES[:4]):
                    bk, bb = self.bank()
                    self.mm(bk, [(wt[:, kc, c4 * 128:(c4 + 1) * 128], uT[:, kc, t0:t0 + tl]) for kc in range(16)], [wb, buT], bb)
                    self.act(lambda e, bk=bk, rw=rw, t0=t0, tl=tl: e.activation(out=rw[:, t0:t0 + tl], in_=bk, func=AF.Silu), [bb], [brw])
                self.dma_sp(rgs[c], rw, [brw], [self.brg[c]])
        A.release()
        self.S.barrier()

    def phase_rwkv(self):
        A = self.A
        A.mark()
        zT = self.scr['zT']
        yin_s = self.scratch('yin', [24, 128, NLAT], BF16)
        self.byin_s = [Buf('yin%d' % i) for i in range(24)]
        self.q_rr = 0
        self.qbufs = [Buf('q%d' % i) for i in range(32)]
        cU, cL, cIU, cIL = self.C('U'), self.C('L'), self.C('IU'), self.C('IL')
        bd3 = self.C('bones', True).rearrange('p (h c) -> p h c', c=64)
        bonesb, bonesf = self.C('bones', True), self.C('bones')
        identb = self.C('ident', True)
        identf = self.C('ident')
        F = lambda: A.alloc([TT], F32)
        H = lambda: A.alloc([TT], BF16)
        dwa, sg25, sg26 = H(), H(), H()
        bsh = Buf('shared')
        A.mark()
        z32 = A.alloc([2306], F32)
        bz32 = Buf()
        for (c, dst, m0, m1, fn) in ((24, dwa, 0, 64, AF.Tanh), (24, dwa, 64, 128, AF.Copy), (25, sg25, 0, 128, AF.Sigmoid), (26, sg26, 0, 32, AF.Sigmoid)):
            if m0 == 0:
                self.dma_sp(z32[0:(32 if c == 26 else 128), :], zT[c][0:(32 if c == 26 else 128)], [self.bz[c]], [bz32])
            self.act(lambda e, dst=dst, m0=m0, m1=m1, fn=fn: e.activation(out=dst[m0:m1, 0:NLAT], in_=z32[m0:m1, 0:NLAT], func=fn), [bz32], [bsh])
            self.act(lambda e, dst=dst, m0=m0, m1=m1, fn=fn: e.activation(out=dst[m0:m1, NLAT:TT], in_=z32[m0:m1, 2050:2306], func=fn), [bz32], [bsh])
        A.release()
        self.S.barrier()
        w2a2 = A.alloc([2, 1024], BF16)
        g2w = A.alloc([2, 1024], BF16)
        self.dma_cast(w2a2[0:64], self.W['rwkv_w2'][0].rearrange('d k n -> k d n'), [], [bsh])
        self.dma_cast(w2a2[64:128], self.W['rwkv_a2'][0].rearrange('d k n -> k d n'), [], [bsh])
        self.dma_cast(g2w[:, 0, :], self.W['rwkv_g2'][0][0:128, :], [], [bsh])
        self.dma_cast(g2w[0:32, 1, :], self.W['rwkv_g2'][0][128:160, :], [], [bsh])
        seg = F()
        self.pool(lambda e: e.memset(seg, 1.0), [], [bsh])
        self.pool(lambda e: e.memset(seg.rearrange('p (n c) -> p n c', c=64)[:, :, 0:1], 0.0), [bsh], [bsh])
        r_, k_, v_, kk_ = F(), F(), F(), F()
        T = [F() for _ in range(6)]
        vb, rkr = H(), H()
        KKg, Bg, Kg, Rg = H(), H(), H(), H()
        wkv = A.alloc([NLAT], F32)
        bonus = A.alloc([NLAT], F32)
        yrow = [A.alloc([NLAT], BF16)] * 2
        byrow = [Buf()] * 2
        bpair, bdir, bwkv, bbon = Buf('pair'), Buf('dir'), Buf('wkv'), Buf('bonus')
        NSET = 2
        def tset():
            d = {}
            for nm in ('KKb', 'Bgb', 'Kgb', 'vTb', 'X', 'XT', 'BT', 'TTa', 'TTb', 'Ya', 'Yb', 'YTa', 'YTb', 'BgTb', 'KgTb', 'vb', 'ub'):
                d[nm] = (A.alloc([128], BF16), Buf(nm))
            for nm in ('MrbT', 'MrkT', 'vst', 'nZ', 'ust'):
                d[nm] = (A.alloc([64], BF16), Buf(nm))
            return d
        sets = [tset() for _ in range(NSET)]
        for d_ in sets:
            for nm in ('KKb', 'Bgb', 'Kgb', 'vTb', 'ub'):
                ap_, bf_ = d_[nm]
                self.pool(lambda e, ap_=ap_: e.memset(ap_, 0.0), [], [bf_])
        Pst = [(A.alloc([64], F32), A.alloc([64], BF16), A.alloc([128], BF16), Buf('P%d' % i)) for i in range(2)]
        ptmp = A.alloc([64], F32)
        bptmp = Buf()
        sm = [A.alloc([512], F32) for _ in range(4)]
        bsm = [Buf() for _ in range(4)]
        w0 = self.P('w0').rearrange('p (d c) -> p d c', c=8)
        a0 = self.P('a0').rearrange('p (d c) -> p d c', c=8)
        kkp, kap, rkp, lng, lnb = self.P('kk'), self.P('ka'), self.P('rk'), self.P('lng'), self.P('lnb')
        LT = TILES[:4]
        un = 0
        for pc in range(getattr(self, 'rw_pairs', 8)):
            for (dst, c) in ((r_, pc), (k_, 8 + pc), (v_, 16 + pc)):
                self.dma_sp(dst[:, 0:NLAT], zT[c][:, 0:NLAT], [self.bz[c]], [bpair])
                self.dma_sp(dst[:, NLAT:TT], zT[c][:, 2050:2306], [self.bz[c]], [bpair])
            self.act(lambda e: e.activation(out=vb, in_=v_, func=AF.Copy), [bpair], [bpair])
            self.dve(lambda e, pc=pc: e.tensor_scalar(out=kk_, in0=k_, scalar1=kkp[:, pc:pc + 1], scalar2=None, op0=ALU.mult), [bpair, self.bpk], [bpair])
            self.act(lambda e: e.activation(out=T[0], in_=kk_, func=AF.Square), [bpair], [bdir])
            for ti, (t0, tl) in enumerate(TILES):
                bk, bb = self.qbank4()
                self.mm(bk[:, 0:tl], [(bonesf, T[0][:, t0:t0 + tl])], [bdir, self.bcst], bb)
                self.act(lambda e, bk=bk, t0=t0, tl=tl: e.activation(out=T[1][:, t0:t0 + tl], in_=bk[:, 0:tl], func=AF.Sqrt, bias=self.C('eps12'), scale=1.0),
                         [bb, self.bcst], [bdir])
            self.dve(lambda e: e.reciprocal(out=T[1], in_=T[1]), [bdir], [bdir])
            self.pool(lambda e: e.tensor_tensor(out=kk_, in0=kk_, in1=T[1], op=ALU.mult), [bdir, bpair], [bpair])
            for d in range(getattr(self, 'rw_dirs', 2)):
                sig, a_, cs, alt, gi, gneg = T[0], T[1], T[2], T[3], T[4], T[5]
                for ti, (t0, tl) in enumerate(TILES):
                    bk, bb = self.qbank4()
                    self.mm(bk[:, 0:tl], [(w2a2[0:64, d, pc * 128:(pc + 1) * 128], dwa[0:64, t0:t0 + tl])], [bsh], bb)
                    self.act(lambda e, bk=bk, t0=t0, tl=tl, d=d, pc=pc: e.activation(out=sig[:, t0:t0 + tl], in_=bk[:, 0:tl], func=AF.Sigmoid,
                                                                               bias=w0[:, d, pc:pc + 1], scale=1.0), [bb, self.bpk], [bdir])
                    bk2, bb2 = self.qbank4()
                    self.mm(bk2[:, 0:tl], [(w2a2[64:128, d, pc * 128:(pc + 1) * 128], dwa[64:128, t0:t0 + tl])], [bsh], bb2)
                    self.act(lambda e, bk2=bk2, t0=t0, tl=tl, d=d, pc=pc: e.activation(out=a_[:, t0:t0 + tl], in_=bk2[:, 0:tl], func=AF.Sigmoid,
                                                                                 bias=a0[:, d, pc:pc + 1], scale=1.0), [bb2, self.bpk], [bdir])
                self.dve(lambda e, cs=cs, sig=sig: e.tensor_tensor_scan(out=cs, data0=seg, data1=sig, initial=0.0, op0=ALU.mult, op1=ALU.add), [bdir, bsh], [bdir])
                if d == 1:
                    cs3 = cs.rearrange('p (n c) -> p n c', c=64)
                    self.dve(lambda e, alt=alt, sig=sig, cs=cs: e.tensor_tensor(out=alt, in0=sig, in1=cs, op=ALU.subtract), [bdir], [bdir])
                    self.dve(lambda e, cs3=cs3, alt=alt: e.tensor_tensor(out=alt.rearrange('p (n c) -> p n c', c=64), in0=alt.rearrange('p (n c) -> p n c', c=64),
                                                               in1=cs3[:, :, 63:64].broadcast_to([128, 36, 64]), op=ALU.add), [bdir], [bdir])
                    cs, alt = alt, cs
                self.act(lambda e, cs=cs: e.activation(out=gi, in_=cs, func=AF.Exp, scale=-KDEC), [bdir], [bdir])
                self.act(lambda e, cs=cs: e.activation(out=gneg, in_=cs, func=AF.Exp, scale=KDEC), [bdir], [bdir])
                self.dve(lambda e, cs=cs: e.tensor_tensor(out=sig, in0=cs, in1=sig, op=ALU.subtract), [bdir], [bdir])
                self.act(lambda e: e.activation(out=sig, in_=sig, func=AF.Exp, scale=-KDEC), [bdir], [bdir])
                self.pool(lambda e: e.tensor_tensor(out=KKg, in0=kk_, in1=sig, op=ALU.mult), [bdir, bpair], [bdir])
                self.pool(lambda e, alt=alt: e.tensor_tensor(out=alt, in0=kk_, in1=a_, op=ALU.mult), [bdir, bpair], [bdir])
                self.pool(lambda e, alt=alt: e.tensor_tensor(out=Bg, in0=alt, in1=gneg, op=ALU.mult), [bdir], [bdir])
                self.dve(lambda e, pc=pc: e.tensor_scalar(out=a_, in0=a_, scalar1=-1.0, scalar2=kap[:, pc:pc + 1], op0=ALU.add, op1=ALU.mult),
                         [bdir, self.bpk], [bdir])
                self.dve(lambda e: e.scalar_tensor_tensor(out=a_, in0=a_, scalar=1.0, in1=k_, op0=ALU.add, op1=ALU.mult), [bdir, bpair], [bdir])
                self.pool(lambda e: e.tensor_tensor(out=Kg, in0=a_, in1=gneg, op=ALU.mult), [bdir], [bdir])
                self.pool(lambda e: e.tensor_tensor(out=Rg, in0=r_, in1=gi, op=ALU.mult), [bdir, bpair], [bdir])
                self.dve(lambda e, pc=pc: e.scalar_tensor_tensor(out=rkr, in0=r_, scalar=rkp[:, pc:pc + 1], in1=a_, op0=ALU.mult, op1=ALU.mult),
                         [bdir, bpair, self.bpk], [bdir])
                for ti, (t0, tl) in enumerate(LT):
                    bk, bb = self.qbank4()
                    self.mm(bk[:, 0:tl], [(bonesb, rkr[:, t0:t0 + tl])], [bdir, self.bcst], bb)
                    if d == 0:
                        self.dve(lambda e, bk=bk, t0=t0, tl=tl: e.tensor_tensor(out=bonus[:, t0:t0 + tl], in0=bk[:, 0:tl], in1=v_[:, t0:t0 + tl], op=ALU.mult),
                                 [bb, bpair], [bbon])
                    else:
                        s_, bs_ = sm[ti % 4], bsm[ti % 4]
                        self.dve(lambda e, bk=bk, s_=s_, t0=t0, tl=tl: e.tensor_tensor(out=s_[:, 0:tl], in0=bk[:, 0:tl], in1=v_[:, t0:t0 + tl], op=ALU.mult),
                                 [bb, bpair], [bs_])
                        self.pool(lambda e, s_=s_, t0=t0, tl=tl: e.tensor_tensor(out=bonus[:, t0:t0 + tl], in0=bonus[:, t0:t0 + tl], in1=s_[:, 0:tl], op=ALU.add),
                                  [bs_, bbon], [bbon])
                Pf, Pb, Pbd, bP = Pst[d]
                self.pool(lambda e, Pf=Pf: e.memset(Pf, 0.0), [], [bP])
                self.pool(lambda e, Pb=Pb: e.memset(Pb, 0.0), [bP], [bP])
                self.pool(lambda e, Pbd=Pbd: e.memset(Pbd, 0.0), [bP], [bP])
                order = [32, 33, 34, 35] + list(range(32)) if d == 0 else [35, 34, 33, 32] + list(range(31, -1, -1))
                mS, mSn, mI = (cU, cL, cIU) if d == 0 else (cL, cU, cIL)
                order = order[:getattr(self, 'rw_chunks', 36)]
                self.rw_level = getattr(self, 'rw_level', 9)
                for n in order:
                    ts_ = sets[un % NSET]
                    un += 1
                    cs_ = slice(n * 64, (n + 1) * 64)
                    lat = n < 32
                    gcol = gi[:, n * 64 + 63:n * 64 + 64] if d == 0 else gi[:, n * 64:n * 64 + 1]
                    def bdexp(name, src):
                        ap, bf = ts_[name]
                        for h2 in range(2):
                            self.pool(lambda e, ap=ap, src=src, h2=h2: e.tensor_copy(out=ap[h2 * 64:(h2 + 1) * 64, h2 * 64:(h2 + 1) * 64],
                                                                                   in_=src[h2 * 64:(h2 + 1) * 64, :]), [bdir, bpair], [bf])
                        return ap, bf
                    KKb, bKKb = bdexp('KKb', KKg[:, cs_])
                    Bgb, bBgb = bdexp('Bgb', Bg[:, cs_])
                    Kgb, bKgb = bdexp('Kgb', Kg[:, cs_])
                    vTb, bvTb = bdexp('vTb', vb[:, cs_])
                    if self.rw_level <= 1:
                        continue
                    def mmq(pairs, reads):
                        q, bq_ = self.qbank()
                        self.mm(q, pairs, reads, bq_)
                        return q, bq_
                    def evac_mask(name, q, bq_, mask, neg):
                        ap, bf = ts_[name]
                        if neg:
                            self.dve(lambda e, ap=ap, q=q, mask=mask: e.scalar_tensor_tensor(out=ap, in0=q, scalar=-1.0, in1=mask, op0=ALU.mult, op1=ALU.mult),
                                     [bq_, self.bcst], [bf])
                        else:
                            self.dve(lambda e, ap=ap, q=q, mask=mask: e.tensor_tensor(out=ap, in0=q, in1=mask, op=ALU.mult), [bq_, self.bcst], [bf])
                        return ap, bf
                    q, bq_ = mmq([(Bgb, KKb)], [bBgb, bKKb])
                    XT, bXT = evac_mask('XT', q, bq_, mS, True)
                    TTc, bTT = ts_['TTa']
                    self.dve(lambda e, TTc=TTc, XT=XT: e.tensor_tensor(out=TTc, in0=XT, in1=identb, op=ALU.add), [bXT, self.bcst], [bTT])
                    q, bq_ = mmq([(KKb, Bgb)], [bBgb, bKKb])
                    X, bX = evac_mask('X', q, bq_, mSn, True)
                    q, bq_ = mmq([(Kgb, KKb)], [bKgb, bKKb])
                    BT, bBT = evac_mask('BT', q, bq_, mS, False)
                    q, bq_ = self.qbank()
                    self.mm(q[:, 0:64], [(Bgb, Rg[:, cs_])], [bBgb, bdir], bq_)
                    ap, bMrb = ts_['MrbT']
                    self.dve(lambda e, ap=ap, q=q, mI=mI: e.tensor_tensor(out=ap, in0=q[:, 0:64], in1=mI, op=ALU.mult), [bq_, self.bcst], [bMrb])
                    MrbT = ap
                    q, bq_ = self.qbank()
                    self.mm(q[:, 0:64], [(Kgb, Rg[:, cs_])], [bKgb, bdir], bq_)
                    ap, bMrk = ts_['MrkT']
                    self.dve(lambda e, ap=ap, q=q, mI=mI: e.tensor_tensor(out=ap, in0=q[:, 0:64], in1=mI, op=ALU.mult), [bq_, self.bcst], [bMrk])
                    MrkT = ap
                    if self.rw_level <= 2:
                        continue
                    def tr(name, src, bsrc):
                        q, bq_ = self.qbank()
                        qb = q.bitcast(BF16)[:, 0:128]
                        self.pe(lambda e, qb=qb, src=src: e.transpose(out=qb, in_=src, identity=identb), [bsrc, self.bcst], [bq_])
                        ap, bf = ts_[name]
                        self.act(lambda e, ap=ap, qb=qb: e.activation(out=ap, in_=qb, func=AF.Copy), [bq_], [bf])
                        return ap, bf, qb, bq_
                    BgTb, bBgT, _, _ = tr('BgTb', Bgb, bBgb)
                    KgTb, bKgT, _, _ = tr('KgTb', Kgb, bKgb)
                    vbd, bvbd, _, _ = tr('vb', vTb, bvTb)
                    vst, bvst = ts_['vst']
                    self.pool(lambda e, vst=vst, vbd=vbd: e.tensor_tensor(out=vst, in0=vbd[:, 0:64], in1=vbd[:, 64:128], op=ALU.add), [bvbd], [bvst])
                    if self.rw_level <= 3:
                        continue
                    Y, bY, YT, bYT = X, bX, XT, bXT
                    for lev in range(5):
                        nm = 'a' if lev % 2 == 0 else 'b'
                        q, bq_ = mmq([(YT, Y)], [bYT, bY])
                        Y2, bY2 = ts_['Y' + nm]
                        self.act(lambda e, Y2=Y2, q=q: e.activation(out=Y2, in_=q, func=AF.Copy), [bq_], [bY2])
                        if lev < 4:
                            q, bq_ = mmq([(Y, YT)], [bYT, bY])
                            YT2, bYT2 = ts_['YT' + nm]
                            self.act(lambda e, YT2=YT2, q=q: e.activation(out=YT2, in_=q, func=AF.Copy), [bq_], [bYT2])
                        q, bq_ = mmq([(Y2, TTc)], [bY2, bTT])
                        TTn, bTTn = ts_['TTb' if lev % 2 == 0 else 'TTa']
                        self.dve(lambda e, TTn=TTn, q=q, TTc=TTc: e.tensor_tensor(out=TTn, in0=q, in1=TTc, op=ALU.add), [bq_, bTT], [bTTn])
                        TTc, bTT = TTn, bTTn
                        Y, bY = Y2, bY2
                        if lev < 4:
                            YT, bYT = YT2, bYT2
                    if self.rw_level <= 4:
                        continue
                    q, bq_ = self.qbank()
                    self.mm(q[:, 0:64], [(KKb, Pb), (BT, vst)], [bKKb, bP, bBT, bvst], bq_)
                    nZ, bnZ = ts_['nZ']
                    self.act(lambda e, nZ=nZ, q=q: e.activation(out=nZ, in_=q[:, 0:64], func=AF.Copy, scale=-1.0), [bq_], [bnZ])
                    qu, bqu = self.qbank()
                    self.mm(qu[:, 0:64], [(TTc, nZ)], [bTT, bnZ], bqu)
                    ust, bust = ts_['ust']
                    self.act(lambda e, ust=ust, qu=qu: e.activation(out=ust, in_=qu[:, 0:64], func=AF.Copy), [bqu], [bust])
                    if lat:
                        ub, bub = ts_['ub']
                        self.act(lambda e, ub=ub, qu=qu: e.activation(out=ub[0:64, 0:64], in_=qu[0:64, 0:64], func=AF.Copy), [bqu], [bub])
                        self.act(lambda e, ub=ub, qu=qu: e.activation(out=ub[64:128, 64:128], in_=qu[64:128, 0:64], func=AF.Copy), [bqu], [bub])
                        qy, bqy = self.qbank()
                        self.mm(qy[:, 0:64], [(Pbd, Rg[:, cs_]), (ub, MrbT), (vbd, MrkT)], [bP, bdir, bub, bMrb, bvbd, bMrk], bqy)
                        if d == 0:
                            self.act(lambda e, qy=qy, cs_=cs_: e.activation(out=wkv[:, cs_], in_=qy[:, 0:64], func=AF.Copy), [bqy], [bwkv])
                        else:
                            self.dve(lambda e, qy=qy, cs_=cs_: e.tensor_tensor(out=wkv[:, cs_], in0=qy[:, 0:64], in1=wkv[:, cs_], op=ALU.add), [bqy, bwkv], [bwkv])
                    qd, bqd = self.qbank()
                    self.mm(qd[:, 0:64], [(BgTb, ust), (KgTb, vst)], [bBgT, bust, bKgT, bvst], bqd)
                    self.dve(lambda e, qd=qd, Pf=Pf: e.tensor_tensor(out=ptmp, in0=qd[:, 0:64], in1=Pf, op=ALU.add), [bqd, bP], [bptmp])
                    self.dve(lambda e, Pf=Pf, gcol=gcol: e.tensor_scalar(out=Pf, in0=ptmp, scalar1=gcol, scalar2=None, op0=ALU.mult), [bptmp, bdir], [bP])
                    self.act(lambda e, Pf=Pf, Pb=Pb: e.activation(out=Pb, in_=Pf, func=AF.Copy), [bP], [bP])
                    for h2 in range(2):
                        self.pool(lambda e, Pf=Pf, Pbd=Pbd, h2=h2: e.tensor_copy(out=Pbd[h2 * 64:(h2 + 1) * 64, h2 * 64:(h2 + 1) * 64],
                                                                               in_=Pf[h2 * 64:(h2 + 1) * 64, :]), [bP], [bP])
            if not getattr(self, 'rw_gn', True):
                continue
            yr, byr = yrow[pc % 2], byrow[pc % 2]
            for ti, (t0, tl) in enumerate(LT):
                s0, s1, s2, s3 = sm
                b0_, b1_, b2_, b3_ = bsm
                bk1, bb1 = self.qbank4()
                self.mm(bk1[:, 0:tl], [(bonesf, wkv[:, t0:t0 + tl])], [bwkv, self.bcst], bb1)
                self.act(lambda e, t0=t0, tl=tl: e.activation(out=s0[:, 0:tl], in_=wkv[:, t0:t0 + tl], func=AF.Square), [bwkv], [b0_])
                bk2, bb2 = self.qbank4()
                self.mm(bk2[:, 0:tl], [(bonesf, s0[:, 0:tl])], [b0_, self.bcst], bb2)
                self.act(lambda e, bk1=bk1, tl=tl: e.activation(out=s1[:, 0:tl], in_=bk1[:, 0:tl], func=AF.Copy, scale=1.0 / 64), [bb1], [b1_])
                self.dve(lambda e, tl=tl: e.tensor_tensor(out=s2[:, 0:tl], in0=s1[:, 0:tl], in1=s1[:, 0:tl], op=ALU.mult), [b1_], [b2_])
                self.dve(lambda e, bk2=bk2, tl=tl: e.scalar_tensor_tensor(out=s2[:, 0:tl], in0=bk2[:, 0:tl], scalar=1.0 / 64, in1=s2[:, 0:tl],
                                                                          op0=ALU.mult, op1=ALU.subtract), [bb2, b2_], [b2_])
                self.act(lambda e, tl=tl: e.activation(out=s2[:, 0:tl], in_=s2[:, 0:tl], func=AF.Sqrt, bias=self.C('epsgn'), scale=1.0), [b2_, self.bcst], [b2_])
                self.dve(lambda e, tl=tl: e.reciprocal(out=s2[:, 0:tl], in_=s2[:, 0:tl]), [b2_], [b2_])
                self.dve(lambda e, t0=t0, tl=tl: e.tensor_tensor(out=s3[:, 0:tl], in0=wkv[:, t0:t0 + tl], in1=s1[:, 0:tl], op=ALU.subtract), [bwkv, b1_], [b3_])
                self.pool(lambda e, tl=tl: e.tensor_tensor(out=s3[:, 0:tl], in0=s3[:, 0:tl], in1=s2[:, 0:tl], op=ALU.mult), [b3_, b2_], [b3_])
                self.act(lambda e, tl=tl, pc=pc: e.activation(out=s3[:, 0:tl], in_=s3[:, 0:tl], func=AF.Identity, bias=lnb[:, pc:pc + 1], scale=lng[:, pc:pc + 1]),
                         [b3_, self.bpk], [b3_])
                self.pool(lambda e, t0=t0, tl=tl: e.tensor_tensor(out=s3[:, 0:tl], in0=s3[:, 0:tl], in1=bonus[:, t0:t0 + tl], op=ALU.add), [b3_, bbon], [b3_])
                bkg, bbg = self.qbank4()
                self.mm(bkg[:, 0:tl], [(g2w[:, 0, pc * 128:(pc + 1) * 128], sg25[:, t0:t0 + tl]), (g2w[0:32, 1, pc * 128:(pc + 1) * 128], sg26[0:32, t0:t0 + tl])],
                        [bsh], bbg)
                self.dve(lambda e, bkg=bkg, yr=yr, t0=t0, tl=tl: e.tensor_tensor(out=yr[:, t0:t0 + tl], in0=bkg[:, 0:tl], in1=s3[:, 0:tl], op=ALU.mult),
                         [bbg, b3_], [byr])
            self.dma_sp(yin_s[pc], yr, [byr], [self.byin_s[pc]])
        A.release()
        self.S.barrier()

    def qbank4(self):
        i = self.q_rr % 8
        self.q_rr += 1
        return self.banks[i][:, :], self.qbufs[i]

    def phase_ret(self):
        A = self.A
        A.mark()
        rqk, rvt, rgs, yin_s = self.scr['rqk'], self.scr['rvt'], self.scr['rgs'], self.scr['yin']
        identb = self.C('ident', True)
        onesf = self.C('ones')
        diffT, maskF, maskB = self.C('diffT'), self.C('maskF'), self.C('maskB')
        irow, irowb, jcol, jcolb, c128 = self.C('irow'), self.C('irowb'), self.C('jcol'), self.C('jcolb'), self.C('c128')
        gng, gnb = self.P('gng'), self.P('gnb')
        SC = 128.0 ** -0.5
        lg = A.alloc([16], F32)
        nlg = A.alloc([16], F32)
        blg = Buf()
        self.act(lambda e: e.activation(out=lg, in_=self.P('dlog'), func=AF.Exp, scale=-1.0), [self.bpk], [blg])
        self.act(lambda e: e.activation(out=lg, in_=lg, func=AF.Ln, bias=self.C('one'), scale=1.0), [blg, self.bcst], [blg])
        self.dve(lambda e: e.tensor_scalar(out=nlg, in0=lg, scalar1=1.0, scalar2=None, op0=ALU.mult), [blg], [blg])
        self.dve(lambda e: e.tensor_scalar(out=lg, in0=nlg, scalar1=-1.0, scalar2=None, op0=ALU.mult), [blg], [blg])
        qT, kT = A.alloc([TT], BF16), A.alloc([TT], BF16)
        vtok = A.alloc([18, 256], BF16)
        rg2 = A.alloc([2, NLAT], BF16)
        oacc = A.alloc([2, NLAT], F32)
        yrow = [A.alloc([NLAT], BF16) for _ in range(2)]
        byrow = [Buf(), Buf()]
        Dc = A.alloc([128], F32)
        e2 = A.alloc([128], F32)
        qdt = [A.alloc([128], F32) for _ in range(2)]
        kdc = A.alloc([4], F32)
        bhd, btab, boacc = Buf('head'), Buf('tab'), Buf('oacc')
        sm_ = [A.alloc([128], BF16) for _ in range(3)]
        bsm_ = [Buf() for _ in range(3)]
        qd_ = [A.alloc([128], BF16) for _ in range(3)]
        bqd_ = [Buf() for _ in range(3)]
        ktk = [[A.alloc([128], BF16) for _ in range(2)] for _ in range(18)]
        bktk = Buf('ktk')
        R32 = [A.alloc([256], F32) for _ in range(2)]
        Rbf = [A.alloc([256], BF16) for _ in range(2)]
        bR = [Buf('R0'), Buf('R1')]
        st = [A.alloc([512], F32) for _ in range(4)]
        bst = [Buf() for _ in range(4)]
        k3 = 0
        for h in range(8):
            self.dma_sp(qT[:, 0:NLAT], rqk[h][:, 0:NLAT], [self.brqk[h]], [bhd])
            self.dma_sp(kT, rqk[8 + h], [self.brqk[8 + h]], [bhd])
            self.dma_sp(vtok, rvt[:, :, h * 256:(h + 1) * 256].rearrange('b p f -> p b f'), [self.brv], [bhd])
            self.dma_sp(rg2, rgs[2 * h:2 * h + 2].rearrange('c p t -> p c t'), self.brg[2 * h:2 * h + 2], [bhd])
            lf, lb, nlb = lg[:, h:h + 1], lg[:, 8 + h:9 + h], nlg[:, 8 + h:9 + h]
            self.act(lambda e, lf=lf: e.activation(out=Dc, in_=diffT, func=AF.Exp, scale=lf), [blg, self.bcst], [btab])
            self.dve(lambda e: e.tensor_tensor(out=Dc, in0=Dc, in1=maskF, op=ALU.mult), [btab, self.bcst], [btab])
            self.act(lambda e, nlb=nlb: e.activation(out=e2, in_=diffT, func=AF.Exp, scale=nlb), [blg, self.bcst], [btab])
            self.dve(lambda e: e.tensor_tensor(out=e2, in0=e2, in1=maskB, op=ALU.mult), [btab, self.bcst], [btab])
            self.dve(lambda e: e.tensor_tensor(out=Dc, in0=Dc, in1=e2, op=ALU.add), [btab], [btab])
            self.dve(lambda e: e.tensor_scalar(out=Dc, in0=Dc, scalar1=SC, scalar2=None, op0=ALU.mult), [btab], [btab])
            self.act(lambda e, lf=lf: e.activation(out=qdt[0], in_=irow, func=AF.Exp, scale=lf), [blg, self.bcst], [btab])
            self.act(lambda e, lb=lb: e.activation(out=qdt[1], in_=irowb, func=AF.Exp, scale=lb), [blg, self.bcst], [btab])
            self.act(lambda e, lf=lf: e.activation(out=kdc[:, 0:1], in_=jcol, func=AF.Exp, scale=lf), [blg, self.bcst], [btab])
            self.act(lambda e, lb=lb: e.activation(out=kdc[:, 1:2], in_=jcolb, func=AF.Exp, scale=lb), [blg, self.bcst], [btab])
            self.act(lambda e, lf=lf: e.activation(out=kdc[:, 2:3], in_=c128, func=AF.Exp, scale=lf), [blg, self.bcst], [btab])
            self.act(lambda e, lb=lb: e.activation(out=kdc[:, 3:4], in_=c128, func=AF.Exp, scale=lb), [blg, self.bcst], [btab])
            self.dve(lambda e: e.tensor_scalar(out=kdc[:, 0:2], in0=kdc[:, 0:2], scalar1=SC, scalar2=None, op0=ALU.mult), [btab], [btab])
            for tb in range(18):
                q, bq_ = self.qbank()
                qb = q.bitcast(BF16)[:, 0:128]
                self.pe(lambda e, qb=qb, tb=tb: e.transpose(out=qb, in_=kT[:, tb * 128:(tb + 1) * 128], identity=identb), [bhd, self.bcst], [bq_])
                for d in range(2):
                    self.act(lambda e, qb=qb, tb=tb, d=d: e.activation(out=ktk[tb][d], in_=qb, func=AF.Copy, scale=kdc[:, d:d + 1]), [bq_, btab], [bktk])
            for tb in range(16):
                tsl = slice(tb * 128, (tb + 1) * 128)
                q, bq_ = self.qbank()
                self.mm(q, [(kT[:, tsl], qT[:, tsl])], [bhd], bq_)
                s_, bs_ = sm_[k3 % 3], bsm_[k3 % 3]
                k3 += 1
                self.dve(lambda e, s_=s_, q=q: e.tensor_tensor(out=s_, in0=q, in1=Dc, op=ALU.mult), [bq_, btab], [bs_])
                for vc in range(2):
                    q2, bq2 = self.qbank()
                    self.mm(q2, [(vtok[:, tb, vc * 128:(vc + 1) * 128], s_)], [bhd, bs_], bq2)
                    self.act(lambda e, q2=q2, vc=vc, tsl=tsl: e.activation(out=oacc[:, vc, tsl], in_=q2, func=AF.Copy), [bq2], [boacc])
            for d in range(2):
                self.pool(lambda e, d=d: e.memset(R32[d], 0.0), [], [bR[d]])
                self.pool(lambda e, d=d: e.memset(Rbf[d], 0.0), [bR[d]], [bR[d]])
                order = [16, 17] + list(range(16)) if d == 0 else [17, 16] + list(range(15, -1, -1))
                for tb in order:
                    tsl = slice(tb * 128, (tb + 1) * 128)
                    if tb < 16:
                        qd, bqd = qd_[k3 % 3], bqd_[k3 % 3]
                        k3 += 1
                        self.pool(lambda e, qd=qd, tsl=tsl, d=d: e.tensor_tensor(out=qd, in0=qT[:, tsl], in1=qdt[d], op=ALU.mult), [bhd, btab], [bqd])
                        for vc in range(2):
                            q2, bq2 = self.qbank()
                            self.mm(q2, [(Rbf[d][:, vc * 128:(vc + 1) * 128], qd)], [bR[d], bqd], bq2)
                            self.dve(lambda e, q2=q2, vc=vc, tsl=tsl: e.tensor_tensor(out=oacc[:, vc, tsl], in0=q2, in1=oacc[:, vc, tsl], op=ALU.add),
                                     [bq2, boacc], [boacc])
                    qr, bqr = self.qbank4()
                    self.mm(qr[:, 0:256], [(ktk[tb][d], vtok[:, tb, :])], [bktk, bhd], bqr)
                    self.dve(lambda e, qr=qr, d=d: e.scalar_tensor_tensor(out=R32[d], in0=R32[d], scalar=kdc[:, 2 + d:3 + d], in1=qr[:, 0:256],
                                                                          op0=ALU.mult, op1=ALU.add), [bqr, bR[d], btab], [bR[d]])
                    self.act(lambda e, d=d: e.activation(out=Rbf[d], in_=R32[d], func=AF.Copy), [bR[d]], [bR[d]])
            for ti, (t0, tl) in enumerate(TILES[:4]):
                s0, s1, s2, s3 = st
                b0_, b1_, b2_, b3_ = bst
                bk1, bb1 = self.qbank4()
                self.mm(bk1, [(onesf, oacc[:, vc, t0:t0 + tl]) for vc in range(2)], [boacc, self.bcst], bb1)
                bk2, bb2 = self.qbank4()
                for vc in range(2):
                    self.act(lambda e, vc=vc, t0=t0, tl=tl: e.activation(out=(s0 if vc == 0 else s3)[:, 0:tl], in_=oacc[:, vc, t0:t0 + tl], func=AF.Square),
                             [boacc], [b0_ if vc == 0 else b3_])
                self.mm(bk2, [(onesf, s0), (onesf, s3)], [b0_, b3_, self.bcst], bb2)
                self.act(lambda e, bk1=bk1: e.activation(out=s1, in_=bk1, func=AF.Copy, scale=1.0 / 256), [bb1], [b1_])
                self.dve(lambda e: e.tensor_tensor(out=s2, in0=s1, in1=s1, op=ALU.mult), [b1_], [b2_])
                self.dve(lambda e, bk2=bk2: e.scalar_tensor_tensor(out=s2, in0=bk2, scalar=1.0 / 256, in1=s2, op0=ALU.mult, op1=ALU.subtract), [bb2, b2_], [b2_])
                self.act(lambda e: e.activation(out=s2, in_=s2, func=AF.Sqrt, bias=self.C('eps6'), scale=1.0), [b2_, self.bcst], [b2_])
                self.dve(lambda e: e.reciprocal(out=s2, in_=s2), [b2_], [b2_])
                for vc in range(2):
                    c = 2 * h + vc
                    yr, byr = yrow[c % 2], byrow[c % 2]
                    self.dve(lambda e, vc=vc, t0=t0, tl=tl: e.tensor_tensor(out=s0, in0=oacc[:, vc, t0:t0 + tl], in1=s1, op=ALU.subtract), [boacc, b1_, b3_], [b0_])
                    self.pool(lambda e: e.tensor_tensor(out=s0, in0=s0, in1=s2, op=ALU.mult), [b0_, b2_], [b0_])
                    self.act(lambda e, c=c: e.activation(out=s0, in_=s0, func=AF.Identity, bias=gnb[:, c:c + 1], scale=gng[:, c:c + 1]), [b0_, self.bpk], [b0_])
                    self.pool(lambda e, yr=yr, vc=vc, t0=t0, tl=tl: e.tensor_tensor(out=yr[:, t0:t0 + tl], in0=s0, in1=rg2[:, vc, t0:t0 + tl], op=ALU.mult),
                              [b0_, bhd], [byr])
            for vc in range(2):
                c = 2 * h + vc
                self.dma_sp(yin_s[8 + c], yrow[c % 2], [byrow[c % 2]], [self.byin_s[8 + c]])
        A.release()
        self.S.barrier()


class MK(MK1):
    def __init__(self, debug=(), only=None):
        self.debug = set(debug)
        self.only = only
        nc = self.nc = bass.Bass("TRN2", target_bir_lowering=False)
        self.S = Sched()
        self.A = Arena(nc)
        din = lambda name, shape, dt=F32: nc.dram_tensor(name, shape, dt, kind="ExternalInput").ap()
        self.x = din('x', [NLAT, D])
        self.ctx = din('ctx', [NCTX, D])
        self.cc = din('cc', [128, 32])
        npk = sum(w for _, w in pack_layout())
        self.pkd = din('pk', [128, npk])
        ncst = sum(w for _, w in CST_LAYOUT)
        self.cstd = din('cst', [128, ncst])
        self.roped = din('rope', [2, 128, NLAT])
        self.W = {k: din(k, WEIGHT_SHAPES[k] if (only is None or k in only) else [1, 1, 1]) for k in WEIGHT_NAMES}
        if only is not None:
            self.u1T = din('u1T', [16, 128, TT])
        self.out = nc.dram_tensor('out', [NLAT, D], F32, kind="ExternalOutput").ap()
        self.bout = Buf('out')
        self.scr = {}
        self.banks = [nc.alloc_psum_tensor('ps%d' % i, [128, 512], F32) for i in range(8)]
        self.bbufs = [Buf('bank%d' % i) for i in range(8)]
        self.bank_rr = 0
        A = self.A
        self.pk = A.alloc([npk], F32)
        self.cst = A.alloc([ncst], F32)
        self.cstb = A.alloc([ncst], BF16)
        self.modv = A.alloc([2, 96, 2], F32)
        self.bpk, self.bcst, self.bmod = Buf('pk'), Buf('cst'), Buf('mod')
        self.pkc = {}
        o = 0
        for name, w in pack_layout():
            self.pkc[name] = (o, w)
            o += w
        self.cc_ = {}
        o = 0
        for name, w in CST_LAYOUT:
            self.cc_[name] = (o, w)
            o += w
        self.dma_sp(self.pk, self.pkd, [], [self.bpk])
        self.dma_sp(self.cst, self.cstd, [], [self.bcst])
        self.dve(lambda e: e.tensor_copy(out=self.cstb, in_=self.cst), [self.bcst], [self.bcst])

    def scratch(self, name, shape, dt):
        kind = "ExternalOutput" if name in self.debug else "Internal"
        t = self.nc.dram_tensor(name, shape, dt, kind=kind).ap()
        self.scr[name] = t
        return t

    def P(self, name):
        o, w = self.pkc[name]
        return self.pk[:, o:o + w]

    def C(self, name, bf=False):
        o, w = self.cc_[name]
        return (self.cstb if bf else self.cst)[:, o:o + w]

    def pe(self, fn, r, w):
        return self.S.op('pe', fn, r, w)

    def act(self, fn, r, w):
        return self.S.op('act', fn, r, w)

    def dve(self, fn, r, w):
        return self.S.op('dve', fn, r, w)

    def pool(self, fn, r, w):
        return self.S.op('pool', fn, r, w)

    def dma_sp(self, out, in_, r, w):
        return self.S.op('sp', lambda e: e.dma_start(out=out, in_=in_), r, w, dma=True)

    def dma_cast(self, out, in_, r, w):
        return self.S.op('pool', lambda e: e.dma_start(out=out, in_=in_), r, w, dma=True)

    def bank(self):
        i = self.bank_rr % 8
        self.bank_rr += 1
        return self.banks[i][:, :], self.bbufs[i]

    def mm(self, out, pairs, r, wbuf):
        n = len(pairs)
        for i, (l, rr) in enumerate(pairs):
            self.pe(lambda e, l=l, rr=rr, i=i: e.matmul(out, lhsT=l, rhs=rr, start=(i == 0), stop=(i == n - 1)), r, [wbuf])

    def wpool(self, nbuf, nelem):
        self.wb_aps = [self.A.alloc([nelem], BF16) for _ in range(nbuf)]
        self.wb_bufs = [Buf('w%d' % i) for i in range(nbuf)]
        self.wb_rr = 0

    def wload(self, src, kc, n, rows=128):
        i = self.wb_rr % len(self.wb_aps)
        self.wb_rr += 1
        ap = self.wb_aps[i][0:rows, 0:kc * n].rearrange('p (c n) -> p c n', n=n)
        b = self.wb_bufs[i]
        self.dma_cast(ap, src.rearrange('(c p) n -> p c n', p=rows), [], [b])
        return ap, b

    def phase_mod(self):
        A = self.A
        A.mark()
        c32 = A.alloc([32], F32)
        sT = A.alloc([16, 2], BF16)
        bc, bs = Buf(), Buf()
        self.dma_sp(c32, self.cc, [], [bc])
        self.act(lambda e: e.activation(out=sT, in_=c32.rearrange('p (c t) -> p c t', t=2), func=AF.Silu), [bc], [bs])
        self.wpool(3, 16 * 512)
        for L in range(2):
            bk, bb = self.bank()
            for g in range(24):
                wt, wb = self.wload(self.W['mod_w'][L][:, g * 512:(g + 1) * 512], 16, 512)
                for oc in range(4):
                    j = g * 4 + oc
                    self.mm(bk[:, 2 * j:2 * j + 2], [(wt[:, c, oc * 128:(oc + 1) * 128], sT[:, c, :]) for c in range(16)],
                            [wb, bs], bb)
            mv = self.modv[:, L]
            self.dve(lambda e, bk=bk, mv=mv, L=L: e.tensor_tensor(
                out=mv, in0=bk[:, 0:192].rearrange('p (a b) -> p a b', b=2),
                in1=self.P('modb%d' % L).unsqueeze(2).broadcast_to([128, 96, 2]), op=ALU.add), [bb, self.bpk], [self.bmod])
            for (lo, gname) in ((16, 'gmix%d' % L), (64, 'gffn%d' % L)):
                self.dve(lambda e, mv=mv, lo=lo, gname=gname: e.scalar_tensor_tensor(
                    out=mv[:, lo:lo + 16, :], in0=mv[:, lo:lo + 16, :], scalar=1.0,
                    in1=self.P(gname).unsqueeze(2).broadcast_to([128, 16, 2]), op0=ALU.add, op1=ALU.mult),
                    [self.bmod, self.bpk], [self.bmod])
        A.release()
        self.S.barrier()

    def mvec(self, L, idx, c, col):
        return self.modv[:, L, idx * 16 + c, col:col + 1]

    def phase_in(self):
        A = self.A
        A.mark()
        hT = self.scratch('hT', [16, 128, TT], F32)
        self.bhT = [Buf('hT%d' % c) for c in range(16)]
        xt = [A.alloc([4, D], F32) for _ in range(2)]
        bx = [Buf(), Buf()]
        ht = [A.alloc([16, 512], F32) for _ in range(2)]
        bh = [Buf(), Buf()]
        identf = self.C('ident')
        for ti, (t0, tl) in enumerate(TILES):
            nb = tl // 128
            xx, bxx = xt[ti % 2], bx[ti % 2]
            hh, bhh = ht[ti % 2], bh[ti % 2]
            src = self.x[t0:t0 + tl, :] if ti < 4 else self.ctx[:, :]
            self.dma_sp(xx[:, 0:nb, :], src.rearrange('(b p) f -> p b f', p=128), [], [bxx])
            for c in range(16):
                bk, bb = self.bank()
                for b in range(nb):
                    self.pe(lambda e, bk=bk, xx=xx, b=b, c=c: e.transpose(out=bk[:, b * 128:(b + 1) * 128],
                                                                           in_=xx[:, b, c * 128:(c + 1) * 128], identity=identf),
                            [bxx, self.bcst], [bb])
                if c % 2 == 0:
                    self.act(lambda e, bk=bk, hh=hh, c=c, tl=tl: e.activation(out=hh[:, c, 0:tl], in_=bk[:, 0:tl], func=AF.Copy), [bb], [bhh])
                else:
                    self.dve(lambda e, bk=bk, hh=hh, c=c, tl=tl: e.tensor_copy(out=hh[:, c, 0:tl], in_=bk[:, 0:tl]), [bb], [bhh])
            self.dma_sp(hT[:, :, t0:t0 + tl].rearrange('c p t -> p c t'), hh[:, :, 0:tl], [bhh], self.bhT)
        A.release()
        self.S.barrier()

    def phase_norm(self, L, which, uT, buT, tiles=TILES):
        A = self.A
        A.mark()
        hT = self.scr['hT']
        ht = [A.alloc([16, 512], F32) for _ in range(2)]
        bh = [Buf(), Buf()]
        sq = A.alloc([16, 512], BF16)
        bsq = Buf()
        rstd = A.alloc([512], F32)
        brs = Buf()
        tmp = [A.alloc([512], F32) for _ in range(3)]
        btmp = [Buf() for _ in range(3)]
        ones = self.C('ones', True)
        ia, ib = (1, 0) if which == 0 else (4, 3)
        k = 0
        for ti, (t0, tl) in enumerate(tiles):
            col = 0 if ti < 4 else 1
            hh, bhh = ht[ti % 2], bh[ti % 2]
            self.dma_sp(hh[:, :, 0:tl], hT[:, :, t0:t0 + tl].rearrange('c p t -> p c t'), self.bhT, [bhh])
            self.act(lambda e, hh=hh, tl=tl: e.activation(out=sq[:, :, 0:tl], in_=hh[:, :, 0:tl], func=AF.Square), [bhh], [bsq])
            bk, bb = self.bank()
            self.mm(bk[:, 0:tl], [(ones, sq[:, c, 0:tl]) for c in range(16)], [bsq, self.bcst], bb)
            self.act(lambda e, bk=bk, tl=tl: e.activation(out=rstd[:, 0:tl], in_=bk[:, 0:tl], func=AF.Sqrt,
                                                          bias=self.C('eps6'), scale=1.0 / D), [bb, self.bcst], [brs])
            self.dve(lambda e, tl=tl: e.reciprocal(out=rstd[:, 0:tl], in_=rstd[:, 0:tl]), [brs], [brs])
            for c in range(16):
                tm, btm = tmp[k % 3], btmp[k % 3]
                k += 1
                self.dve(lambda e, tm=tm, hh=hh, c=c, tl=tl, col=col: e.scalar_tensor_tensor(
                    out=tm[:, 0:tl], in0=hh[:, c, 0:tl], scalar=self.mvec(L, ia, c, col), in1=rstd[:, 0:tl],
                    op0=ALU.mult, op1=ALU.mult), [bhh, brs, self.bmod], [btm])
                self.act(lambda e, tm=tm, c=c, t0=t0, tl=tl, col=col: e.activation(
                    out=uT[:, c, t0:t0 + tl], in_=tm[:, 0:tl], func=AF.Identity, bias=self.mvec(L, ib, c, col), scale=1.0),
                    [btm, self.bmod], [buT])
        A.release()
        self.S.barrier()

    def phase_resid_proj(self, L, gidx, actT, bact, KC, w_dram, tiles=TILES):
        A = self.A
        A.mark()
        hT = self.scr['hT']
        self.wpool(2, KC * 512)
        hrow = [A.alloc([TT], F32) for _ in range(3)]
        bhr = [Buf() for _ in range(3)]
        for og in range(4):
            wt, wb = self.wload(w_dram[:, og * 512:(og + 1) * 512], KC, 512)
            for o4 in range(4):
                oc = og * 4 + o4
                hr, bh = hrow[oc % 3], bhr[oc % 3]
                self.dma_sp(hr, hT[oc], [self.bhT[oc]], [bh])
                for ti, (t0, tl) in enumerate(tiles):
                    col = 0 if ti < 4 else 1
                    bk, bb = self.bank()
                    self.mm(bk[:, 0:tl], [(wt[:, c, o4 * 128:(o4 + 1) * 128], actT[:, c, t0:t0 + tl]) for c in range(KC)],
                            [wb, bact], bb)
                    self.dve(lambda e, bk=bk, hr=hr, t0=t0, tl=tl, oc=oc, col=col: e.scalar_tensor_tensor(
                        out=hr[:, t0:t0 + tl], in0=bk[:, 0:tl], scalar=self.mvec(L, gidx, oc, col), in1=hr[:, t0:t0 + tl],
                        op0=ALU.mult, op1=ALU.add), [bb, bh, self.bmod], [bh])
                self.dma_sp(hT[oc], hr, [bh], [self.bhT[oc]])
        A.release()
        self.S.barrier()

    def phase_ffn_up(self, L, fT, bfT, tiles=TILES):
        A = self.A
        A.mark()
        hid = self.scr.get('hid')
        if hid is None:
            hid = self.scratch('hid', [44, 128, TT], BF16)
            self.bhid = [Buf('hid%d' % c) for c in range(44)]
        GP = 2308
        self.wpool(4, 16 * 512)
        gpad = [A.alloc([GP], BF16) for _ in range(2)]
        bgp = [Buf(), Buf()]
        vsb = [A.alloc([TT], BF16) for _ in range(2)]
        bvs = [Buf(), Buf()]
        dg = [A.alloc([3, 128], BF16) for _ in range(2)]
        bdg = [Buf(), Buf()]
        sg = [A.alloc([512], BF16) for _ in range(3)]
        bsg = [Buf() for _ in range(3)]
        hrow = [A.alloc([TT], BF16) for _ in range(3)]
        bhr = [Buf() for _ in range(3)]
        for i in range(2):
            self.pool(lambda e, i=i: e.memset(gpad[i], 0.0), [], [bgp[i]])
        wup = self.W['ffn_w_up'][L]
        fcw = self.P('fcw%d' % L).rearrange('p (c j) -> p c j', j=3)
        fcb = self.P('fcb%d' % L)
        identb = self.C('ident', True)
        k = 0
        for g in range(11):
            wg, bwg = self.wload(wup[:, g * 512:(g + 1) * 512], 16, 512)
            wv, bwv = self.wload(wup[:, FFN + g * 512:FFN + (g + 1) * 512], 16, 512)
            for c4 in range(4):
                c = g * 4 + c4
                gp, bg = gpad[c % 2], bgp[c % 2]
                vs, bv = vsb[c % 2], bvs[c % 2]
                dd, bd = dg[c % 2], bdg[c % 2]
                hr, bh = hrow[c % 3], bhr[c % 3]
                for j in range(3):
                    self.pool(lambda e, dd=dd, j=j, c=c: e.tensor_scalar(out=dd[:, j, :], in0=identb, scalar1=fcw[:, c, j:j + 1],
                                                                        scalar2=None, op0=ALU.mult), [self.bcst, self.bpk], [bd])
                for ti, (t0, tl) in enumerate(tiles):
                    off = 1 + t0 if ti < 4 else 2051
                    bk, bb = self.bank()
                    self.mm(bk[:, 0:tl], [(wg[:, kc, c4 * 128:(c4 + 1) * 128], fT[:, kc, t0:t0 + tl]) for kc in range(16)], [bwg, bfT], bb)
                    self.act(lambda e, bk=bk, gp=gp, off=off, tl=tl: e.activation(out=gp[:, off:off + tl], in_=bk[:, 0:tl], func=AF.Copy), [bb], [bg])
                    bk2, bb2 = self.bank()
                    self.mm(bk2[:, 0:tl], [(wv[:, kc, c4 * 128:(c4 + 1) * 128], fT[:, kc, t0:t0 + tl]) for kc in range(16)], [bwv, bfT], bb2)
                    self.dve(lambda e, bk2=bk2, vs=vs, t0=t0, tl=tl: e.tensor_copy(out=vs[:, t0:t0 + tl], in_=bk2[:, 0:tl]), [bb2], [bv])
                for ti, (t0, tl) in enumerate(tiles):
                    base = t0 if ti < 4 else 2050
                    bk, bb = self.bank()
                    self.mm(bk[:, 0:tl], [(dd[:, j, :], gp[:, base + j:base + j + tl]) for j in range(3)], [bd, bg], bb)
                    s_, bs_ = sg[k % 3], bsg[k % 3]
                    k += 1
                    self.act(lambda e, bk=bk, s_=s_, tl=tl, c=c: e.activation(out=s_[:, 0:tl], in_=bk[:, 0:tl], func=AF.Silu,
                                                                             bias=fcb[:, c:c + 1], scale=1.0), [bb, self.bpk], [bs_])
                    self.pool(lambda e, s_=s_, hr=hr, vs=vs, t0=t0, tl=tl: e.tensor_tensor(out=hr[:, t0:t0 + tl], in0=s_[:, 0:tl],
                                                                                         in1=vs[:, t0:t0 + tl], op=ALU.mult), [bs_, bv], [bh])
                self.dma_sp(hid[c], hr, [bh], [self.bhid[c]])
        A.release()
        self.S.barrier()

    def phase_ffn_down(self, L, KG=4, tiles=TILES):
        A = self.A
        A.mark()
        hid = self.scr['hid']
        hT = self.scr['hT']
        wdn = self.W['ffn_w_down'][L]
        ng = 44 // KG
        self.wpool(2, KG * 1024)
        hg = [A.alloc([KG, TT], BF16) for _ in range(2)]
        bhg = [Buf(), Buf()]
        acc = A.alloc([8, TT], F32)
        bacc = [Buf('acc%d' % i) for i in range(8)]
        hrow = [A.alloc([TT], F32) for _ in range(2)]
        bhr = [Buf(), Buf()]
        n = 0
        for half in range(2):
            for kg in range(ng):
                h_, bh_ = hg[n % 2], bhg[n % 2]
                n += 1
                self.dma_sp(h_, hid[kg * KG:(kg + 1) * KG].rearrange('c p t -> p c t'), self.bhid[kg * KG:(kg + 1) * KG], [bh_])
                wt, wb = self.wload(wdn[kg * KG * 128:(kg + 1) * KG * 128, half * 1024:(half + 1) * 1024], KG, 1024)
                for o in range(8):
                    for ti, (t0, tl) in enumerate(tiles):
                        bk, bb = self.bank()
                        self.mm(bk[:, 0:tl], [(wt[:, kc, o * 128:(o + 1) * 128], h_[:, kc, t0:t0 + tl]) for kc in range(KG)], [wb, bh_], bb)
                        if kg == 0:
                            self.act(lambda e, bk=bk, o=o, t0=t0, tl=tl: e.activation(out=acc[:, o, t0:t0 + tl], in_=bk[:, 0:tl], func=AF.Copy),
                                     [bb], [bacc[o]])
                        else:
                            self.dve(lambda e, bk=bk, o=o, t0=t0, tl=tl: e.tensor_tensor(out=acc[:, o, t0:t0 + tl], in0=bk[:, 0:tl],
                                                                                      in1=acc[:, o, t0:t0 + tl], op=ALU.add), [bb, bacc[o]], [bacc[o]])
            for o in range(8):
                oc = half * 8 + o
                hr, bh = hrow[o % 2], bhr[o % 2]
                self.dma_sp(hr, hT[oc], [self.bhT[oc]], [bh])
                for (lo, hi, col) in (((0, NLAT, 0), (NLAT, TT, 1)) if len(tiles) == 5 else ((0, NLAT, 0),)):
                    self.dve(lambda e, hr=hr, o=o, oc=oc, lo=lo, hi=hi, col=col: e.scalar_tensor_tensor(
                        out=hr[:, lo:hi], in0=acc[:, o, lo:hi], scalar=self.mvec(L, 5, oc, col), in1=hr[:, lo:hi],
                        op0=ALU.mult, op1=ALU.add), [bacc[o], bh, self.bmod], [bh])
                self.dma_sp(hT[oc], hr, [bh], [self.bhT[oc]])
        A.release()
        self.S.barrier()

    def phase_final(self):
        A = self.A
        A.mark()
        hT = self.scr['hT']
        ht = [A.alloc([16, 512], F32) for _ in range(2)]
        bh = [Buf(), Buf()]
        sq = A.alloc([16, 512], BF16)
        bsq = Buf()
        rstd = A.alloc([512], F32)
        brs = Buf()
        yn = [A.alloc([16, 512], F32) for _ in range(2)]
        byn = [Buf(), Buf()]
        ot = [A.alloc([D], F32) for _ in range(3)]
        bot = [Buf() for _ in range(3)]
        ones = self.C('ones', True)
        identf = self.C('ident')
        gfin = self.P('gfin')
        k = 0
        for ti, (t0, tl) in enumerate(TILES[:4]):
            hh, bhh = ht[ti % 2], bh[ti % 2]
            y_, by_ = yn[ti % 2], byn[ti % 2]
            self.dma_sp(hh, hT[:, :, t0:t0 + tl].rearrange('c p t -> p c t'), self.bhT, [bhh])
            self.act(lambda e, hh=hh: e.activation(out=sq, in_=hh, func=AF.Square), [bhh], [bsq])
            bk, bb = self.bank()
            self.mm(bk, [(ones, sq[:, c, :]) for c in range(16)], [bsq, self.bcst], bb)
            self.act(lambda e, bk=bk: e.activation(out=rstd, in_=bk, func=AF.Sqrt, bias=self.C('eps6'), scale=1.0 / D), [bb, self.bcst], [brs])
            self.dve(lambda e: e.reciprocal(out=rstd, in_=rstd), [brs], [brs])
            for c in range(16):
                self.dve(lambda e, hh=hh, y_=y_, c=c: e.scalar_tensor_tensor(out=y_[:, c, :], in0=hh[:, c, :], scalar=gfin[:, c:c + 1],
                                                                       in1=rstd, op0=ALU.mult, op1=ALU.mult), [bhh, brs, self.bpk], [by_])
            for b in range(4):
                o_, bo_ = ot[k % 3], bot[k % 3]
                k += 1
                for fg in range(4):
                    bk, bb = self.bank()
                    for f4 in range(4):
                        c = fg * 4 + f4
                        self.pe(lambda e, bk=bk, y_=y_, c=c, b=b, f4=f4: e.transpose(out=bk[:, f4 * 128:(f4 + 1) * 128],
                                                                                      in_=y_[:, c, b * 128:(b + 1) * 128], identity=identf),
                                [by_, self.bcst], [bb])
                    if fg % 2 == 0:
                        self.act(lambda e, bk=bk, o_=o_, fg=fg: e.activation(out=o_[:, fg * 512:(fg + 1) * 512], in_=bk, func=AF.Copy), [bb], [bo_])
                    else:
                        self.dve(lambda e, bk=bk, o_=o_, fg=fg: e.tensor_copy(out=o_[:, fg * 512:(fg + 1) * 512], in_=bk), [bb], [bo_])
                r0 = t0 + b * 128
                self.dma_sp(self.out[r0:r0 + 128, :], o_, [bo_], [self.bout])
        A.release()
        self.S.barrier()

    def phase_ab_qkv(self, uT, buT, qT, kT, vtok, bq, bkk, bv):
        A = self.A
        A.mark()
        w_in = self.W['ab_w_in'][0]
        rope = A.alloc([2, NLAT], F32)
        brope = Buf()
        self.dma_sp(rope, self.roped.rearrange('a p t -> p a t'), [], [brope])
        self.wpool(2, 16 * 256)
        ws_ap = [A.alloc([16 * 256], BF16) for _ in range(2)]
        bws = [Buf(), Buf()]
        sq = [A.alloc([512], BF16) for _ in range(2)]
        bsq = [Buf(), Buf()]
        rstd = [A.alloc([512], F32) for _ in range(2)]
        brs = [Buf(), Buf()]
        t1 = [A.alloc([512], F32) for _ in range(2)]
        bt1 = [Buf(), Buf()]
        t2 = [A.alloc([512], F32) for _ in range(2)]
        bt2 = [Buf(), Buf()]
        ones = self.C('ones', True)
        k = 0
        groups = [(g * 256, 2, qT, bq, 'qg', 'qgs', 2 * g) for g in range(4)] + [(1024, 2, kT, bkk, 'kg', 'kgs', 0)]
        for gi, (col0, nh, dst, bdst, gn, gsn, h0) in enumerate(groups):
            n = nh * 128
            wt, wb = self.wload(w_in[:, col0:col0 + n], 16, n)
            ws = ws_ap[gi % 2][:, 0:16 * n].rearrange('p (c n) -> p c n', n=n)
            bw_ = bws[gi % 2]
            wtv = wt.rearrange('p c (g b e) -> p (c g) b e', b=2, e=32)
            wsv = ws.rearrange('p c (g b e) -> p (c g) b e', b=2, e=32)
            for b in range(2):
                self.pool(lambda e, wsv=wsv, wtv=wtv, b=b: e.tensor_copy(out=wsv[:, :, b, :], in_=wtv[:, :, 1 - b, :]), [wb], [bw_])
            g_ap, gs_ap = self.P(gn), self.P(gsn)
            for hh in range(nh):
                hd = h0 + hh
                for ti, (t0, tl) in enumerate(TILES):
                    i2 = k % 2
                    k += 1
                    bkq, bbq = self.bank()
                    self.mm(bkq[:, 0:tl], [(wt[:, c, hh * 128:(hh + 1) * 128], uT[:, c, t0:t0 + tl]) for c in range(16)], [wb, buT], bbq)
                    self.act(lambda e, bkq=bkq, i2=i2, tl=tl: e.activation(out=sq[i2][:, 0:tl], in_=bkq[:, 0:tl], func=AF.Square), [bbq], [bsq[i2]])
                    bks, bbs = self.bank()
                    self.mm(bks[:, 0:tl], [(ones, sq[i2][:, 0:tl])], [bsq[i2], self.bcst], bbs)
                    self.act(lambda e, bks=bks, i2=i2, tl=tl: e.activation(out=rstd[i2][:, 0:tl], in_=bks[:, 0:tl], func=AF.Sqrt,
                                                                          bias=self.C('eps6'), scale=1.0 / 128), [bbs, self.bcst], [brs[i2]])
                    self.dve(lambda e, i2=i2, tl=tl: e.reciprocal(out=rstd[i2][:, 0:tl], in_=rstd[i2][:, 0:tl]), [brs[i2]], [brs[i2]])
                    if ti < 4:
                        bkw, bbw = self.bank()
                        self.mm(bkw[:, 0:tl], [(ws[:, c, hh * 128:(hh + 1) * 128], uT[:, c, t0:t0 + tl]) for c in range(16)], [bw_, buT], bbw)
                        self.dve(lambda e, bkq=bkq, i2=i2, t0=t0, tl=tl, g_ap=g_ap: e.scalar_tensor_tensor(
                            out=t1[i2][:, 0:tl], in0=bkq[:, 0:tl], scalar=g_ap, in1=rope[:, 0, t0:t0 + tl], op0=ALU.mult, op1=ALU.mult),
                            [bbq, brope, self.bpk], [bt1[i2]])
                        self.dve(lambda e, bkw=bkw, i2=i2, t0=t0, tl=tl, gs_ap=gs_ap: e.scalar_tensor_tensor(
                            out=t2[i2][:, 0:tl], in0=bkw[:, 0:tl], scalar=gs_ap, in1=rope[:, 1, t0:t0 + tl], op0=ALU.mult, op1=ALU.mult),
                            [bbw, brope, self.bpk], [bt2[i2]])
                        self.pool(lambda e, i2=i2, tl=tl: e.tensor_tensor(out=t1[i2][:, 0:tl], in0=t1[i2][:, 0:tl], in1=t2[i2][:, 0:tl], op=ALU.add),
                                  [bt1[i2], bt2[i2]], [bt1[i2]])
                        self.pool(lambda e, i2=i2, dst=dst, hd=hd, t0=t0, tl=tl: e.tensor_tensor(
                            out=dst[:, hd, t0:t0 + tl], in0=t1[i2][:, 0:tl], in1=rstd[i2][:, 0:tl], op=ALU.mult), [bt1[i2], brs[i2]], [bdst])
                    else:
                        self.dve(lambda e, bkq=bkq, i2=i2, dst=dst, hd=hd, t0=t0, tl=tl, g_ap=g_ap: e.scalar_tensor_tensor(
                            out=dst[:, hd, t0:t0 + tl], in0=bkq[:, 0:tl], scalar=g_ap, in1=rstd[i2][:, 0:tl], op0=ALU.mult, op1=ALU.mult),
                            [bbq, brs[i2], self.bpk], [bdst])
        wv, bwv = self.wload(w_in[:, 1280:1536], 16, 256)
        for tb in range(18):
            bk, bb = self.bank()
            self.mm(bk[:, 0:256], [(uT[:, c, tb * 128:(tb + 1) * 128], wv[:, c, :]) for c in range(16)], [bwv, buT], bb)
            self.act(lambda e, bk=bk, tb=tb: e.activation(out=vtok[:, tb, :], in_=bk[:, 0:256], func=AF.Copy), [bb], [bv])
        A.release()
        self.S.barrier()

    def phase_ab_conv(self, uT, buT):
        A = self.A
        A.mark()
        w_in = self.W['ab_w_in'][0]
        cv = self.scratch('cv', [8, 128, TT], F32)
        self.bcv = [Buf('cv%d' % c) for c in range(8)]
        GP = 2364
        self.wpool(4, 16 * 512)
        gpad = [A.alloc([GP], BF16) for _ in range(2)]
        bgp = [Buf(), Buf()]
        dg = [A.alloc([31, 128], BF16) for _ in range(2)]
        bdg = [Buf(), Buf()]
        sig = [A.alloc([512], F32) for _ in range(2)]
        bsig = [Buf(), Buf()]
        cvr = [A.alloc([TT], F32) for _ in range(2)]
        bcr = [Buf(), Buf()]
        for i in range(2):
            self.pool(lambda e, i=i: e.memset(gpad[i], 0.0), [], [bgp[i]])
        acw = self.P('acw').rearrange('p (c j) -> p c j', j=31)
        acb = self.P('acb')
        identb = self.C('ident', True)
        k = 0
        for g in range(2):
            wa, bwa = self.wload(w_in[:, 1536 + g * 512:1536 + (g + 1) * 512], 16, 512)
            wb_, bwb = self.wload(w_in[:, 2560 + g * 512:2560 + (g + 1) * 512], 16, 512)
            for c4 in range(4):
                cc = g * 4 + c4
                gp, bg = gpad[cc % 2], bgp[cc % 2]
                dd, bd = dg[cc % 2], bdg[cc % 2]
                cr, bc = cvr[cc % 2], bcr[cc % 2]
                for j in range(31):
                    self.pool(lambda e, dd=dd, j=j, cc=cc: e.tensor_scalar(out=dd[:, j, :], in0=identb, scalar1=acw[:, cc, j:j + 1],
                                                                          scalar2=None, op0=ALU.mult), [self.bcst, self.bpk], [bd])
                for ti, (t0, tl) in enumerate(TILES):
                    off = 15 + t0 if ti < 4 else 2093
                    i2 = k % 2
                    k += 1
                    bka, bba = self.bank()
                    self.mm(bka[:, 0:tl], [(wa[:, kc, c4 * 128:(c4 + 1) * 128], uT[:, kc, t0:t0 + tl]) for kc in range(16)], [bwa, buT], bba)
                    bkb, bbb = self.bank()
                    self.mm(bkb[:, 0:tl], [(wb_[:, kc, c4 * 128:(c4 + 1) * 128], uT[:, kc, t0:t0 + tl]) for kc in range(16)], [bwb, buT], bbb)
                    self.act(lambda e, bkb=bkb, i2=i2, tl=tl: e.activation(out=sig[i2][:, 0:tl], in_=bkb[:, 0:tl], func=AF.Sigmoid), [bbb], [bsig[i2]])
                    self.dve(lambda e, bka=bka, i2=i2, gp=gp, off=off, tl=tl: e.tensor_tensor(out=gp[:, off:off + tl], in0=bka[:, 0:tl],
                                                                                           in1=sig[i2][:, 0:tl], op=ALU.mult), [bba, bsig[i2]], [bg])
                for ti, (t0, tl) in enumerate(TILES):
                    base = t0 if ti < 4 else 2078
                    bk, bb = self.bank()
                    self.mm(bk[:, 0:tl], [(dd[:, j, :], gp[:, base + j:base + j + tl]) for j in range(31)], [bd, bg], bb)
                    self.act(lambda e, bk=bk, cr=cr, t0=t0, tl=tl, cc=cc: e.activation(out=cr[:, t0:t0 + tl], in_=bk[:, 0:tl], func=AF.Identity,
                                                                                    bias=acb[:, cc:cc + 1], scale=1.0), [bb, self.bpk], [bc])
                self.dma_sp(cv[cc], cr, [bc], [self.bcv[cc]])
        A.release()
        self.S.barrier()

    def phase_att(self, qT, kT, vtok, bq, bkk, bv, yin, byin):
        A = self.A
        A.mark()
        pT = [A.alloc([512], BF16) for _ in range(4)]
        bp = [Buf() for _ in range(4)]
        rinv = [A.alloc([512], F32) for _ in range(2)]
        bri = [Buf(), Buf()]
        ones = self.C('ones', True)
        scale = 128.0 ** -0.5
        k = 0
        n = 0
        for h in range(8):
            hk = h // 4
            for ti, (t0, tl) in enumerate(TILES):
                chunks = list(range(18)) if ti < 4 else [16, 17]
                ai = 2 * (n % 2)
                bko, bbo = self.banks[ai], self.bbufs[ai]
                bkr, bbr = self.banks[ai + 1], self.bbufs[ai + 1]
                nck = len(chunks)
                for ci, kc in enumerate(chunks):
                    bks, bbs = self.banks[4 + k % 4], self.bbufs[4 + k % 4]
                    self.mm(bks[:, 0:tl], [(kT[:, hk, kc * 128:(kc + 1) * 128], qT[:, h, t0:t0 + tl])], [bkk, bq], bbs)
                    p_, bp_ = pT[k % 4], bp[k % 4]
                    k += 1
                    self.act(lambda e, bks=bks, p_=p_, tl=tl: e.activation(out=p_[:, 0:tl], in_=bks[:, 0:tl], func=AF.Exp, scale=scale), [bbs], [bp_])
                    self.pe(lambda e, bko=bko, p_=p_, kc=kc, hk=hk, tl=tl, ci=ci, nck=nck: e.matmul(
                        bko[:, 0:tl], lhsT=vtok[:, kc, hk * 128:(hk + 1) * 128], rhs=p_[:, 0:tl], start=(ci == 0), stop=(ci == nck - 1)),
                        [bv, bp_], [bbo])
                    self.pe(lambda e, bkr=bkr, p_=p_, tl=tl, ci=ci, nck=nck: e.matmul(
                        bkr[:, 0:tl], lhsT=ones, rhs=p_[:, 0:tl], start=(ci == 0), stop=(ci == nck - 1)), [self.bcst, bp_], [bbr])
                ri, bri_ = rinv[n % 2], bri[n % 2]
                n += 1
                self.dve(lambda e, bkr=bkr, ri=ri, tl=tl: e.reciprocal(out=ri[:, 0:tl], in_=bkr[:, 0:tl]), [bbr], [bri_])
                self.dve(lambda e, bko=bko, ri=ri, h=h, t0=t0, tl=tl: e.tensor_tensor(out=yin[:, h, t0:t0 + tl], in0=bko[:, 0:tl],
                                                                                  in1=ri[:, 0:tl], op=ALU.mult), [bbo, bri_], [byin])
        A.release()
        self.S.barrier()

    def phase_conv_ln(self, yin, byin):
        A = self.A
        A.mark()
        cv = self.scr['cv']
        ct = [A.alloc([8, 512], F32) for _ in range(2)]
        bct = [Buf(), Buf()]
        sq = A.alloc([8, 512], F32)
        bsq = Buf()
        mean = A.alloc([512], F32)
        msq = A.alloc([512], F32)
        rstd = A.alloc([512], F32)
        bst = Buf()
        xc = [A.alloc([512], F32) for _ in range(2)]
        bxc = [Buf(), Buf()]
        onesf = self.C('ones')
        ang, anb = self.P('ang'), self.P('anb')
        k = 0
        for ti, (t0, tl) in enumerate(TILES):
            c_, bc_ = ct[ti % 2], bct[ti % 2]
            self.dma_sp(c_[:, :, 0:tl], cv[:, :, t0:t0 + tl].rearrange('c p t -> p c t'), self.bcv, [bc_])
            self.act(lambda e, c_=c_, tl=tl: e.activation(out=sq[:, :, 0:tl], in_=c_[:, :, 0:tl], func=AF.Square), [bc_], [bsq])
            bk1, bb1 = self.bank()
            self.mm(bk1[:, 0:tl], [(onesf, c_[:, c, 0:tl]) for c in range(8)], [bc_, self.bcst], bb1)
            bk2, bb2 = self.bank()
            self.mm(bk2[:, 0:tl], [(onesf, sq[:, c, 0:tl]) for c in range(8)], [bsq, self.bcst], bb2)
            self.act(lambda e, bk1=bk1, tl=tl: e.activation(out=mean[:, 0:tl], in_=bk1[:, 0:tl], func=AF.Copy, scale=1.0 / 1024), [bb1], [bst])
            self.dve(lambda e, tl=tl: e.tensor_tensor(out=msq[:, 0:tl], in0=mean[:, 0:tl], in1=mean[:, 0:tl], op=ALU.mult), [bst], [bst])
            self.dve(lambda e, bk2=bk2, tl=tl: e.scalar_tensor_tensor(out=rstd[:, 0:tl], in0=bk2[:, 0:tl], scalar=1.0 / 1024, in1=msq[:, 0:tl],
                                                                      op0=ALU.mult, op1=ALU.subtract), [bb2, bst], [bst])
            self.act(lambda e, tl=tl: e.activation(out=rstd[:, 0:tl], in_=rstd[:, 0:tl], func=AF.Sqrt, bias=self.C('eps6'), scale=1.0), [bst, self.bcst], [bst])
            self.dve(lambda e, tl=tl: e.reciprocal(out=rstd[:, 0:tl], in_=rstd[:, 0:tl]), [bst], [bst])
            for c in range(8):
                x_, bx_ = xc[k % 2], bxc[k % 2]
                k += 1
                self.dve(lambda e, x_=x_, c_=c_, c=c, tl=tl: e.tensor_tensor(out=x_[:, 0:tl], in0=c_[:, c, 0:tl], in1=mean[:, 0:tl], op=ALU.subtract),
                         [bc_, bst], [bx_])
                self.pool(lambda e, x_=x_, tl=tl: e.tensor_tensor(out=x_[:, 0:tl], in0=x_[:, 0:tl], in1=rstd[:, 0:tl], op=ALU.mult), [bx_, bst], [bx_])
                self.act(lambda e, x_=x_, c=c, t0=t0, tl=tl: e.activation(out=yin[:, 8 + c, t0:t0 + tl], in_=x_[:, 0:tl], func=AF.Silu,
                                                                        bias=anb[:, c:c + 1], scale=ang[:, c:c + 1]), [bx_, self.bpk], [byin])
        A.release()
        self.S.barrier()

    def build(self, stop_after=None, start_layer=0):
        A = self.A
        self.phase_mod()
        self.phase_in()
        if start_layer == 1:
            return self.build_l1()
        A.mark()
        uT = A.alloc([16, TT], BF16)
        buT = Buf('uT')
        self.phase_norm(0, 0, uT, buT)
        self.phase_ab_conv(uT, buT)
        qT = A.alloc([8, TT], BF16)
        kT = A.alloc([2, TT], BF16)
        vtok = A.alloc([18, 256], BF16)
        bq, bkk, bv = Buf('q'), Buf('k'), Buf('v')
        self.phase_ab_qkv(uT, buT, qT, kT, vtok, bq, bkk, bv)
        yin, byin = uT, buT
        self.phase_att(qT, kT, vtok, bq, bkk, bv, yin, byin)
        A.release()
        A.mark()
        yin = A.alloc([16, TT], BF16)
        self.phase_conv_ln(yin, byin)
        self.phase_resid_proj(0, 2, yin, byin, 16, self.W['ab_w_out'][0])
        if stop_after == 'mix0':
            A.release()
            return self.finish()
        self.phase_norm(0, 1, yin, byin)
        self.phase_ffn_up(0, yin, byin)
        A.release()
        self.phase_ffn_down(0)
        if stop_after == 'l0':
            return self.finish()
        return self.build_l1()

    def build_dbg(self, stop_after):
        A = self.A
        A.mark()
        uT = A.alloc([16, TT], BF16)
        buT = Buf('uT1')
        A.mark()
        tmp = [A.alloc([16, 512], F32) for _ in range(2)]
        bt = [Buf(), Buf()]
        for ti, (t0, tl) in enumerate(TILES):
            self.dma_sp(tmp[ti % 2][:, :, 0:tl], self.u1T[:, :, t0:t0 + tl].rearrange('c p t -> p c t'), [], [bt[ti % 2]])
            self.dve(lambda e, ti=ti, t0=t0, tl=tl: e.tensor_copy(out=uT[:, :, t0:t0 + tl], in_=tmp[ti % 2][:, :, 0:tl]), [bt[ti % 2]], [buT])
        A.release()
        self.S.barrier()
        self.phase_cd_proj(uT, buT)
        A.release()
        if stop_after == 'cdproj':
            return self.finish()
        self.phase_rwkv()
        if stop_after == 'rwkv':
            return self.finish()
        self.phase_ret()
        return self.finish()

    def build_l1(self):
        A = self.A
        LT = TILES[:4]
        A.mark()
        uT = A.alloc([16, TT], BF16)
        buT = Buf('uT1')
        self.phase_norm(1, 0, uT, buT)
        self.phase_cd_proj(uT, buT)
        A.release()
        self.phase_rwkv()
        self.phase_ret()
        A.mark()
        yin = A.alloc([24, NLAT], BF16)
        byin = Buf('yin1')
        self.dma_sp(yin, self.scr['yin'].rearrange('c p t -> p c t'), self.byin_s, [byin])
        self.phase_resid_proj(1, 2, yin, byin, 24, self.W['cd_w_out'][0], tiles=LT)
        A.release()
        A.mark()
        fT = A.alloc([16, TT], BF16)
        bfT = Buf('fT1')
        self.phase_norm(1, 1, fT, bfT, tiles=LT)
        self.phase_ffn_up(1, fT, bfT, tiles=LT)
        A.release()
        self.phase_ffn_down(1, tiles=LT)
        self.phase_final()
        return self.finish()

    def finish(self):
        self.S.barrier()
        self.S.emit(self.nc)
        return self.nc


def make_in_maps(inp):
    pk = pack_params(inp).array()
    cst = const_pack().array()
    rope = rope_tables()
    maps = []
    for b in range(8):
        cc = np.stack([fm(inp['c'][b]), fm(inp['c_ctx'])], axis=2).reshape(128, 32)
        m = {'x': np.ascontiguousarray(inp['x'][b]), 'ctx': np.ascontiguousarray(inp['ctx'][b]), 'cc': np.ascontiguousarray(cc),
             'pk': pk, 'cst': cst, 'rope': rope}
        for k in WEIGHT_NAMES:
            m[k] = np.ascontiguousarray(np.asarray(inp[k], np.float32))
        maps.append(m)
    return maps


def kernel(**inputs):
    inp = {k: np.asarray(v) for k, v in inputs.items()}
    mk = MK()
    nc = mk.build()
    res = run_bass_kernel_spmd(nc, make_in_maps(inp), core_ids=list(range(8)))
    return np.stack([r['out'] for r in res.results], axis=0).astype(np.float32)
```

```python
import numpy as np
import concourse.bass as bass
import concourse.mybir as mybir
from concourse.bass_utils import run_bass_kernel_spmd
from contextlib import ExitStack

F32 = mybir.dt.float32
BF16 = mybir.dt.bfloat16
AF = mybir.ActivationFunctionType
ALU = mybir.AluOpType
AX = mybir.AxisListType

D = 2048
NLAT = 2048
NCTX = 256
TT = NLAT + NCTX
FFN = 5632
TILES = [(0, 512), (512, 512), (1024, 512), (1536, 512), (2048, 256)]
COMPUTE = ('pe', 'act', 'dve', 'pool')
STREAMS = ('pe', 'act', 'dve', 'pool', 'sp')
NDSEM = 8
NRING = 8
RING_CHUNK = 512


class Buf:
    __slots__ = ('name', 'w', 'rs')

    def __init__(self, name=''):
        self.name = name
        self.w = None
        self.rs = {}


class Op:
    __slots__ = ('stream', 'fn', 'pos', 'dma', 'deps', 'waits', 'inc', 'val', 'dsem', 'gid', 'ring')

    def __init__(self, stream, fn, dma):
        self.stream = stream
        self.fn = fn
        self.dma = dma
        self.deps = {}
        self.waits = []
        self.inc = False
        self.val = 0
        self.dsem = None


class Sched:
    def __init__(self, same_engine_sync=True):
        self.streams = {s: [] for s in STREAMS}
        self.all = []
        self.dma_rr = {s: 0 for s in STREAMS}
        self.dma_last = {}
        self.last_c = {}
        self.same = same_engine_sync

    def _adddep(self, o, d):
        if d.dma:
            key = ('d',) + d.dsem + (d.gid,)
        else:
            if d.stream == o.stream and (not self.same or o.stream == 'pe'):
                return
            key = ('e', d.stream)
        cur = o.deps.get(key)
        if cur is None or d.pos > cur.pos:
            o.deps[key] = d

    @staticmethod
    def _flat(xs):
        out = []
        for x in xs:
            if isinstance(x, (list, tuple)):
                out.extend(Sched._flat(x))
            else:
                out.append(x)
        return out

    def op(self, stream, fn, reads=(), writes=(), dma=False):
        reads = self._flat(reads)
        writes = self._flat(writes)
        o = Op(stream, fn, dma)
        o.pos = len(self.streams[stream])
        o.gid = len(self.all)
        for b in reads:
            if b.w is not None:
                self._adddep(o, b.w)
        for b in writes:
            if b.w is not None:
                self._adddep(o, b.w)
            for d in b.rs.values():
                self._adddep(o, d)
        if dma:
            k = self.dma_rr[stream] % NDSEM
            self.dma_rr[stream] += 1
            o.dsem = (stream, k)
            prev = self.dma_last.get(o.dsem)
            if prev is not None:
                self._adddep(o, prev)
            self.dma_last[o.dsem] = o
        for b in writes:
            b.w = o
            b.rs = {}
        rk = ('d', o.gid) if dma else stream
        for b in reads:
            if b.w is not o:
                b.rs[rk] = o
        if not dma:
            self.last_c[stream] = o
        self.streams[stream].append(o)
        self.all.append(o)
        return o

    def barrier(self):
        tails = list(self.last_c.values())
        dl = list(self.dma_last.values())
        for s in STREAMS:
            o = Op(s, None, False)
            o.pos = len(self.streams[s])
            o.gid = len(self.all)
            for d in tails + dl:
                if (not d.dma) and d.stream == s:
                    continue
                if d.dma:
                    key = ('d',) + d.dsem + (d.gid,)
                else:
                    key = ('e', d.stream)
                cur = o.deps.get(key)
                if cur is None or d.pos > cur.pos:
                    o.deps[key] = d
            self.streams[s].append(o)
            self.all.append(o)

    def finalize(self):
        seen = {s: {} for s in STREAMS}
        for o in self.all:
            sn = seen[o.stream]
            for key, d in o.deps.items():
                if d.dma:
                    k2 = ('d',) + d.dsem
                    if sn.get(k2, -1) >= d.gid:
                        continue
                    sn[k2] = d.gid
                else:
                    if sn.get(key, -1) >= d.pos:
                        continue
                    sn[key] = d.pos
                d.inc = True
                o.waits.append(d)
        for s in STREAMS:
            c = 0
            per = [0] * NRING
            for o in self.streams[s]:
                if o.dma or o.fn is None:
                    continue
                if o.inc:
                    r = (c // RING_CHUNK) % NRING
                    c += 1
                    per[r] += 1
                    o.ring = r
                    o.val = per[r]
        cnt = {}
        for o in self.all:
            if o.dma:
                cnt[o.dsem] = cnt.get(o.dsem, 0) + 16
                o.val = cnt[o.dsem]

    def emit(self, nc):
        self.finalize()
        with ExitStack() as es:
            esem = {(s, r): es.enter_context(nc.semaphore('e_%s%d' % (s, r))) for s in COMPUTE for r in range(NRING)}
            dsem = {}
            for s in STREAMS:
                for k in range(min(NDSEM, self.dma_rr[s])):
                    dsem[(s, k)] = es.enter_context(nc.semaphore('d_%s%d' % (s, k)))
            block = es.enter_context(nc.Block())

            def replay(stream, eng):
                for o in self.streams[stream]:
                    for d in o.waits:
                        if d.dma:
                            eng.wait_ge(dsem[d.dsem], d.val)
                        else:
                            eng.wait_ge(esem[(d.stream, d.ring)], d.val)
                    if o.fn is None:
                        continue
                    ins = o.fn(eng)
                    if o.dma:
                        ins.then_inc(dsem[o.dsem], 16)
                    elif o.inc:
                        ins.then_inc(esem[(o.stream, o.ring)], 1)

            @block.sync
            def _(e):
                replay('sp', e)

            @block.tensor
            def _(e):
                replay('pe', e)

            @block.scalar
            def _(e):
                replay('act', e)

            @block.vector
            def _(e):
                replay('dve', e)

            @block.gpsimd
            def _(e):
                replay('pool', e)


class Arena:
    def __init__(self, nc, nbytes=212480):
        self.n32 = nbytes // 4
        self.t = nc.alloc_sbuf_tensor('arena', [128, self.n32], F32)
        self.off = 0
        self.marks = []
        self.peak = 0

    def alloc(self, free_shape, dtype=F32):
        n = 1
        for s in free_shape:
            n *= s
        nb = n * (2 if dtype == BF16 else 4)
        nb = (nb + 63) // 64 * 64
        o32 = self.off // 4
        assert self.off + nb <= self.n32 * 4, ('SBUF arena overflow', self.off, nb)
        self.off += nb
        self.peak = max(self.peak, self.off)
        ap = self.t[:, o32:o32 + nb // 4]
        if dtype == BF16:
            ap = ap.bitcast(BF16)[:, 0:n]
        else:
            ap = ap[:, 0:n]
        if len(free_shape) == 2:
            ap = ap.rearrange('p (a b) -> p a b', b=free_shape[1])
        elif len(free_shape) == 3:
            ap = ap.rearrange('p (a b c) -> p a b c', b=free_shape[1], c=free_shape[2])
        return ap

    def mark(self):
        self.marks.append(self.off)

    def release(self):
        self.off = self.marks.pop()


def fm(v):
    v = np.asarray(v, np.float32).reshape(-1, 128)
    return np.ascontiguousarray(v.T)


class Pack:
    def __init__(self):
        self.cols = {}
        self.parts = []
        self.n = 0

    def add(self, name, arr):
        arr = np.asarray(arr, np.float32)
        assert arr.shape[0] == 128
        arr = arr.reshape(128, -1)
        self.cols[name] = (self.n, arr.shape[1])
        self.parts.append(arr)
        self.n += arr.shape[1]

    def array(self):
        return np.ascontiguousarray(np.concatenate(self.parts, axis=1))


def pack_layout():
    L = []
    for l in range(2):
        L += [('gmix%d' % l, 16), ('gffn%d' % l, 16), ('modb%d' % l, 96), ('fcb%d' % l, 44), ('fcw%d' % l, 132)]
    L += [('qg', 1), ('qgs', 1), ('kg', 1), ('kgs', 1), ('acw', 248), ('acb', 8), ('ang', 8), ('anb', 8)]
    L += [('mu', 54), ('w0', 16), ('a0', 16), ('kk', 8), ('ka', 8), ('rk', 8), ('lng', 8), ('lnb', 8),
          ('gng', 16), ('gnb', 16), ('dlog', 16), ('gfin', 16)]
    return L


def pack_params(inp):
    P = Pack()
    for l in range(2):
        P.add('gmix%d' % l, fm(inp['norm_mix_g'][l]))
        P.add('gffn%d' % l, fm(inp['norm_ffn_g'][l]))
        P.add('modb%d' % l, fm(inp['mod_b'][l]))
        P.add('fcb%d' % l, fm(inp['ffn_conv_b'][l]))
        w = np.asarray(inp['ffn_conv_w'][l], np.float32)
        P.add('fcw%d' % l, np.stack([fm(w[j]) for j in range(3)], axis=2))
    sw = np.arange(128) ^ 32
    qg = np.asarray(inp['ab_q_norm'][0], np.float32)
    kg = np.asarray(inp['ab_k_norm'][0], np.float32)
    P.add('qg', qg.reshape(128, 1))
    P.add('qgs', qg[sw].reshape(128, 1))
    P.add('kg', kg.reshape(128, 1))
    P.add('kgs', kg[sw].reshape(128, 1))
    w = np.asarray(inp['ab_conv_w'][0], np.float32)
    P.add('acw', np.stack([fm(w[j]) for j in range(31)], axis=2))
    P.add('acb', fm(inp['ab_conv_b'][0]))
    P.add('ang', fm(inp['ab_conv_norm_g'][0]))
    P.add('anb', fm(inp['ab_conv_norm_b'][0]))
    mu = np.zeros((2, 27 * 128), np.float32)
    mu[:, :3360] = np.asarray(inp['cd_shift_mu'][0], np.float32)
    P.add('mu', np.stack([fm(mu[j]) for j in range(2)], axis=2))
    P.add('w0', np.stack([fm(inp['rwkv_w0'][0][j]) for j in range(2)], axis=1))
    P.add('a0', np.stack([fm(inp['rwkv_a0'][0][j]) for j in range(2)], axis=1))
    P.add('kk', fm(inp['rwkv_k_k'][0]))
    P.add('ka', fm(inp['rwkv_k_a'][0]))
    P.add('rk', fm(np.asarray(inp['rwkv_r_k'][0]).reshape(-1)))
    P.add('lng', fm(inp['rwkv_ln_g'][0]))
    P.add('lnb', fm(inp['rwkv_ln_b'][0]))
    P.add('gng', fm(inp['ret_gn_g'][0]))
    P.add('gnb', fm(inp['ret_gn_b'][0]))
    P.add('dlog', np.broadcast_to(np.asarray(inp['ret_decay_logit'][0], np.float32).reshape(1, 16), (128, 16)))
    P.add('gfin', fm(inp['final_norm_g']))
    assert [(k, v[1]) for k, v in P.cols.items()] == pack_layout()
    return P


CST_LAYOUT = [('ident', 128), ('ones', 128), ('bones', 128), ('eps6', 1), ('eps12', 1), ('epsgn', 1), ('one', 1),
              ('U', 128), ('L', 128), ('IU', 64), ('IL', 64), ('diffT', 128), ('maskF', 128), ('maskB', 128), ('irow', 128), ('irowb', 128),
              ('jcol', 1), ('jcolb', 1), ('c128', 1)]


def const_pack():
    P = Pack()
    P.add('ident', np.eye(128, dtype=np.float32))
    P.add('ones', np.ones((128, 128), np.float32))
    bo = np.zeros((128, 128), np.float32)
    bo[:64, :64] = 1
    bo[64:, 64:] = 1
    P.add('bones', bo)
    P.add('eps6', np.full((128, 1), 1e-6, np.float32))
    P.add('eps12', np.full((128, 1), 1e-12, np.float32))
    P.add('epsgn', np.full((128, 1), 64e-5, np.float32))
    P.add('one', np.ones((128, 1), np.float32))
    p = np.arange(128)
    hp, sp = p // 64, p % 64
    U = ((hp[:, None] == hp[None, :]) & (sp[None, :] > sp[:, None])).astype(np.float32)
    P.add('U', U)
    P.add('L', np.ascontiguousarray(U.T))
    t64 = np.arange(64)
    P.add('IU', (sp[:, None] <= t64[None, :]).astype(np.float32))
    P.add('IL', (sp[:, None] >= t64[None, :]).astype(np.float32))
    diffT = (p[None, :] - p[:, None]).astype(np.float32)
    P.add('diffT', diffT)
    P.add('maskF', (diffT >= 0).astype(np.float32))
    P.add('maskB', (diffT <= 0).astype(np.float32))
    P.add('irow', np.broadcast_to((p + 1.0).astype(np.float32)[None, :], (128, 128)))
    P.add('irowb', np.broadcast_to((128.0 - p).astype(np.float32)[None, :], (128, 128)))
    P.add('jcol', (127.0 - p).astype(np.float32).reshape(128, 1))
    P.add('jcolb', p.astype(np.float32).reshape(128, 1))
    P.add('c128', np.full((128, 1), 128.0, np.float32))
    assert [(k, v[1]) for k, v in P.cols.items()] == CST_LAYOUT
    return P


def rope_tables():
    rows = NLAT // 64
    row = np.repeat(np.arange(rows, dtype=np.float32), 64)
    col = np.tile(np.arange(64, dtype=np.float32), rows)
    inv = (10000.0 ** (-np.arange(0, 64, 2, dtype=np.float32) / 64)).astype(np.float32)
    ang = np.concatenate([row[:, None] * inv, col[:, None] * inv], axis=-1).astype(np.float32)
    cos, sin = np.cos(ang).astype(np.float32), np.sin(ang).astype(np.float32)
    C = np.zeros((128, NLAT), np.float32)
    S = np.zeros((128, NLAT), np.float32)
    for d in range(128):
        axis, r = d // 64, d % 64
        f = axis * 32 + (r % 32)
        C[d] = cos[:, f]
        S[d] = -sin[:, f] if r < 32 else sin[:, f]
    return np.stack([C, S])


WEIGHT_NAMES = ['mod_w', 'ffn_w_up', 'ffn_w_down', 'ab_w_in', 'ab_w_out', 'cd_w_in', 'rwkv_w2', 'rwkv_a2', 'rwkv_g2',
                'cd_w_out']
WEIGHT_SHAPES = {'mod_w': [2, D, 6 * D], 'ffn_w_up': [2, D, 2 * FFN], 'ffn_w_down': [2, FFN, D], 'ab_w_in': [1, D, 3584],
                 'ab_w_out': [1, D, D], 'cd_w_in': [1, D, 9504], 'rwkv_w2': [1, 2, 64, 1024], 'rwkv_a2': [1, 2, 64, 1024],
                 'rwkv_g2': [1, 160, 1024], 'cd_w_out': [1, 3072, D]}


KDEC = 0.6065306597126334


class MK1:
    def qbank(self):
        i = self.q_rr % 8
        self.q_rr += 1
        return self.banks[i][:, 0:128], self.qbufs[i]

    def phase_cd_proj(self, uT, buT):
        A = self.A
        w_in = self.W['cd_w_in'][0]
        zT = self.scratch('zT', [27, 128, 2306], F32)
        self.bz = [Buf('z%d' % i) for i in range(27)]
        rqk = self.scratch('rqk', [16, 128, TT], BF16)
        self.brqk = [Buf('rqk%d' % i) for i in range(16)]
        rvt = self.scratch('rvt', [18, 128, 2048], BF16)
        self.brv = Buf('rvt')
        rgs = self.scratch('rgs', [16, 128, NLAT], BF16)
        self.brg = [Buf('rg%d' % i) for i in range(16)]
        A.mark()
        self.wpool(2, 16 * 512)
        zpad = [A.alloc([2308], F32) for _ in range(2)]
        bzp = [Buf(), Buf()]
        zo = [A.alloc([2306], F32) for _ in range(2)]
        bzo = [Buf(), Buf()]
        c0 = A.alloc([27], F32)
        bc0 = Buf()
        mu = self.P('mu').rearrange('p (c j) -> p c j', j=2)
        for i in range(2):
            self.pool(lambda e, i=i: e.memset(zpad[i], 0.0), [], [bzp[i]])
        self.dve(lambda e: e.tensor_tensor(out=c0, in0=mu[:, :, 0], in1=mu[:, :, 1], op=ALU.add), [self.bpk], [bc0])
        self.dve(lambda e: e.tensor_scalar(out=c0, in0=c0, scalar1=-1.0, scalar2=1.0, op0=ALU.mult, op1=ALU.add), [bc0], [bc0])
        for g in range(7):
            ncol = 512 if g < 6 else 288
            wt, wb = self.wload(w_in[:, g * 512:g * 512 + ncol], 16, ncol)
            for c4 in range(4 if g < 6 else 3):
                c = g * 4 + c4
                m = 128 if c < 26 else 32
                zp, bzp_ = zpad[c % 2], bzp[c % 2]
                zo_, bzo_ = zo[c % 2], bzo[c % 2]
                for ti, (t0, tl) in enumerate(TILES):
                    off = 1 + t0 if ti < 4 else 2051
                    bk, bb = self.bank()
                    self.mm(bk[0:m, 0:tl], [(wt[:, kc, c4 * 128:c4 * 128 + m], uT[:, kc, t0:t0 + tl]) for kc in range(16)], [wb, buT], bb)
                    self.act(lambda e, bk=bk, zp=zp, m=m, off=off, tl=tl: e.activation(out=zp[0:m, off:off + tl], in_=bk[0:m, 0:tl], func=AF.Copy), [bb], [bzp_])
                self.dve(lambda e, zo_=zo_, zp=zp, m=m, c=c: e.tensor_scalar(out=zo_[0:m, :], in0=zp[0:m, 1:2307], scalar1=c0[0:m, c:c + 1],
                                                                           scalar2=None, op0=ALU.mult), [bzp_, bc0], [bzo_])
                self.dve(lambda e, zo_=zo_, zp=zp, m=m, c=c: e.scalar_tensor_tensor(out=zo_[0:m, :], in0=zp[0:m, 0:2306], scalar=mu[0:m, c, 0:1],
                                                                                  in1=zo_[0:m, :], op0=ALU.mult, op1=ALU.add), [bzp_, self.bpk, bzo_], [bzo_])
                self.dve(lambda e, zo_=zo_, zp=zp, m=m, c=c: e.scalar_tensor_tensor(out=zo_[0:m, :], in0=zp[0:m, 2:2308], scalar=mu[0:m, c, 1:2],
                                                                                  in1=zo_[0:m, :], op0=ALU.mult, op1=ALU.add), [bzp_, self.bpk, bzo_], [bzo_])
                self.dma_sp(zT[c][0:m], zo_[0:m, :], [bzo_], [self.bz[c]])
        A.release()
        self.S.barrier()
        A.mark()
        rope = A.alloc([2, NLAT], F32)
        brope = Buf()
        self.dma_sp(rope, self.roped.rearrange('a p t -> p a t'), [], [brope])
        self.wpool(2, 16 * 256)
        ws_ap = [A.alloc([16 * 256], BF16) for _ in range(2)]
        bws = [Buf(), Buf()]
        t1 = [A.alloc([512], F32) for _ in range(2)]
        bt1 = [Buf(), Buf()]
        t2 = [A.alloc([512], F32) for _ in range(2)]
        bt2 = [Buf(), Buf()]
        row = [A.alloc([TT], BF16) for _ in range(2)]
        brow = [Buf(), Buf()]
        k = 0
        for g in range(8):
            col0 = 3360 + g * 256
            wt, wb = self.wload(w_in[:, col0:col0 + 256], 16, 256)
            ws = ws_ap[g % 2].rearrange('p (c n) -> p c n', n=256)
            bw_ = bws[g % 2]
            wtv = wt.rearrange('p c (g b e) -> p (c g) b e', b=2, e=32)
            wsv = ws.rearrange('p c (g b e) -> p (c g) b e', b=2, e=32)
            for b in range(2):
                self.pool(lambda e, wsv=wsv, wtv=wtv, b=b: e.tensor_copy(out=wsv[:, :, b, :], in_=wtv[:, :, 1 - b, :]), [wb], [bw_])
            for hh in range(2):
                hd = g * 2 + hh
                rw, brw = row[hd % 2], brow[hd % 2]
                tiles = TILES[:4] if hd < 8 else TILES
                for ti, (t0, tl) in enumerate(tiles):
                    i2 = k % 2
                    k += 1
                    bkq, bbq = self.bank()
                    self.mm(bkq[:, 0:tl], [(wt[:, c, hh * 128:(hh + 1) * 128], uT[:, c, t0:t0 + tl]) for c in range(16)], [wb, buT], bbq)
                    if ti < 4:
                        bkw, bbw = self.bank()
                        self.mm(bkw[:, 0:tl], [(ws[:, c, hh * 128:(hh + 1) * 128], uT[:, c, t0:t0 + tl]) for c in range(16)], [bw_, buT], bbw)
                        self.dve(lambda e, bkq=bkq, i2=i2, t0=t0, tl=tl: e.tensor_tensor(out=t1[i2][:, 0:tl], in0=bkq[:, 0:tl], in1=rope[:, 0, t0:t0 + tl],
                                                                                      op=ALU.mult), [bbq, brope], [bt1[i2]])
                        self.dve(lambda e, bkw=bkw, i2=i2, t0=t0, tl=tl: e.tensor_tensor(out=t2[i2][:, 0:tl], in0=bkw[:, 0:tl], in1=rope[:, 1, t0:t0 + tl],
                                                                                      op=ALU.mult), [bbw, brope], [bt2[i2]])
                        self.pool(lambda e, i2=i2, rw=rw, t0=t0, tl=tl: e.tensor_tensor(out=rw[:, t0:t0 + tl], in0=t1[i2][:, 0:tl], in1=t2[i2][:, 0:tl],
                                                                                     op=ALU.add), [bt1[i2], bt2[i2]], [brw])
                    else:
                        self.act(lambda e, bkq=bkq, rw=rw, t0=t0, tl=tl: e.activation(out=rw[:, t0:t0 + tl], in_=bkq[:, 0:tl], func=AF.Copy), [bbq], [brw])
                self.dma_sp(rqk[hd], rw, [brw], [self.brqk[hd]])
        A.release()
        self.S.barrier()
        A.mark()
        wv = A.alloc([16, 2048], BF16)
        bwv = Buf()
        for n4 in range(4):
            self.dma_cast(wv[:, :, n4 * 512:(n4 + 1) * 512], w_in[:, 5408 + n4 * 512:5408 + (n4 + 1) * 512].rearrange('(c p) n -> p c n', p=128), [], [bwv])
        vt = [A.alloc([2048], BF16) for _ in range(2)]
        bvt = [Buf(), Buf()]
        for tb in range(18):
            v_, bv_ = vt[tb % 2], bvt[tb % 2]
            for n4 in range(4):
                bk, bb = self.bank()
                self.mm(bk, [(uT[:, c, tb * 128:(tb + 1) * 128], wv[:, c, n4 * 512:(n4 + 1) * 512]) for c in range(16)], [bwv, buT], bb)
                if n4 % 2 == 0:
                    self.act(lambda e, bk=bk, v_=v_, n4=n4: e.activation(out=v_[:, n4 * 512:(n4 + 1) * 512], in_=bk, func=AF.Copy), [bb], [bv_])
                else:
                    self.dve(lambda e, bk=bk, v_=v_, n4=n4: e.tensor_copy(out=v_[:, n4 * 512:(n4 + 1) * 512], in_=bk), [bb], [bv_])
            self.dma_sp(rvt[tb], v_, [bv_], [self.brv])
        A.release()
        self.S.barrier()
        A.mark()
        self.wpool(2, 16 * 512)
        row = [A.alloc([NLAT], BF16) for _ in range(2)]
        brow = [Buf(), Buf()]
        for g in range(4):
            wt, wb = self.wload(w_in[:, 7456 + g * 512:7456 + (g + 1) * 512], 16, 512)
            for c4 in range(4):
                c = g * 4 + c4
                rw, brw = row[c % 2], brow[c % 2]
                for ti, (t0, tl) in enumerate(TILES[:4]):
                    bk, bb = self.bank()
                    self.mm(bk, [(wt[:, kc, c4 * 128:(c4 + 1) * 128], uT[:, kc, t0:t0 + tl]) for kc in range(16)], [wb, buT], bb)
                    self.act(lambda e, bk=bk, rw=rw, t0=t0, tl=tl: e.activation(out=rw[:, t0:t0 + tl], in_=bk, func=AF.Silu), [bb], [brw])
                self.dma_sp(rgs[c], rw, [brw], [self.brg[c]])
        A.release()
        self.S.barrier()

    def phase_rwkv(self):
        A = self.A
        A.mark()
        zT = self.scr['zT']
        yin_s = self.scratch('yin', [24, 128, NLAT], BF16)
        self.byin_s = [Buf('yin%d' % i) for i in range(24)]
        self.q_rr = 0
        self.qbufs = [Buf('q%d' % i) for i in range(32)]
        cU, cL, cIU, cIL = self.C('U'), self.C('L'), self.C('IU'), self.C('IL')
        bd3 = self.C('bones', True).rearrange('p (h c) -> p h c', c=64)
        bonesb, bonesf = self.C('bones', True), self.C('bones')
        identb = self.C('ident', True)
        identf = self.C('ident')
        F = lambda: A.alloc([TT], F32)
        H = lambda: A.alloc([TT], BF16)
        dwa, sg25, sg26 = H(), H(), H()
        bsh = Buf('shared')
        A.mark()
        z32 = A.alloc([2306], F32)
        bz32 = Buf()
        for (c, dst, m0, m1, fn) in ((24, dwa, 0, 64, AF.Tanh), (24, dwa, 64, 128, AF.Copy), (25, sg25, 0, 128, AF.Sigmoid), (26, sg26, 0, 32, AF.Sigmoid)):
            if m0 == 0:
                self.dma_sp(z32[0:(32 if c == 26 else 128), :], zT[c][0:(32 if c == 26 else 128)], [self.bz[c]], [bz32])
            self.act(lambda e, dst=dst, m0=m0, m1=m1, fn=fn: e.activation(out=dst[m0:m1, 0:NLAT], in_=z32[m0:m1, 0:NLAT], func=fn), [bz32], [bsh])
            self.act(lambda e, dst=dst, m0=m0, m1=m1, fn=fn: e.activation(out=dst[m0:m1, NLAT:TT], in_=z32[m0:m1, 2050:2306], func=fn), [bz32], [bsh])
        A.release()
        self.S.barrier()
        w2a2 = A.alloc([2, 1024], BF16)
        g2w = A.alloc([2, 1024], BF16)
        self.dma_cast(w2a2[0:64], self.W['rwkv_w2'][0].rearrange('d k n -> k d n'), [], [bsh])
        self.dma_cast(w2a2[64:128], self.W['rwkv_a2'][0].rearrange('d k n -> k d n'), [], [bsh])
        self.dma_cast(g2w[:, 0, :], self.W['rwkv_g2'][0][0:128, :], [], [bsh])
        self.dma_cast(g2w[0:32, 1, :], self.W['rwkv_g2'][0][128:160, :], [], [bsh])
        seg = F()
        self.pool(lambda e: e.memset(seg, 1.0), [], [bsh])
        self.pool(lambda e: e.memset(seg.rearrange('p (n c) -> p n c', c=64)[:, :, 0:1], 0.0), [bsh], [bsh])
        r_, k_, v_, kk_ = F(), F(), F(), F()
        T = [F() for _ in range(6)]
        vb, rkr = H(), H()
        KKg, Bg, Kg, Rg = H(), H(), H(), H()
        wkv = A.alloc([NLAT], F32)
        bonus = A.alloc([NLAT], F32)
        yrow = [A.alloc([NLAT], BF16)] * 2
        byrow = [Buf()] * 2
        bpair, bdir, bwkv, bbon = Buf('pair'), Buf('dir'), Buf('wkv'), Buf('bonus')
        NSET = 3
        def tset():
            d = {}
            for nm in ('KKb', 'Bgb', 'Kgb', 'vTb', 'X', 'XT', 'BT', 'TTa', 'TTb', 'Ya', 'Yb', 'YTa', 'YTb', 'BgTb', 'KgTb', 'vb', 'ub'):
                d[nm] = (A.alloc([128], BF16), Buf(nm))
            for nm in ('MrbT', 'MrkT', 'vst', 'nZ', 'ust'):
                d[nm] = (A.alloc([64], BF16), Buf(nm))
            return d
        sets = [tset() for _ in range(NSET)]
        for d_ in sets:
            for nm in ('KKb', 'Bgb', 'Kgb', 'vTb', 'ub'):
                ap_, bf_ = d_[nm]
                self.pool(lambda e, ap_=ap_: e.memset(ap_, 0.0), [], [bf_])
        Pst = [(A.alloc([64], F32), A.alloc([64], BF16), A.alloc([128], BF16), Buf('P%d' % i)) for i in range(2)]
        ptmp = A.alloc([64], F32)
        bptmp = Buf()
        sm = [A.alloc([512], F32) for _ in range(4)]
        bsm = [Buf() for _ in range(4)]
        w0 = self.P('w0').rearrange('p (d c) -> p d c', c=8)
        a0 = self.P('a0').rearrange('p (d c) -> p d c', c=8)
        kkp, kap, rkp, lng, lnb = self.P('kk'), self.P('ka'), self.P('rk'), self.P('lng'), self.P('lnb')
        LT = TILES[:4]
        un = 0
        for pc in range(getattr(self, 'rw_pairs', 8)):
            for (dst, c) in ((r_, pc), (k_, 8 + pc), (v_, 16 + pc)):
                self.dma_sp(dst[:, 0:NLAT], zT[c][:, 0:NLAT], [self.bz[c]], [bpair])
                self.dma_sp(dst[:, NLAT:TT], zT[c][:, 2050:2306], [self.bz[c]], [bpair])
            self.act(lambda e: e.activation(out=vb, in_=v_, func=AF.Copy), [bpair], [bpair])
            self.dve(lambda e, pc=pc: e.tensor_scalar(out=kk_, in0=k_, scalar1=kkp[:, pc:pc + 1], scalar2=None, op0=ALU.mult), [bpair, self.bpk], [bpair])
            self.act(lambda e: e.activation(out=T[0], in_=kk_, func=AF.Square), [bpair], [bdir])
            for ti, (t0, tl) in enumerate(TILES):
                bk, bb = self.qbank4()
                self.mm(bk[:, 0:tl], [(bonesf, T[0][:, t0:t0 + tl])], [bdir, self.bcst], bb)
                self.act(lambda e, bk=bk, t0=t0, tl=tl: e.activation(out=T[1][:, t0:t0 + tl], in_=bk[:, 0:tl], func=AF.Sqrt, bias=self.C('eps12'), scale=1.0),
                         [bb, self.bcst], [bdir])
            self.dve(lambda e: e.reciprocal(out=T[1], in_=T[1]), [bdir], [bdir])
            self.pool(lambda e: e.tensor_tensor(out=kk_, in0=kk_, in1=T[1], op=ALU.mult), [bdir, bpair], [bpair])
            for d in range(getattr(self, 'rw_dirs', 2)):
                sig, a_, cs, alt, gi, gneg = T[0], T[1], T[2], T[3], T[4], T[5]
                for ti, (t0, tl) in enumerate(TILES):
                    bk, bb = self.qbank4()
                    self.mm(bk[:, 0:tl], [(w2a2[0:64, d, pc * 128:(pc + 1) * 128], dwa[0:64, t0:t0 + tl])], [bsh], bb)
                    self.act(lambda e, bk=bk, t0=t0, tl=tl, d=d, pc=pc: e.activation(out=sig[:, t0:t0 + tl], in_=bk[:, 0:tl], func=AF.Sigmoid,
                                                                               bias=w0[:, d, pc:pc + 1], scale=1.0), [bb, self.bpk], [bdir])
                    bk2, bb2 = self.qbank4()
                    self.mm(bk2[:, 0:tl], [(w2a2[64:128, d, pc * 128:(pc + 1) * 128], dwa[64:128, t0:t0 + tl])], [bsh], bb2)
                    self.act(lambda e, bk2=bk2, t0=t0, tl=tl, d=d, pc=pc: e.activation(out=a_[:, t0:t0 + tl], in_=bk2[:, 0:tl], func=AF.Sigmoid,
                                                                                 bias=a0[:, d, pc:pc + 1], scale=1.0), [bb2, self.bpk], [bdir])
                self.dve(lambda e, cs=cs, sig=sig: e.tensor_tensor_scan(out=cs, data0=seg, data1=sig, initial=0.0, op0=ALU.mult, op1=ALU.add), [bdir, bsh], [bdir])
                if d == 1:
                    cs3 = cs.rearrange('p (n c) -> p n c', c=64)
                    self.dve(lambda e, alt=alt, sig=sig, cs=cs: e.tensor_tensor(out=alt, in0=sig, in1=cs, op=ALU.subtract), [bdir], [bdir])
                    self.dve(lambda e, cs3=cs3, alt=alt: e.tensor_tensor(out=alt.rearrange('p (n c) -> p n c', c=64), in0=alt.rearrange('p (n c) -> p n c', c=64),
                                                               in1=cs3[:, :, 63:64].broadcast_to([128, 36, 64]), op=ALU.add), [bdir], [bdir])
                    cs, alt = alt, cs
                self.act(lambda e, cs=cs: e.activation(out=gi, in_=cs, func=AF.Exp, scale=-KDEC), [bdir], [bdir])
                self.act(lambda e, cs=cs: e.activation(out=gneg, in_=cs, func=AF.Exp, scale=KDEC), [bdir], [bdir])
                self.dve(lambda e, cs=cs: e.tensor_tensor(out=sig, in0=cs, in1=sig, op=ALU.subtract), [bdir], [bdir])
                self.act(lambda e: e.activation(out=sig, in_=sig, func=AF.Exp, scale=-KDEC), [bdir], [bdir])
                self.pool(lambda e: e.tensor_tensor(out=KKg, in0=kk_, in1=sig, op=ALU.mult), [bdir, bpair], [bdir])
                self.pool(lambda e, alt=alt: e.tensor_tensor(out=alt, in0=kk_, in1=a_, op=ALU.mult), [bdir, bpair], [bdir])
                self.pool(lambda e, alt=alt: e.tensor_tensor(out=Bg, in0=alt, in1=gneg, op=ALU.mult), [bdir], [bdir])
                self.dve(lambda e, pc=pc: e.tensor_scalar(out=a_, in0=a_, scalar1=-1.0, scalar2=kap[:, pc:pc + 1], op0=ALU.add, op1=ALU.mult),
                         [bdir, self.bpk], [bdir])
                self.dve(lambda e: e.scalar_tensor_tensor(out=a_, in0=a_, scalar=1.0, in1=k_, op0=ALU.add, op1=ALU.mult), [bdir, bpair], [bdir])
                self.pool(lambda e: e.tensor_tensor(out=Kg, in0=a_, in1=gneg, op=ALU.mult), [bdir], [bdir])
                self.pool(lambda e: e.tensor_tensor(out=Rg, in0=r_, in1=gi, op=ALU.mult), [bdir, bpair], [bdir])
                self.dve(lambda e, pc=pc: e.scalar_tensor_tensor(out=rkr, in0=r_, scalar=rkp[:, pc:pc + 1], in1=a_, op0=ALU.mult, op1=ALU.mult),
                         [bdir, bpair, self.bpk], [bdir])
                for ti, (t0, tl) in enumerate(LT):
                    bk, bb = self.qbank4()
                    self.mm(bk[:, 0:tl], [(bonesb, rkr[:, t0:t0 + tl])], [bdir, self.bcst], bb)
                    if d == 0:
                        self.dve(lambda e, bk=bk, t0=t0, tl=tl: e.tensor_tensor(out=bonus[:, t0:t0 + tl], in0=bk[:, 0:tl], in1=v_[:, t0:t0 + tl], op=ALU.mult),
                                 [bb, bpair], [bbon])
                    else:
                        s_, bs_ = sm[ti % 4], bsm[ti % 4]
                        self.dve(lambda e, bk=bk, s_=s_, t0=t0, tl=tl: e.tensor_tensor(out=s_[:, 0:tl], in0=bk[:, 0:tl], in1=v_[:, t0:t0 + tl], op=ALU.mult),
                                 [bb, bpair], [bs_])
                        self.pool(lambda e, s_=s_, t0=t0, tl=tl: e.tensor_tensor(out=bonus[:, t0:t0 + tl], in0=bonus[:, t0:t0 + tl], in1=s_[:, 0:tl], op=ALU.add),
                                  [bs_, bbon], [bbon])
                Pf, Pb, Pbd, bP = Pst[d]
                self.pool(lambda e, Pf=Pf: e.memset(Pf, 0.0), [], [bP])
                self.pool(lambda e, Pb=Pb: e.memset(Pb, 0.0), [bP], [bP])
                self.pool(lambda e, Pbd=Pbd: e.memset(Pbd, 0.0), [bP], [bP])
                order = [32, 33, 34, 35] + list(range(32)) if d == 0 else [35, 34, 33, 32] + list(range(31, -1, -1))
                mS, mSn, mI = (cU, cL, cIU) if d == 0 else (cL, cU, cIL)
                order = order[:getattr(self, 'rw_chunks', 36)]
                self.rw_level = getattr(self, 'rw_level', 9)
                for n in order:
                    ts_ = sets[un % NSET]
                    un += 1
                    cs_ = slice(n * 64, (n + 1) * 64)
                    lat = n < 32
                    gcol = gi[:, n * 64 + 63:n * 64 + 64] if d == 0 else gi[:, n * 64:n * 64 + 1]
                    def bdexp(name, src):
                        ap, bf = ts_[name]
                        for h2 in range(2):
                            self.pool(lambda e, ap=ap, src=src, h2=h2: e.tensor_copy(out=ap[h2 * 64:(h2 + 1) * 64, h2 * 64:(h2 + 1) * 64],
                                                                                   in_=src[h2 * 64:(h2 + 1) * 64, :]), [bdir, bpair], [bf])
                        return ap, bf
                    KKb, bKKb = bdexp('KKb', KKg[:, cs_])
                    Bgb, bBgb = bdexp('Bgb', Bg[:, cs_])
                    Kgb, bKgb = bdexp('Kgb', Kg[:, cs_])
                    vTb, bvTb = bdexp('vTb', vb[:, cs_])
                    if self.rw_level <= 1:
                        continue
                    def mmq(pairs, reads):
                        q, bq_ = self.qbank()
                        self.mm(q, pairs, reads, bq_)
                        return q, bq_
                    def evac_mask(name, q, bq_, mask, neg):
                        ap, bf = ts_[name]
                        if neg:
                            self.dve(lambda e, ap=ap, q=q, mask=mask: e.scalar_tensor_tensor(out=ap, in0=q, scalar=-1.0, in1=mask, op0=ALU.mult, op1=ALU.mult),
                                     [bq_, self.bcst], [bf])
                        else:
                            self.dve(lambda e, ap=ap, q=q, mask=mask: e.tensor_tensor(out=ap, in0=q, in1=mask, op=ALU.mult), [bq_, self.bcst], [bf])
                        return ap, bf
                    q, bq_ = mmq([(Bgb, KKb)], [bBgb, bKKb])
                    XT, bXT = evac_mask('XT', q, bq_, mS, True)
                    TTc, bTT = ts_['TTa']
                    self.dve(lambda e, TTc=TTc, XT=XT: e.tensor_tensor(out=TTc, in0=XT, in1=identb, op=ALU.add), [bXT, self.bcst], [bTT])
                    q, bq_ = mmq([(KKb, Bgb)], [bBgb, bKKb])
                    X, bX = evac_mask('X', q, bq_, mSn, True)
                    q, bq_ = mmq([(Kgb, KKb)], [bKgb, bKKb])
                    BT, bBT = evac_mask('BT', q, bq_, mS, False)
                    q, bq_ = self.qbank()
                    self.mm(q[:, 0:64], [(Bgb, Rg[:, cs_])], [bBgb, bdir], bq_)
                    ap, bMrb = ts_['MrbT']
                    self.dve(lambda e, ap=ap, q=q, mI=mI: e.tensor_tensor(out=ap, in0=q[:, 0:64], in1=mI, op=ALU.mult), [bq_, self.bcst], [bMrb])
                    MrbT = ap
                    q, bq_ = self.qbank()
                    self.mm(q[:, 0:64], [(Kgb, Rg[:, cs_])], [bKgb, bdir], bq_)
                    ap, bMrk = ts_['MrkT']
                    self.dve(lambda e, ap=ap, q=q, mI=mI: e.tensor_tensor(out=ap, in0=q[:, 0:64], in1=mI, op=ALU.mult), [bq_, self.bcst], [bMrk])
                    MrkT = ap
                    if self.rw_level <= 2:
                        continue
                    def tr(name, src, bsrc):
                        q, bq_ = self.qbank()
                        qb = q.bitcast(BF16)[:, 0:128]
                        self.pe(lambda e, qb=qb, src=src: e.transpose(out=qb, in_=src, identity=identb), [bsrc, self.bcst], [bq_])
                        ap, bf = ts_[name]
                        self.act(lambda e, ap=ap, qb=qb: e.activation(out=ap, in_=qb, func=AF.Copy), [bq_], [bf])
                        return ap, bf, qb, bq_
                    BgTb, bBgT, _, _ = tr('BgTb', Bgb, bBgb)
                    KgTb, bKgT, _, _ = tr('KgTb', Kgb, bKgb)
                    vbd, bvbd, _, _ = tr('vb', vTb, bvTb)
                    vst, bvst = ts_['vst']
                    self.pool(lambda e, vst=vst, vbd=vbd: e.tensor_tensor(out=vst, in0=vbd[:, 0:64], in1=vbd[:, 64:128], op=ALU.add), [bvbd], [bvst])
                    if self.rw_level <= 3:
                        continue
                    Y, bY, YT, bYT = X, bX, XT, bXT
                    for lev in range(5):
                        nm = 'a' if lev % 2 == 0 else 'b'
                        q, bq_ = mmq([(YT, Y)], [bYT, bY])
                        Y2, bY2 = ts_['Y' + nm]
                        self.act(lambda e, Y2=Y2, q=q: e.activation(out=Y2, in_=q, func=AF.Copy), [bq_], [bY2])
                        if lev < 4:
                            q, bq_ = mmq([(Y, YT)], [bYT, bY])
                            YT2, bYT2 = ts_['YT' + nm]
                            self.act(lambda e, YT2=YT2, q=q: e.activation(out=YT2, in_=q, func=AF.Copy), [bq_], [bYT2])
                        q, bq_ = mmq([(Y2, TTc)], [bY2, bTT])
                        TTn, bTTn = ts_['TTb' if lev % 2 == 0 else 'TTa']
                        self.dve(lambda e, TTn=TTn, q=q, TTc=TTc: e.tensor_tensor(out=TTn, in0=q, in1=TTc, op=ALU.add), [bq_, bTT], [bTTn])
                        TTc, bTT = TTn, bTTn
                        Y, bY = Y2, bY2
                        if lev < 4:
                            YT, bYT = YT2, bYT2
                    if self.rw_level <= 4:
                        continue
                    q, bq_ = self.qbank()
                    self.mm(q[:, 0:64], [(KKb, Pb), (BT, vst)], [bKKb, bP, bBT, bvst], bq_)
                    nZ, bnZ = ts_['nZ']
                    self.act(lambda e, nZ=nZ, q=q: e.activation(out=nZ, in_=q[:, 0:64], func=AF.Copy, scale=-1.0), [bq_], [bnZ])
                    qu, bqu = self.qbank()
                    self.mm(qu[:, 0:64], [(TTc, nZ)], [bTT, bnZ], bqu)
                    ust, bust = ts_['ust']
                    self.act(lambda e, ust=ust, qu=qu: e.activation(out=ust, in_=qu[:, 0:64], func=AF.Copy), [bqu], [bust])
                    if lat:
                        ub, bub = ts_['ub']
                        self.act(lambda e, ub=ub, qu=qu: e.activation(out=ub[0:64, 0:64], in_=qu[0:64, 0:64], func=AF.Copy), [bqu], [bub])
                        self.act(lambda e, ub=ub, qu=qu: e.activation(out=ub[64:128, 64:128], in_=qu[64:128, 0:64], func=AF.Copy), [bqu], [bub])
                        qy, bqy = self.qbank()
                        self.mm(qy[:, 0:64], [(Pbd, Rg[:, cs_]), (ub, MrbT), (vbd, MrkT)], [bP, bdir, bub, bMrb, bvbd, bMrk], bqy)
                        if d == 0:
                            self.act(lambda e, qy=qy, cs_=cs_: e.activation(out=wkv[:, cs_], in_=qy[:, 0:64], func=AF.Copy), [bqy], [bwkv])
                        else:
                            self.dve(lambda e, qy=qy, cs_=cs_: e.tensor_tensor(out=wkv[:, cs_], in0=qy[:, 0:64], in1=wkv[:, cs_], op=ALU.add), [bqy, bwkv], [bwkv])
                    qd, bqd = self.qbank()
                    self.mm(qd[:, 0:64], [(BgTb, ust), (KgTb, vst)], [bBgT, bust, bKgT, bvst], bqd)
                    self.dve(lambda e, qd=qd, Pf=Pf: e.tensor_tensor(out=ptmp, in0=qd[:, 0:64], in1=Pf, op=ALU.add), [bqd, bP], [bptmp])
                    self.dve(lambda e, Pf=Pf, gcol=gcol: e.tensor_scalar(out=Pf, in0=ptmp, scalar1=gcol, scalar2=None, op0=ALU.mult), [bptmp, bdir], [bP])
                    self.act(lambda e, Pf=Pf, Pb=Pb: e.activation(out=Pb, in_=Pf, func=AF.Copy), [bP], [bP])
                    for h2 in range(2):
                        self.pool(lambda e, Pf=Pf, Pbd=Pbd, h2=h2: e.tensor_copy(out=Pbd[h2 * 64:(h2 + 1) * 64, h2 * 64:(h2 + 1) * 64],
                                                                               in_=Pf[h2 * 64:(h2 + 1) * 64, :]), [bP], [bP])
            if not getattr(self, 'rw_gn', True):
                continue
            yr, byr = yrow[pc % 2], byrow[pc % 2]
            for ti, (t0, tl) in enumerate(LT):
                s0, s1, s2, s3 = sm
                b0_, b1_, b2_, b3_ = bsm
                bk1, bb1 = self.qbank4()
                self.mm(bk1[:, 0:tl], [(bonesf, wkv[:, t0:t0 + tl])], [bwkv, self.bcst], bb1)
                self.act(lambda e, t0=t0, tl=tl: e.activation(out=s0[:, 0:tl], in_=wkv[:, t0:t0 + tl], func=AF.Square), [bwkv], [b0_])
                bk2, bb2 = self.qbank4()
                self.mm(bk2[:, 0:tl], [(bonesf, s0[:, 0:tl])], [b0_, self.bcst], bb2)
                self.act(lambda e, bk1=bk1, tl=tl: e.activation(out=s1[:, 0:tl], in_=bk1[:, 0:tl], func=AF.Copy, scale=1.0 / 64), [bb1], [b1_])
                self.dve(lambda e, tl=tl: e.tensor_tensor(out=s2[:, 0:tl], in0=s1[:, 0:tl], in1=s1[:, 0:tl], op=ALU.mult), [b1_], [b2_])
                self.dve(lambda e, bk2=bk2, tl=tl: e.scalar_tensor_tensor(out=s2[:, 0:tl], in0=bk2[:, 0:tl], scalar=1.0 / 64, in1=s2[:, 0:tl],
                                                                          op0=ALU.mult, op1=ALU.subtract), [bb2, b2_], [b2_])
                self.act(lambda e, tl=tl: e.activation(out=s2[:, 0:tl], in_=s2[:, 0:tl], func=AF.Sqrt, bias=self.C('epsgn'), scale=1.0), [b2_, self.bcst], [b2_])
                self.dve(lambda e, tl=tl: e.reciprocal(out=s2[:, 0:tl], in_=s2[:, 0:tl]), [b2_], [b2_])
                self.dve(lambda e, t0=t0, tl=tl: e.tensor_tensor(out=s3[:, 0:tl], in0=wkv[:, t0:t0 + tl], in1=s1[:, 0:tl], op=ALU.subtract), [bwkv, b1_], [b3_])
                self.pool(lambda e, tl=tl: e.tensor_tensor(out=s3[:, 0:tl], in0=s3[:, 0:tl], in1=s2[:, 0:tl], op=ALU.mult), [b3_, b2_], [b3_])
                self.act(lambda e, tl=tl, pc=pc: e.activation(out=s3[:, 0:tl], in_=s3[:, 0:tl], func=AF.Identity, bias=lnb[:, pc:pc + 1], scale=lng[:, pc:pc + 1]),
                         [b3_, self.bpk], [b3_])
                self.pool(lambda e, t0=t0, tl=tl: e.tensor_tensor(out=s3[:, 0:tl], in0=s3[:, 0:tl], in1=bonus[:, t0:t0 + tl], op=ALU.add), [b3_, bbon], [b3_])
                bkg, bbg = self.qbank4()
                self.mm(bkg[:, 0:tl], [(g2w[:, 0, pc * 128:(pc + 1) * 128], sg25[:, t0:t0 + tl]), (g2w[0:32, 1, pc * 128:(pc + 1) * 128], sg26[0:32, t0:t0 + tl])],
                        [bsh], bbg)
                self.dve(lambda e, bkg=bkg, yr=yr, t0=t0, tl=tl: e.tensor_tensor(out=yr[:, t0:t0 + tl], in0=bkg[:, 0:tl], in1=s3[:, 0:tl], op=ALU.mult),
                         [bbg, b3_], [byr])
            self.dma_sp(yin_s[pc], yr, [byr], [self.byin_s[pc]])
        A.release()
        self.S.barrier()

    def qbank4(self):
        i = self.q_rr % 8
        self.q_rr += 1
        return self.banks[i][:, :], self.qbufs[i]

    def phase_ret(self):
        A = self.A
        A.mark()
        rqk, rvt, rgs, yin_s = self.scr['rqk'], self.scr['rvt'], self.scr['rgs'], self.scr['yin']
        identb = self.C('ident', True)
        onesf = self.C('ones')
        diffT, maskF, maskB = self.C('diffT'), self.C('maskF'), self.C('maskB')
        irow, irowb, jcol, jcolb, c128 = self.C('irow'), self.C('irowb'), self.C('jcol'), self.C('jcolb'), self.C('c128')
        gng, gnb = self.P('gng'), self.P('gnb')
        SC = 128.0 ** -0.5
        lg = A.alloc([16], F32)
        nlg = A.alloc([16], F32)
        blg = Buf()
        self.act(lambda e: e.activation(out=lg, in_=self.P('dlog'), func=AF.Exp, scale=-1.0), [self.bpk], [blg])
        self.act(lambda e: e.activation(out=lg, in_=lg, func=AF.Ln, bias=self.C('one'), scale=1.0), [blg, self.bcst], [blg])
        self.dve(lambda e: e.tensor_scalar(out=nlg, in0=lg, scalar1=1.0, scalar2=None, op0=ALU.mult), [blg], [blg])
        self.dve(lambda e: e.tensor_scalar(out=lg, in0=nlg, scalar1=-1.0, scalar2=None, op0=ALU.mult), [blg], [blg])
        qT, kT = A.alloc([TT], BF16), A.alloc([TT], BF16)
        vtok = A.alloc([18, 256], BF16)
        rg2 = A.alloc([2, NLAT], BF16)
        oacc = A.alloc([2, NLAT], F32)
        yrow = [A.alloc([NLAT], BF16) for _ in range(2)]
        byrow = [Buf(), Buf()]
        Dc = A.alloc([128], F32)
        e2 = A.alloc([128], F32)
        qdt = [A.alloc([128], F32) for _ in range(2)]
        kdc = A.alloc([4], F32)
        bhd, btab, boacc = Buf('head'), Buf('tab'), Buf('oacc')
        sm_ = [A.alloc([128], BF16) for _ in range(3)]
        bsm_ = [Buf() for _ in range(3)]
        qd_ = [A.alloc([128], BF16) for _ in range(3)]
        bqd_ = [Buf() for _ in range(3)]
        ktk = [[A.alloc([128], BF16) for _ in range(2)] for _ in range(18)]
        bktk = Buf('ktk')
        R32 = [A.alloc([256], F32) for _ in range(2)]
        Rbf = [A.alloc([256], BF16) for _ in range(2)]
        bR = [Buf('R0'), Buf('R1')]
        st = [A.alloc([512], F32) for _ in range(4)]
        bst = [Buf() for _ in range(4)]
        k3 = 0
        for h in range(8):
            self.dma_sp(qT[:, 0:NLAT], rqk[h][:, 0:NLAT], [self.brqk[h]], [bhd])
            self.dma_sp(kT, rqk[8 + h], [self.brqk[8 + h]], [bhd])
            self.dma_sp(vtok, rvt[:, :, h * 256:(h + 1) * 256].rearrange('b p f -> p b f'), [self.brv], [bhd])
            self.dma_sp(rg2, rgs[2 * h:2 * h + 2].rearrange('c p t -> p c t'), self.brg[2 * h:2 * h + 2], [bhd])
            lf, lb, nlb = lg[:, h:h + 1], lg[:, 8 + h:9 + h], nlg[:, 8 + h:9 + h]
            self.act(lambda e, lf=lf: e.activation(out=Dc, in_=diffT, func=AF.Exp, scale=lf), [blg, self.bcst], [btab])
            self.dve(lambda e: e.tensor_tensor(out=Dc, in0=Dc, in1=maskF, op=ALU.mult), [btab, self.bcst], [btab])
            self.act(lambda e, nlb=nlb: e.activation(out=e2, in_=diffT, func=AF.Exp, scale=nlb), [blg, self.bcst], [btab])
            self.dve(lambda e: e.tensor_tensor(out=e2, in0=e2, in1=maskB, op=ALU.mult), [btab, self.bcst], [btab])
            self.dve(lambda e: e.tensor_tensor(out=Dc, in0=Dc, in1=e2, op=ALU.add), [btab], [btab])
            self.dve(lambda e: e.tensor_scalar(out=Dc, in0=Dc, scalar1=SC, scalar2=None, op0=ALU.mult), [btab], [btab])
            self.act(lambda e, lf=lf: e.activation(out=qdt[0], in_=irow, func=AF.Exp, scale=lf), [blg, self.bcst], [btab])
            self.act(lambda e, lb=lb: e.activation(out=qdt[1], in_=irowb, func=AF.Exp, scale=lb), [blg, self.bcst], [btab])
            self.act(lambda e, lf=lf: e.activation(out=kdc[:, 0:1], in_=jcol, func=AF.Exp, scale=lf), [blg, self.bcst], [btab])
            self.act(lambda e, lb=lb: e.activation(out=kdc[:, 1:2], in_=jcolb, func=AF.Exp, scale=lb), [blg, self.bcst], [btab])
            self.act(lambda e, lf=lf: e.activation(out=kdc[:, 2:3], in_=c128, func=AF.Exp, scale=lf), [blg, self.bcst], [btab])
            self.act(lambda e, lb=lb: e.activation(out=kdc[:, 3:4], in_=c128, func=AF.Exp, scale=lb), [blg, self.bcst], [btab])
            self.dve(lambda e: e.tensor_scalar(out=kdc[:, 0:2], in0=kdc[:, 0:2], scalar1=SC, scalar2=None, op0=ALU.mult), [btab], [btab])
            for tb in range(18):
                q, bq_ = self.qbank()
                qb = q.bitcast(BF16)[:, 0:128]
                self.pe(lambda e, qb=qb, tb=tb: e.transpose(out=qb, in_=kT[:, tb * 128:(tb + 1) * 128], identity=identb), [bhd, self.bcst], [bq_])
                for d in range(2):
                    self.act(lambda e, qb=qb, tb=tb, d=d: e.activation(out=ktk[tb][d], in_=qb, func=AF.Copy, scale=kdc[:, d:d + 1]), [bq_, btab], [bktk])
            for tb in range(16):
                tsl = slice(tb * 128, (tb + 1) * 128)
                q, bq_ = self.qbank()
                self.mm(q, [(kT[:, tsl], qT[:, tsl])], [bhd], bq_)
                s_, bs_ = sm_[k3 % 3], bsm_[k3 % 3]
                k3 += 1
                self.dve(lambda e, s_=s_, q=q: e.tensor_tensor(out=s_, in0=q, in1=Dc, op=ALU.mult), [bq_, btab], [bs_])
                for vc in range(2):
                    q2, bq2 = self.qbank()
                    self.mm(q2, [(vtok[:, tb, vc * 128:(vc + 1) * 128], s_)], [bhd, bs_], bq2)
                    self.act(lambda e, q2=q2, vc=vc, tsl=tsl: e.activation(out=oacc[:, vc, tsl], in_=q2, func=AF.Copy), [bq2], [boacc])
            for d in range(2):
                self.pool(lambda e, d=d: e.memset(R32[d], 0.0), [], [bR[d]])
                self.pool(lambda e, d=d: e.memset(Rbf[d], 0.0), [bR[d]], [bR[d]])
                order = [16, 17] + list(range(16)) if d == 0 else [17, 16] + list(range(15, -1, -1))
                for tb in order:
                    tsl = slice(tb * 128, (tb + 1) * 128)
                    if tb < 16:
                        qd, bqd = qd_[k3 % 3], bqd_[k3 % 3]
                        k3 += 1
                        self.pool(lambda e, qd=qd, tsl=tsl, d=d: e.tensor_tensor(out=qd, in0=qT[:, tsl], in1=qdt[d], op=ALU.mult), [bhd, btab], [bqd])
                        for vc in range(2):
                            q2, bq2 = self.qbank()
                            self.mm(q2, [(Rbf[d][:, vc * 128:(vc + 1) * 128], qd)], [bR[d], bqd], bq2)
                            self.dve(lambda e, q2=q2, vc=vc, tsl=tsl: e.tensor_tensor(out=oacc[:, vc, tsl], in0=q2, in1=oacc[:, vc, tsl], op=ALU.add),
                                     [bq2, boacc], [boacc])
                    qr, bqr = self.qbank4()
                    self.mm(qr[:, 0:256], [(ktk[tb][d], vtok[:, tb, :])], [bktk, bhd], bqr)
                    self.dve(lambda e, qr=qr, d=d: e.scalar_tensor_tensor(out=R32[d], in0=R32[d], scalar=kdc[:, 2 + d:3 + d], in1=qr[:, 0:256],
                                                                          op0=ALU.mult, op1=ALU.add), [bqr, bR[d], btab], [bR[d]])
                    self.act(lambda e, d=d: e.activation(out=Rbf[d], in_=R32[d], func=AF.Copy), [bR[d]], [bR[d]])
            for ti, (t0, tl) in enumerate(TILES[:4]):
                s0, s1, s2, s3 = st
                b0_, b1_, b2_, b3_ = bst
                bk1, bb1 = self.qbank4()
                self.mm(bk1, [(onesf, oacc[:, vc, t0:t0 + tl]) for vc in range(2)], [boacc, self.bcst], bb1)
                bk2, bb2 = self.qbank4()
                for vc in range(2):
                    self.act(lambda e, vc=vc, t0=t0, tl=tl: e.activation(out=(s0 if vc == 0 else s3)[:, 0:tl], in_=oacc[:, vc, t0:t0 + tl], func=AF.Square),
                             [boacc], [b0_ if vc == 0 else b3_])
                self.mm(bk2, [(onesf, s0), (onesf, s3)], [b0_, b3_, self.bcst], bb2)
                self.act(lambda e, bk1=bk1: e.activation(out=s1, in_=bk1, func=AF.Copy, scale=1.0 / 256), [bb1], [b1_])
                self.dve(lambda e: e.tensor_tensor(out=s2, in0=s1, in1=s1, op=ALU.mult), [b1_], [b2_])
                self.dve(lambda e, bk2=bk2: e.scalar_tensor_tensor(out=s2, in0=bk2, scalar=1.0 / 256, in1=s2, op0=ALU.mult, op1=ALU.subtract), [bb2, b2_], [b2_])
                self.act(lambda e: e.activation(out=s2, in_=s2, func=AF.Sqrt, bias=self.C('eps6'), scale=1.0), [b2_, self.bcst], [b2_])
                self.dve(lambda e: e.reciprocal(out=s2, in_=s2), [b2_], [b2_])
                for vc in range(2):
                    c = 2 * h + vc
                    yr, byr = yrow[c % 2], byrow[c % 2]
                    self.dve(lambda e, vc=vc, t0=t0, tl=tl: e.tensor_tensor(out=s0, in0=oacc[:, vc, t0:t0 + tl], in1=s1, op=ALU.subtract), [boacc, b1_, b3_], [b0_])
                    self.pool(lambda e: e.tensor_tensor(out=s0, in0=s0, in1=s2, op=ALU.mult), [b0_, b2_], [b0_])
                    self.act(lambda e, c=c: e.activation(out=s0, in_=s0, func=AF.Identity, bias=gnb[:, c:c + 1], scale=gng[:, c:c + 1]), [b0_, self.bpk], [b0_])
                    self.pool(lambda e, yr=yr, vc=vc, t0=t0, tl=tl: e.tensor_tensor(out=yr[:, t0:t0 + tl], in0=s0, in1=rg2[:, vc, t0:t0 + tl], op=ALU.mult),
                              [b0_, bhd], [byr])
            for vc in range(2):
                c = 2 * h + vc
                self.dma_sp(yin_s[8 + c], yrow[c % 2], [byrow[c % 2]], [self.byin_s[8 + c]])
        A.release()
        self.S.barrier()


class MK(MK1):
    def __init__(self, debug=(), only=None):
        self.debug = set(debug)
        self.only = only
        nc = self.nc = bass.Bass("TRN2", target_bir_lowering=False)
        self.S = Sched()
        self.A = Arena(nc)
        din = lambda name, shape, dt=F32: nc.dram_tensor(name, shape, dt, kind="ExternalInput").ap()
        self.x = din('x', [NLAT, D])
        self.ctx = din('ctx', [NCTX, D])
        self.cc = din('cc', [128, 32])
        npk = sum(w for _, w in pack_layout())
        self.pkd = din('pk', [128, npk])
        ncst = sum(w for _, w in CST_LAYOUT)
        self.cstd = din('cst', [128, ncst])
        self.roped = din('rope', [2, 128, NLAT])
        self.W = {k: din(k, WEIGHT_SHAPES[k] if (only is None or k in only) else [1, 1, 1]) for k in WEIGHT_NAMES}
        if only is not None:
            self.u1T = din('u1T', [16, 128, TT])
        self.out = nc.dram_tensor('out', [NLAT, D], F32, kind="ExternalOutput").ap()
        self.bout = Buf('out')
        self.scr = {}
        self.banks = [nc.alloc_psum_tensor('ps%d' % i, [128, 512], F32) for i in range(8)]
        self.bbufs = [Buf('bank%d' % i) for i in range(8)]
        self.bank_rr = 0
        A = self.A
        self.pk = A.alloc([npk], F32)
        self.cst = A.alloc([ncst], F32)
        self.cstb = A.alloc([ncst], BF16)
        self.modv = A.alloc([2, 96, 2], F32)
        self.bpk, self.bcst, self.bmod = Buf('pk'), Buf('cst'), Buf('mod')
        self.pkc = {}
        o = 0
        for name, w in pack_layout():
            self.pkc[name] = (o, w)
            o += w
        self.cc_ = {}
        o = 0
        for name, w in CST_LAYOUT:
            self.cc_[name] = (o, w)
            o += w
        self.dma_sp(self.pk, self.pkd, [], [self.bpk])
        self.dma_sp(self.cst, self.cstd, [], [self.bcst])
        self.dve(lambda e: e.tensor_copy(out=self.cstb, in_=self.cst), [self.bcst], [self.bcst])

    def scratch(self, name, shape, dt):
        kind = "ExternalOutput" if name in self.debug else "Internal"
        t = self.nc.dram_tensor(name, shape, dt, kind=kind).ap()
        self.scr[name] = t
        return t

    def P(self, name):
        o, w = self.pkc[name]
        return self.pk[:, o:o + w]

    def C(self, name, bf=False):
        o, w = self.cc_[name]
        return (self.cstb if bf else self.cst)[:, o:o + w]

    def pe(self, fn, r, w):
        return self.S.op('pe', fn, r, w)

    def act(self, fn, r, w):
        return self.S.op('act', fn, r, w)

    def dve(self, fn, r, w):
        return self.S.op('dve', fn, r, w)

    def pool(self, fn, r, w):
        return self.S.op('pool', fn, r, w)

    def dma_sp(self, out, in_, r, w):
        return self.S.op('sp', lambda e: e.dma_start(out=out, in_=in_), r, w, dma=True)

    def dma_cast(self, out, in_, r, w):
        return self.S.op('pool', lambda e: e.dma_start(out=out, in_=in_), r, w, dma=True)

    def bank(self):
        i = self.bank_rr % 8
        self.bank_rr += 1
        return self.banks[i][:, :], self.bbufs[i]

    def mm(self, out, pairs, r, wbuf):
        n = len(pairs)
        for i, (l, rr) in enumerate(pairs):
            self.pe(lambda e, l=l, rr=rr, i=i: e.matmul(out, lhsT=l, rhs=rr, start=(i == 0), stop=(i == n - 1)), r, [wbuf])

    def wpool(self, nbuf, nelem):
        self.wb_aps = [self.A.alloc([nelem], BF16) for _ in range(nbuf)]
        self.wb_bufs = [Buf('w%d' % i) for i in range(nbuf)]
        self.wb_rr = 0

    def wload(self, src, kc, n, rows=128):
        i = self.wb_rr % len(self.wb_aps)
        self.wb_rr += 1
        ap = self.wb_aps[i][0:rows, 0:kc * n].rearrange('p (c n) -> p c n', n=n)
        b = self.wb_bufs[i]
        self.dma_cast(ap, src.rearrange('(c p) n -> p c n', p=rows), [], [b])
        return ap, b

    def phase_mod(self):
        A = self.A
        A.mark()
        c32 = A.alloc([32], F32)
        sT = A.alloc([16, 2], BF16)
        bc, bs = Buf(), Buf()
        self.dma_sp(c32, self.cc, [], [bc])
        self.act(lambda e: e.activation(out=sT, in_=c32.rearrange('p (c t) -> p c t', t=2), func=AF.Silu), [bc], [bs])
        self.wpool(3, 16 * 512)
        for L in range(2):
            bk, bb = self.bank()
            for g in range(24):
                wt, wb = self.wload(self.W['mod_w'][L][:, g * 512:(g + 1) * 512], 16, 512)
                for oc in range(4):
                    j = g * 4 + oc
                    self.mm(bk[:, 2 * j:2 * j + 2], [(wt[:, c, oc * 128:(oc + 1) * 128], sT[:, c, :]) for c in range(16)],
                            [wb, bs], bb)
            mv = self.modv[:, L]
            self.dve(lambda e, bk=bk, mv=mv, L=L: e.tensor_tensor(
                out=mv, in0=bk[:, 0:192].rearrange('p (a b) -> p a b', b=2),
                in1=self.P('modb%d' % L).unsqueeze(2).broadcast_to([128, 96, 2]), op=ALU.add), [bb, self.bpk], [self.bmod])
            for (lo, gname) in ((16, 'gmix%d' % L), (64, 'gffn%d' % L)):
                self.dve(lambda e, mv=mv, lo=lo, gname=gname: e.scalar_tensor_tensor(
                    out=mv[:, lo:lo + 16, :], in0=mv[:, lo:lo + 16, :], scalar=1.0,
                    in1=self.P(gname).unsqueeze(2).broadcast_to([128, 16, 2]), op0=ALU.add, op1=ALU.mult),
                    [self.bmod, self.bpk], [self.bmod])
        A.release()
        self.S.barrier()

    def mvec(self, L, idx, c, col):
        return self.modv[:, L, idx * 16 + c, col:col + 1]

    def phase_in(self):
        A = self.A
        A.mark()
        hT = self.scratch('hT', [16, 128, TT], F32)
        self.bhT = [Buf('hT%d' % c) for c in range(16)]
        xt = [A.alloc([4, D], F32) for _ in range(2)]
        bx = [Buf(), Buf()]
        ht = [A.alloc([16, 512], F32) for _ in range(2)]
        bh = [Buf(), Buf()]
        identf = self.C('ident')
        for ti, (t0, tl) in enumerate(TILES):
            nb = tl // 128
            xx, bxx = xt[ti % 2], bx[ti % 2]
            hh, bhh = ht[ti % 2], bh[ti % 2]
            src = self.x[t0:t0 + tl, :] if ti < 4 else self.ctx[:, :]
            self.dma_sp(xx[:, 0:nb, :], src.rearrange('(b p) f -> p b f', p=128), [], [bxx])
            for c in range(16):
                bk, bb = self.bank()
                for b in range(nb):
                    self.pe(lambda e, bk=bk, xx=xx, b=b, c=c: e.transpose(out=bk[:, b * 128:(b + 1) * 128],
                                                                           in_=xx[:, b, c * 128:(c + 1) * 128], identity=identf),
                            [bxx, self.bcst], [bb])
                if c % 2 == 0:
                    self.act(lambda e, bk=bk, hh=hh, c=c, tl=tl: e.activation(out=hh[:, c, 0:tl], in_=bk[:, 0:tl], func=AF.Copy), [bb], [bhh])
                else:
                    self.dve(lambda e, bk=bk, hh=hh, c=c, tl=tl: e.tensor_copy(out=hh[:, c, 0:tl], in_=bk[:, 0:tl]), [bb], [bhh])
            self.dma_sp(hT[:, :, t0:t0 + tl].rearrange('c p t -> p c t'), hh[:, :, 0:tl], [bhh], self.bhT)
        A.release()
        self.S.barrier()

    def phase_norm(self, L, which, uT, buT, tiles=TILES):
        A = self.A
        A.mark()
        hT = self.scr['hT']
        ht = [A.alloc([16, 512], F32) for _ in range(2)]
        bh = [Buf(), Buf()]
        sq = A.alloc([16, 512], BF16)
        bsq = Buf()
        rstd = A.alloc([512], F32)
        brs = Buf()
        tmp = [A.alloc([512], F32) for _ in range(3)]
        btmp = [Buf() for _ in range(3)]
        ones = self.C('ones', True)
        ia, ib = (1, 0) if which == 0 else (4, 3)
        k = 0
        for ti, (t0, tl) in enumerate(tiles):
            col = 0 if ti < 4 else 1
            hh, bhh = ht[ti % 2], bh[ti % 2]
            self.dma_sp(hh[:, :, 0:tl], hT[:, :, t0:t0 + tl].rearrange('c p t -> p c t'), self.bhT, [bhh])
            self.act(lambda e, hh=hh, tl=tl: e.activation(out=sq[:, :, 0:tl], in_=hh[:, :, 0:tl], func=AF.Square), [bhh], [bsq])
            bk, bb = self.bank()
            self.mm(bk[:, 0:tl], [(ones, sq[:, c, 0:tl]) for c in range(16)], [bsq, self.bcst], bb)
            self.act(lambda e, bk=bk, tl=tl: e.activation(out=rstd[:, 0:tl], in_=bk[:, 0:tl], func=AF.Sqrt,
                                                          bias=self.C('eps6'), scale=1.0 / D), [bb, self.bcst], [brs])
            self.dve(lambda e, tl=tl: e.reciprocal(out=rstd[:, 0:tl], in_=rstd[:, 0:tl]), [brs], [brs])
            for c in range(16):
                tm, btm = tmp[k % 3], btmp[k % 3]
                k += 1
                self.dve(lambda e, tm=tm, hh=hh, c=c, tl=tl, col=col: e.scalar_tensor_tensor(
                    out=tm[:, 0:tl], in0=hh[:, c, 0:tl], scalar=self.mvec(L, ia, c, col), in1=rstd[:, 0:tl],
                    op0=ALU.mult, op1=ALU.mult), [bhh, brs, self.bmod], [btm])
                self.act(lambda e, tm=tm, c=c, t0=t0, tl=tl, col=col: e.activation(
                    out=uT[:, c, t0:t0 + tl], in_=tm[:, 0:tl], func=AF.Identity, bias=self.mvec(L, ib, c, col), scale=1.0),
                    [btm, self.bmod], [buT])
        A.release()
        self.S.barrier()

    def phase_resid_proj(self, L, gidx, actT, bact, KC, w_dram, tiles=TILES):
        A = self.A
        A.mark()
        hT = self.scr['hT']
        self.wpool(2, KC * 512)
        hrow = [A.alloc([TT], F32) for _ in range(3)]
        bhr = [Buf() for _ in range(3)]
        for og in range(4):
            wt, wb = self.wload(w_dram[:, og * 512:(og + 1) * 512], KC, 512)
            for o4 in range(4):
                oc = og * 4 + o4
                hr, bh = hrow[oc % 3], bhr[oc % 3]
                self.dma_sp(hr, hT[oc], [self.bhT[oc]], [bh])
                for ti, (t0, tl) in enumerate(tiles):
                    col = 0 if ti < 4 else 1
                    bk, bb = self.bank()
                    self.mm(bk[:, 0:tl], [(wt[:, c, o4 * 128:(o4 + 1) * 128], actT[:, c, t0:t0 + tl]) for c in range(KC)],
                            [wb, bact], bb)
                    self.dve(lambda e, bk=bk, hr=hr, t0=t0, tl=tl, oc=oc, col=col: e.scalar_tensor_tensor(
                        out=hr[:, t0:t0 + tl], in0=bk[:, 0:tl], scalar=self.mvec(L, gidx, oc, col), in1=hr[:, t0:t0 + tl],
                        op0=ALU.mult, op1=ALU.add), [bb, bh, self.bmod], [bh])
                self.dma_sp(hT[oc], hr, [bh], [self.bhT[oc]])
        A.release()
        self.S.barrier()

    def phase_ffn_up(self, L, fT, bfT, tiles=TILES):
        A = self.A
        A.mark()
        hid = self.scr.get('hid')
        if hid is None:
            hid = self.scratch('hid', [44, 128, TT], BF16)
            self.bhid = [Buf('hid%d' % c) for c in range(44)]
        GP = 2308
        self.wpool(4, 16 * 512)
        gpad = [A.alloc([GP], BF16) for _ in range(2)]
        bgp = [Buf(), Buf()]
        vsb = [A.alloc([TT], BF16) for _ in range(2)]
        bvs = [Buf(), Buf()]
        dg = [A.alloc([3, 128], BF16) for _ in range(2)]
        bdg = [Buf(), Buf()]
        sg = [A.alloc([512], BF16) for _ in range(3)]
        bsg = [Buf() for _ in range(3)]
        hrow = [A.alloc([TT], BF16) for _ in range(3)]
        bhr = [Buf() for _ in range(3)]
        for i in range(2):
            self.pool(lambda e, i=i: e.memset(gpad[i], 0.0), [], [bgp[i]])
        wup = self.W['ffn_w_up'][L]
        fcw = self.P('fcw%d' % L).rearrange('p (c j) -> p c j', j=3)
        fcb = self.P('fcb%d' % L)
        identb = self.C('ident', True)
        k = 0
        for g in range(11):
            wg, bwg = self.wload(wup[:, g * 512:(g + 1) * 512], 16, 512)
            wv, bwv = self.wload(wup[:, FFN + g * 512:FFN + (g + 1) * 512], 16, 512)
            for c4 in range(4):
                c = g * 4 + c4
                gp, bg = gpad[c % 2], bgp[c % 2]
                vs, bv = vsb[c % 2], bvs[c % 2]
                dd, bd = dg[c % 2], bdg[c % 2]
                hr, bh = hrow[c % 3], bhr[c % 3]
                for j in range(3):
                    self.pool(lambda e, dd=dd, j=j, c=c: e.tensor_scalar(out=dd[:, j, :], in0=identb, scalar1=fcw[:, c, j:j + 1],
                                                                        scalar2=None, op0=ALU.mult), [self.bcst, self.bpk], [bd])
                for ti, (t0, tl) in enumerate(tiles):
                    off = 1 + t0 if ti < 4 else 2051
                    bk, bb = self.bank()
                    self.mm(bk[:, 0:tl], [(wg[:, kc, c4 * 128:(c4 + 1) * 128], fT[:, kc, t0:t0 + tl]) for kc in range(16)], [bwg, bfT], bb)
                    self.act(lambda e, bk=bk, gp=gp, off=off, tl=tl: e.activation(out=gp[:, off:off + tl], in_=bk[:, 0:tl], func=AF.Copy), [bb], [bg])
                    bk2, bb2 = self.bank()
                    self.mm(bk2[:, 0:tl], [(wv[:, kc, c4 * 128:(c4 + 1) * 128], fT[:, kc, t0:t0 + tl]) for kc in range(16)], [bwv, bfT], bb2)
                    self.dve(lambda e, bk2=bk2, vs=vs, t0=t0, tl=tl: e.tensor_copy(out=vs[:, t0:t0 + tl], in_=bk2[:, 0:tl]), [bb2], [bv])
                for ti, (t0, tl) in enumerate(tiles):
                    base = t0 if ti < 4 else 2050
                    bk, bb = self.bank()
                    self.mm(bk[:, 0:tl], [(dd[:, j, :], gp[:, base + j:base + j + tl]) for j in range(3)], [bd, bg], bb)
                    s_, bs_ = sg[k % 3], bsg[k % 3]
                    k += 1
                    self.act(lambda e, bk=bk, s_=s_, tl=tl, c=c: e.activation(out=s_[:, 0:tl], in_=bk[:, 0:tl], func=AF.Silu,
                                                                             bias=fcb[:, c:c + 1], scale=1.0), [bb, self.bpk], [bs_])
                    self.pool(lambda e, s_=s_, hr=hr, vs=vs, t0=t0, tl=tl: e.tensor_tensor(out=hr[:, t0:t0 + tl], in0=s_[:, 0:tl],
                                                                                         in1=vs[:, t0:t0 + tl], op=ALU.mult), [bs_, bv], [bh])
                self.dma_sp(hid[c], hr, [bh], [self.bhid[c]])
        A.release()
        self.S.barrier()

    def phase_ffn_down(self, L, KG=4, tiles=TILES):
        A = self.A
        A.mark()
        hid = self.scr['hid']
        hT = self.scr['hT']
        wdn = self.W['ffn_w_down'][L]
        ng = 44 // KG
        self.wpool(2, KG * 1024)
        hg = [A.alloc([KG, TT], BF16) for _ in range(2)]
        bhg = [Buf(), Buf()]
        acc = A.alloc([8, TT], F32)
        bacc = [Buf('acc%d' % i) for i in range(8)]
        hrow = [A.alloc([TT], F32) for _ in range(2)]
        bhr = [Buf(), Buf()]
        n = 0
        for half in range(2):
            for kg in range(ng):
                h_, bh_ = hg[n % 2], bhg[n % 2]
                n += 1
                self.dma_sp(h_, hid[kg * KG:(kg + 1) * KG].rearrange('c p t -> p c t'), self.bhid[kg * KG:(kg + 1) * KG], [bh_])
                wt, wb = self.wload(wdn[kg * KG * 128:(kg + 1) * KG * 128, half * 1024:(half + 1) * 1024], KG, 1024)
                for o in range(8):
                    for ti, (t0, tl) in enumerate(tiles):
                        bk, bb = self.bank()
                        self.mm(bk[:, 0:tl], [(wt[:, kc, o * 128:(o + 1) * 128], h_[:, kc, t0:t0 + tl]) for kc in range(KG)], [wb, bh_], bb)
                        if kg == 0:
                            self.act(lambda e, bk=bk, o=o, t0=t0, tl=tl: e.activation(out=acc[:, o, t0:t0 + tl], in_=bk[:, 0:tl], func=AF.Copy),
                                     [bb], [bacc[o]])
                        else:
                            self.dve(lambda e, bk=bk, o=o, t0=t0, tl=tl: e.tensor_tensor(out=acc[:, o, t0:t0 + tl], in0=bk[:, 0:tl],
                                                                                      in1=acc[:, o, t0:t0 + tl], op=ALU.add), [bb, bacc[o]], [bacc[o]])
            for o in range(8):
                oc = half * 8 + o
                hr, bh = hrow[o % 2], bhr[o % 2]
                self.dma_sp(hr, hT[oc], [self.bhT[oc]], [bh])
                for (lo, hi, col) in (((0, NLAT, 0), (NLAT, TT, 1)) if len(tiles) == 5 else ((0, NLAT, 0),)):
                    self.dve(lambda e, hr=hr, o=o, oc=oc, lo=lo, hi=hi, col=col: e.scalar_tensor_tensor(
                        out=hr[:, lo:hi], in0=acc[:, o, lo:hi], scalar=self.mvec(L, 5, oc, col), in1=hr[:, lo:hi],
                        op0=ALU.mult, op1=ALU.add), [bacc[o], bh, self.bmod], [bh])
                self.dma_sp(hT[oc], hr, [bh], [self.bhT[oc]])
        A.release()
        self.S.barrier()

    def phase_final(self):
        A = self.A
        A.mark()
        hT = self.scr['hT']
        ht = [A.alloc([16, 512], F32) for _ in range(2)]
        bh = [Buf(), Buf()]
        sq = A.alloc([16, 512], BF16)
        bsq = Buf()
        rstd = A.alloc([512], F32)
        brs = Buf()
        yn = [A.alloc([16, 512], F32) for _ in range(2)]
        byn = [Buf(), Buf()]
        ot = [A.alloc([D], F32) for _ in range(3)]
        bot = [Buf() for _ in range(3)]
        ones = self.C('ones', True)
        identf = self.C('ident')
        gfin = self.P('gfin')
        k = 0
        for ti, (t0, tl) in enumerate(TILES[:4]):
            hh, bhh = ht[ti % 2], bh[ti % 2]
            y_, by_ = yn[ti % 2], byn[ti % 2]
            self.dma_sp(hh, hT[:, :, t0:t0 + tl].rearrange('c p t -> p c t'), self.bhT, [bhh])
            self.act(lambda e, hh=hh: e.activation(out=sq, in_=hh, func=AF.Square), [bhh], [bsq])
            bk, bb = self.bank()
            self.mm(bk, [(ones, sq[:, c, :]) for c in range(16)], [bsq, self.bcst], bb)
            self.act(lambda e, bk=bk: e.activation(out=rstd, in_=bk, func=AF.Sqrt, bias=self.C('eps6'), scale=1.0 / D), [bb, self.bcst], [brs])
            self.dve(lambda e: e.reciprocal(out=rstd, in_=rstd), [brs], [brs])
            for c in range(16):
                self.dve(lambda e, hh=hh, y_=y_, c=c: e.scalar_tensor_tensor(out=y_[:, c, :], in0=hh[:, c, :], scalar=gfin[:, c:c + 1],
                                                                       in1=rstd, op0=ALU.mult, op1=ALU.mult), [bhh, brs, self.bpk], [by_])
            for b in range(4):
                o_, bo_ = ot[k % 3], bot[k % 3]
                k += 1
                for fg in range(4):
                    bk, bb = self.bank()
                    for f4 in range(4):
                        c = fg * 4 + f4
                        self.pe(lambda e, bk=bk, y_=y_, c=c, b=b, f4=f4: e.transpose(out=bk[:, f4 * 128:(f4 + 1) * 128],
                                                                                      in_=y_[:, c, b * 128:(b + 1) * 128], identity=identf),
                                [by_, self.bcst], [bb])
                    if fg % 2 == 0:
                        self.act(lambda e, bk=bk, o_=o_, fg=fg: e.activation(out=o_[:, fg * 512:(fg + 1) * 512], in_=bk, func=AF.Copy), [bb], [bo_])
                    else:
                        self.dve(lambda e, bk=bk, o_=o_, fg=fg: e.tensor_copy(out=o_[:, fg * 512:(fg + 1) * 512], in_=bk), [bb], [bo_])
                r0 = t0 + b * 128
                self.dma_sp(self.out[r0:r0 + 128, :], o_, [bo_], [self.bout])
        A.release()
        self.S.barrier()

    def phase_ab_qkv(self, uT, buT, qT, kT, vtok, bq, bkk, bv):
        A = self.A
        A.mark()
        w_in = self.W['ab_w_in'][0]
        rope = A.alloc([2, NLAT], F32)
        brope = Buf()
        self.dma_sp(rope, self.roped.rearrange('a p t -> p a t'), [], [brope])
        self.wpool(2, 16 * 256)
        ws_ap = [A.alloc([16 * 256], BF16) for _ in range(2)]
        bws = [Buf(), Buf()]
        sq = [A.alloc([512], BF16) for _ in range(2)]
        bsq = [Buf(), Buf()]
        rstd = [A.alloc([512], F32) for _ in range(2)]
        brs = [Buf(), Buf()]
        t1 = [A.alloc([512], F32) for _ in range(2)]
        bt1 = [Buf(), Buf()]
        t2 = [A.alloc([512], F32) for _ in range(2)]
        bt2 = [Buf(), Buf()]
        ones = self.C('ones', True)
        k = 0
        groups = [(g * 256, 2, qT, bq, 'qg', 'qgs', 2 * g) for g in range(4)] + [(1024, 2, kT, bkk, 'kg', 'kgs', 0)]
        for gi, (col0, nh, dst, bdst, gn, gsn, h0) in enumerate(groups):
            n = nh * 128
            wt, wb = self.wload(w_in[:, col0:col0 + n], 16, n)
            ws = ws_ap[gi % 2][:, 0:16 * n].rearrange('p (c n) -> p c n', n=n)
            bw_ = bws[gi % 2]
            wtv = wt.rearrange('p c (g b e) -> p (c g) b e', b=2, e=32)
            wsv = ws.rearrange('p c (g b e) -> p (c g) b e', b=2, e=32)
            for b in range(2):
                self.pool(lambda e, wsv=wsv, wtv=wtv, b=b: e.tensor_copy(out=wsv[:, :, b, :], in_=wtv[:, :, 1 - b, :]), [wb], [bw_])
            g_ap, gs_ap = self.P(gn), self.P(gsn)
            for hh in range(nh):
                hd = h0 + hh
                for ti, (t0, tl) in enumerate(TILES):
                    i2 = k % 2
                    k += 1
                    bkq, bbq = self.bank()
                    self.mm(bkq[:, 0:tl], [(wt[:, c, hh * 128:(hh + 1) * 128], uT[:, c, t0:t0 + tl]) for c in range(16)], [wb, buT], bbq)
                    self.act(lambda e, bkq=bkq, i2=i2, tl=tl: e.activation(out=sq[i2][:, 0:tl], in_=bkq[:, 0:tl], func=AF.Square), [bbq], [bsq[i2]])
                    bks, bbs = self.bank()
                    self.mm(bks[:, 0:tl], [(ones, sq[i2][:, 0:tl])], [bsq[i2], self.bcst], bbs)
                    self.act(lambda e, bks=bks, i2=i2, tl=tl: e.activation(out=rstd[i2][:, 0:tl], in_=bks[:, 0:tl], func=AF.Sqrt,
                                                                          bias=self.C('eps6'), scale=1.0 / 128), [bbs, self.bcst], [brs[i2]])
                    self.dve(lambda e, i2=i2, tl=tl: e.reciprocal(out=rstd[i2][:, 0:tl], in_=rstd[i2][:, 0:tl]), [brs[i2]], [brs[i2]])
                    if ti < 4:
                        bkw, bbw = self.bank()
                        self.mm(bkw[:, 0:tl], [(ws[:, c, hh * 128:(hh + 1) * 128], uT[:, c, t0:t0 + tl]) for c in range(16)], [bw_, buT], bbw)
                        self.dve(lambda e, bkq=bkq, i2=i2, t0=t0, tl=tl, g_ap=g_ap: e.scalar_tensor_tensor(
                            out=t1[i2][:, 0:tl], in0=bkq[:, 0:tl], scalar=g_ap, in1=rope[:, 0, t0:t0 + tl], op0=ALU.mult, op1=ALU.mult),
                            [bbq, brope, self.bpk], [bt1[i2]])
                        self.dve(lambda e, bkw=bkw, i2=i2, t0=t0, tl=tl, gs_ap=gs_ap: e.scalar_tensor_tensor(
                            out=t2[i2][:, 0:tl], in0=bkw[:, 0:tl], scalar=gs_ap, in1=rope[:, 1, t0:t0 + tl], op0=ALU.mult, op1=ALU.mult),
                            [bbw, brope, self.bpk], [bt2[i2]])
                        self.pool(lambda e, i2=i2, tl=tl: e.tensor_tensor(out=t1[i2][:, 0:tl], in0=t1[i2][:, 0:tl], in1=t2[i2][:, 0:tl], op=ALU.add),
                                  [bt1[i2], bt2[i2]], [bt1[i2]])
                        self.pool(lambda e, i2=i2, dst=dst, hd=hd, t0=t0, tl=tl: e.tensor_tensor(
                            out=dst[:, hd, t0:t0 + tl], in0=t1[i2][:, 0:tl], in1=rstd[i2][:, 0:tl], op=ALU.mult), [bt1[i2], brs[i2]], [bdst])
                    else:
                        self.dve(lambda e, bkq=bkq, i2=i2, dst=dst, hd=hd, t0=t0, tl=tl, g_ap=g_ap: e.scalar_tensor_tensor(
                            out=dst[:, hd, t0:t0 + tl], in0=bkq[:, 0:tl], scalar=g_ap, in1=rstd[i2][:, 0:tl], op0=ALU.mult, op1=ALU.mult),
                            [bbq, brs[i2], self.bpk], [bdst])
        wv, bwv = self.wload(w_in[:, 1280:1536], 16, 256)
        for tb in range(18):
            bk, bb = self.bank()
            self.mm(bk[:, 0:256], [(uT[:, c, tb * 128:(tb + 1) * 128], wv[:, c, :]) for c in range(16)], [bwv, buT], bb)
            self.act(lambda e, bk=bk, tb=tb: e.activation(out=vtok[:, tb, :], in_=bk[:, 0:256], func=AF.Copy), [bb], [bv])
        A.release()
        self.S.barrier()

    def phase_ab_conv(self, uT, buT):
        A = self.A
        A.mark()
        w_in = self.W['ab_w_in'][0]
        cv = self.scratch('cv', [8, 128, TT], F32)
        self.bcv = [Buf('cv%d' % c) for c in range(8)]
        GP = 2364
        self.wpool(4, 16 * 512)
        gpad = [A.alloc([GP], BF16) for _ in range(2)]
        bgp = [Buf(), Buf()]
        dg = [A.alloc([31, 128], BF16) for _ in range(2)]
        bdg = [Buf(), Buf()]
        sig = [A.alloc([512], F32) for _ in range(2)]
        bsig = [Buf(), Buf()]
        cvr = [A.alloc([TT], F32) for _ in range(2)]
        bcr = [Buf(), Buf()]
        for i in range(2):
            self.pool(lambda e, i=i: e.memset(gpad[i], 0.0), [], [bgp[i]])
        acw = self.P('acw').rearrange('p (c j) -> p c j', j=31)
        acb = self.P('acb')
        identb = self.C('ident', True)
        k = 0
        for g in range(2):
            wa, bwa = self.wload(w_in[:, 1536 + g * 512:1536 + (g + 1) * 512], 16, 512)
            wb_, bwb = self.wload(w_in[:, 2560 + g * 512:2560 + (g + 1) * 512], 16, 512)
            for c4 in range(4):
                cc = g * 4 + c4
                gp, bg = gpad[cc % 2], bgp[cc % 2]
                dd, bd = dg[cc % 2], bdg[cc % 2]
                cr, bc = cvr[cc % 2], bcr[cc % 2]
                for j in range(31):
                    self.pool(lambda e, dd=dd, j=j, cc=cc: e.tensor_scalar(out=dd[:, j, :], in0=identb, scalar1=acw[:, cc, j:j + 1],
                                                                          scalar2=None, op0=ALU.mult), [self.bcst, self.bpk], [bd])
                for ti, (t0, tl) in enumerate(TILES):
                    off = 15 + t0 if ti < 4 else 2093
                    i2 = k % 2
                    k += 1
                    bka, bba = self.bank()
                    self.mm(bka[:, 0:tl], [(wa[:, kc, c4 * 128:(c4 + 1) * 128], uT[:, kc, t0:t0 + tl]) for kc in range(16)], [bwa, buT], bba)
                    bkb, bbb = self.bank()
                    self.mm(bkb[:, 0:tl], [(wb_[:, kc, c4 * 128:(c4 + 1) * 128], uT[:, kc, t0:t0 + tl]) for kc in range(16)], [bwb, buT], bbb)
                    self.act(lambda e, bkb=bkb, i2=i2, tl=tl: e.activation(out=sig[i2][:, 0:tl], in_=bkb[:, 0:tl], func=AF.Sigmoid), [bbb], [bsig[i2]])
                    self.dve(lambda e, bka=bka, i2=i2, gp=gp, off=off, tl=tl: e.tensor_tensor(out=gp[:, off:off + tl], in0=bka[:, 0:tl],
                                                                                           in1=sig[i2][:, 0:tl], op=ALU.mult), [bba, bsig[i2]], [bg])
                for ti, (t0, tl) in enumerate(TILES):
                    base = t0 if ti < 4 else 2078
                    bk, bb = self.bank()
                    self.mm(bk[:, 0:tl], [(dd[:, j, :], gp[:, base + j:base + j + tl]) for j in range(31)], [bd, bg], bb)
                    self.act(lambda e, bk=bk, cr=cr, t0=t0, tl=tl, cc=cc: e.activation(out=cr[:, t0:t0 + tl], in_=bk[:, 0:tl], func=AF.Identity,
                                                                                    bias=acb[:, cc:cc + 1], scale=1.0), [bb, self.bpk], [bc])
                self.dma_sp(cv[cc], cr, [bc], [self.bcv[cc]])
        A.release()
        self.S.barrier()

    def phase_att(self, qT, kT, vtok, bq, bkk, bv, yin, byin):
        A = self.A
        A.mark()
        pT = [A.alloc([512], BF16) for _ in range(4)]
        bp = [Buf() for _ in range(4)]
        rinv = [A.alloc([512], F32) for _ in range(2)]
        bri = [Buf(), Buf()]
        ones = self.C('ones', True)
        scale = 128.0 ** -0.5
        k = 0
        n = 0
        for h in range(8):
            hk = h // 4
            for ti, (t0, tl) in enumerate(TILES):
                chunks = list(range(18)) if ti < 4 else [16, 17]
                ai = 2 * (n % 2)
                bko, bbo = self.banks[ai], self.bbufs[ai]
                bkr, bbr = self.banks[ai + 1], self.bbufs[ai + 1]
                nck = len(chunks)
                for ci, kc in enumerate(chunks):
                    bks, bbs = self.banks[4 + k % 4], self.bbufs[4 + k % 4]
                    self.mm(bks[:, 0:tl], [(kT[:, hk, kc * 128:(kc + 1) * 128], qT[:, h, t0:t0 + tl])], [bkk, bq], bbs)
                    p_, bp_ = pT[k % 4], bp[k % 4]
                    k += 1
                    self.act(lambda e, bks=bks, p_=p_, tl=tl: e.activation(out=p_[:, 0:tl], in_=bks[:, 0:tl], func=AF.Exp, scale=scale), [bbs], [bp_])
                    self.pe(lambda e, bko=bko, p_=p_, kc=kc, hk=hk, tl=tl, ci=ci, nck=nck: e.matmul(
                        bko[:, 0:tl], lhsT=vtok[:, kc, hk * 128:(hk + 1) * 128], rhs=p_[:, 0:tl], start=(ci == 0), stop=(ci == nck - 1)),
                        [bv, bp_], [bbo])
                    self.pe(lambda e, bkr=bkr, p_=p_, tl=tl, ci=ci, nck=nck: e.matmul(
                        bkr[:, 0:tl], lhsT=ones, rhs=p_[:, 0:tl], start=(ci == 0), stop=(ci == nck - 1)), [self.bcst, bp_], [bbr])
                ri, bri_ = rinv[n % 2], bri[n % 2]
                n += 1
                self.dve(lambda e, bkr=bkr, ri=ri, tl=tl: e.reciprocal(out=ri[:, 0:tl], in_=bkr[:, 0:tl]), [bbr], [bri_])
                self.dve(lambda e, bko=bko, ri=ri, h=h, t0=t0, tl=tl: e.tensor_tensor(out=yin[:, h, t0:t0 + tl], in0=bko[:, 0:tl],
                                                                                  in1=ri[:, 0:tl], op=ALU.mult), [bbo, bri_], [byin])
        A.release()
        self.S.barrier()

    def phase_conv_ln(self, yin, byin):
        A = self.A
        A.mark()
        cv = self.scr['cv']
        ct = [A.alloc([8, 512], F32) for _ in range(2)]
        bct = [Buf(), Buf()]
        sq = A.alloc([8, 512], F32)
        bsq = Buf()
        mean = A.alloc([512], F32)
        msq = A.alloc([512], F32)
        rstd = A.alloc([512], F32)
        bst = Buf()
        xc = [A.alloc([512], F32) for _ in range(2)]
        bxc = [Buf(), Buf()]
        onesf = self.C('ones')
        ang, anb = self.P('ang'), self.P('anb')
        k = 0
        for ti, (t0, tl) in enumerate(TILES):
            c_, bc_ = ct[ti % 2], bct[ti % 2]
            self.dma_sp(c_[:, :, 0:tl], cv[:, :, t0:t0 + tl].rearrange('c p t -> p c t'), self.bcv, [bc_])
            self.act(lambda e, c_=c_, tl=tl: e.activation(out=sq[:, :, 0:tl], in_=c_[:, :, 0:tl], func=AF.Square), [bc_], [bsq])
            bk1, bb1 = self.bank()
            self.mm(bk1[:, 0:tl], [(onesf, c_[:, c, 0:tl]) for c in range(8)], [bc_, self.bcst], bb1)
            bk2, bb2 = self.bank()
            self.mm(bk2[:, 0:tl], [(onesf, sq[:, c, 0:tl]) for c in range(8)], [bsq, self.bcst], bb2)
            self.act(lambda e, bk1=bk1, tl=tl: e.activation(out=mean[:, 0:tl], in_=bk1[:, 0:tl], func=AF.Copy, scale=1.0 / 1024), [bb1], [bst])
            self.dve(lambda e, tl=tl: e.tensor_tensor(out=msq[:, 0:tl], in0=mean[:, 0:tl], in1=mean[:, 0:tl], op=ALU.mult), [bst], [bst])
            self.dve(lambda e, bk2=bk2, tl=tl: e.scalar_tensor_tensor(out=rstd[:, 0:tl], in0=bk2[:, 0:tl], scalar=1.0 / 1024, in1=msq[:, 0:tl],
                                                                      op0=ALU.mult, op1=ALU.subtract), [bb2, bst], [bst])
            self.act(lambda e, tl=tl: e.activation(out=rstd[:, 0:tl], in_=rstd[:, 0:tl], func=AF.Sqrt, bias=self.C('eps6'), scale=1.0), [bst, self.bcst], [bst])
            self.dve(lambda e, tl=tl: e.reciprocal(out=rstd[:, 0:tl], in_=rstd[:, 0:tl]), [bst], [bst])
            for c in range(8):
                x_, bx_ = xc[k % 2], bxc[k % 2]
                k += 1
                self.dve(lambda e, x_=x_, c_=c_, c=c, tl=tl: e.tensor_tensor(out=x_[:, 0:tl], in0=c_[:, c, 0:tl], in1=mean[:, 0:tl], op=ALU.subtract),
                         [bc_, bst], [bx_])
                self.pool(lambda e, x_=x_, tl=tl: e.tensor_tensor(out=x_[:, 0:tl], in0=x_[:, 0:tl], in1=rstd[:, 0:tl], op=ALU.mult), [bx_, bst], [bx_])
                self.act(lambda e, x_=x_, c=c, t0=t0, tl=tl: e.activation(out=yin[:, 8 + c, t0:t0 + tl], in_=x_[:, 0:tl], func=AF.Silu,
                                                                        bias=anb[:, c:c + 1], scale=ang[:, c:c + 1]), [bx_, self.bpk], [byin])
        A.release()
        self.S.barrier()

    def build(self, stop_after=None, start_layer=0):
        A = self.A
        self.phase_mod()
        self.phase_in()
        if start_layer == 1:
            return self.build_l1()
        A.mark()
        uT = A.alloc([16, TT], BF16)
        buT = Buf('uT')
        self.phase_norm(0, 0, uT, buT)
        self.phase_ab_conv(uT, buT)
        qT = A.alloc([8, TT], BF16)
        kT = A.alloc([2, TT], BF16)
        vtok = A.alloc([18, 256], BF16)
        bq, bkk, bv = Buf('q'), Buf('k'), Buf('v')
        self.phase_ab_qkv(uT, buT, qT, kT, vtok, bq, bkk, bv)
        yin, byin = uT, buT
        self.phase_att(qT, kT, vtok, bq, bkk, bv, yin, byin)
        A.release()
        A.mark()
        yin = A.alloc([16, TT], BF16)
        self.phase_conv_ln(yin, byin)
        self.phase_resid_proj(0, 2, yin, byin, 16, self.W['ab_w_out'][0])
        if stop_after == 'mix0':
            A.release()
            return self.finish()
        self.phase_norm(0, 1, yin, byin)
        self.phase_ffn_up(0, yin, byin)
        A.release()
        self.phase_ffn_down(0)
        if stop_after == 'l0':
            return self.finish()
        return self.build_l1()

    def build_dbg(self, stop_after):
        A = self.A
        A.mark()
        uT = A.alloc([16, TT], BF16)
        buT = Buf('uT1')
        A.mark()
        tmp = [A.alloc([16, 512], F32) for _ in range(2)]
        bt = [Buf(), Buf()]
        for ti, (t0, tl) in enumerate(TILES):
            self.dma_sp(tmp[ti % 2][:, :, 0:tl], self.u1T[:, :, t0:t0 + tl].rearrange('c p t -> p c t'), [], [bt[ti % 2]])
            self.dve(lambda e, ti=ti, t0=t0, tl=tl: e.tensor_copy(out=uT[:, :, t0:t0 + tl], in_=tmp[ti % 2][:, :, 0:tl]), [bt[ti % 2]], [buT])
        A.release()
        self.S.barrier()
        self.phase_cd_proj(uT, buT)
        A.release()
        if stop_after == 'cdproj':
            return self.finish()
        self.phase_rwkv()
        if stop_after == 'rwkv':
            return self.finish()
        self.phase_ret()
        return self.finish()

    def build_l1(self):
        A = self.A
        LT = TILES[:4]
        A.mark()
        uT = A.alloc([16, TT], BF16)
        buT = Buf('uT1')
        self.phase_norm(1, 0, uT, buT)
        self.phase_cd_proj(uT, buT)
        A.release()
        self.phase_rwkv()
        self.phase_ret()
        A.mark()
        yin = A.alloc([24, NLAT], BF16)
        byin = Buf('yin1')
        self.dma_sp(yin, self.scr['yin'].rearrange('c p t -> p c t'), self.byin_s, [byin])
        self.phase_resid_proj(1, 2, yin, byin, 24, self.W['cd_w_out'][0], tiles=LT)
        A.release()
        A.mark()
        fT = A.alloc([16, TT], BF16)
        bfT = Buf('fT1')
        self.phase_norm(1, 1, fT, bfT, tiles=LT)
        self.phase_ffn_up(1, fT, bfT, tiles=LT)
        A.release()
        self.phase_ffn_down(1, tiles=LT)
        self.phase_final()
        return self.finish()

    def finish(self):
        self.S.barrier()
        self.S.emit(self.nc)
        return self.nc


def make_in_maps(inp):
    pk = pack_params(inp).array()
    cst = const_pack().array()
    rope = rope_tables()
    maps = []
    for b in range(8):
        cc = np.stack([fm(inp['c'][b]), fm(inp['c_ctx'])], axis=2).reshape(128, 32)
        m = {'x': np.ascontiguousarray(inp['x'][b]), 'ctx': np.ascontiguousarray(inp['ctx'][b]), 'cc': np.ascontiguousarray(cc),
             'pk': pk, 'cst': cst, 'rope': rope}
        for k in WEIGHT_NAMES:
            m[k] = np.ascontiguousarray(np.asarray(inp[k], np.float32))
        maps.append(m)
    return maps


def kernel(**inputs):
    inp = {k: np.asarray(v) for k, v in inputs.items()}
    mk = MK()
    nc = mk.build()
    res = run_bass_kernel_spmd(nc, make_in_maps(inp), core_ids=list(range(8)))
    return np.stack([r['out'] for r in res.results], axis=0).astype(np.float32)
```

```python
import numpy as np
import concourse.bass as bass
import concourse.mybir as mybir
from concourse.bass_utils import run_bass_kernel_spmd
from contextlib import ExitStack

F32 = mybir.dt.float32
BF16 = mybir.dt.bfloat16
AF = mybir.ActivationFunctionType
ALU = mybir.AluOpType
AX = mybir.AxisListType

D = 2048
NLAT = 2048
NCTX = 256
TT = NLAT + NCTX
FFN = 5632
TILES = [(0, 512), (512, 512), (1024, 512), (1536, 512), (2048, 256)]
COMPUTE = ('pe', 'act', 'dve', 'pool')
STREAMS = ('pe', 'act', 'dve', 'pool', 'sp')
NDSEM = 8
NRING = 8
RING_CHUNK = 512


class Buf:
    __slots__ = ('name', 'w', 'rs')

    def __init__(self, name=''):
        self.name = name
        self.w = None
        self.rs = {}


class Op:
    __slots__ = ('stream', 'fn', 'pos', 'dma', 'deps', 'waits', 'inc', 'val', 'dsem', 'gid', 'ring')

    def __init__(self, stream, fn, dma):
        self.stream = stream
        self.fn = fn
        self.dma = dma
        self.deps = {}
        self.waits = []
        self.inc = False
        self.val = 0
        self.dsem = None


class Sched:
    def __init__(self, same_engine_sync=True):
        self.streams = {s: [] for s in STREAMS}
        self.all = []
        self.dma_rr = {s: 0 for s in STREAMS}
        self.dma_last = {}
        self.last_c = {}
        self.same = same_engine_sync

    def _adddep(self, o, d):
        if d.dma:
            key = ('d',) + d.dsem + (d.gid,)
        else:
            if d.stream == o.stream and (not self.same or o.stream == 'pe'):
                return
            key = ('e', d.stream)
        cur = o.deps.get(key)
        if cur is None or d.pos > cur.pos:
            o.deps[key] = d

    @staticmethod
    def _flat(xs):
        out = []
        for x in xs:
            if isinstance(x, (list, tuple)):
                out.extend(Sched._flat(x))
            else:
                out.append(x)
        return out

    def op(self, stream, fn, reads=(), writes=(), dma=False):
        reads = self._flat(reads)
        writes = self._flat(writes)
        o = Op(stream, fn, dma)
        o.pos = len(self.streams[stream])
        o.gid = len(self.all)
        for b in reads:
            if b.w is not None:
                self._adddep(o, b.w)
        for b in writes:
            if b.w is not None:
                self._adddep(o, b.w)
            for d in b.rs.values():
                self._adddep(o, d)
        if dma:
            k = self.dma_rr[stream] % NDSEM
            self.dma_rr[stream] += 1
            o.dsem = (stream, k)
            prev = self.dma_last.get(o.dsem)
            if prev is not None:
                self._adddep(o, prev)
            self.dma_last[o.dsem] = o
        for b in writes:
            b.w = o
            b.rs = {}
        rk = ('d', o.gid) if dma else stream
        for b in reads:
            if b.w is not o:
                b.rs[rk] = o
        if not dma:
            self.last_c[stream] = o
        self.streams[stream].append(o)
        self.all.append(o)
        return o

    def barrier(self):
        tails = list(self.last_c.values())
        dl = list(self.dma_last.values())
        for s in STREAMS:
            o = Op(s, None, False)
            o.pos = len(self.streams[s])
            o.gid = len(self.all)
            for d in tails + dl:
                if (not d.dma) and d.stream == s:
                    continue
                if d.dma:
                    key = ('d',) + d.dsem + (d.gid,)
                else:
                    key = ('e', d.stream)
                cur = o.deps.get(key)
                if cur is None or d.pos > cur.pos:
                    o.deps[key] = d
            self.streams[s].append(o)
            self.all.append(o)

    def finalize(self):
        seen = {s: {} for s in STREAMS}
        for o in self.all:
            sn = seen[o.stream]
            for key, d in o.deps.items():
                if d.dma:
                    k2 = ('d',) + d.dsem
                    if sn.get(k2, -1) >= d.gid:
                        continue
                    sn[k2] = d.gid
                else:
                    if sn.get(key, -1) >= d.pos:
                        continue
                    sn[key] = d.pos
                d.inc = True
                o.waits.append(d)
        for s in STREAMS:
            c = 0
            per = [0] * NRING
            for o in self.streams[s]:
                if o.dma or o.fn is None:
                    continue
                if o.inc:
                    r = (c // RING_CHUNK) % NRING
                    c += 1
                    per[r] += 1
                    o.ring = r
                    o.val = per[r]
        cnt = {}
        for o in self.all:
            if o.dma:
                cnt[o.dsem] = cnt.get(o.dsem, 0) + 16
                o.val = cnt[o.dsem]

    def emit(self, nc):
        self.finalize()
        with ExitStack() as es:
            esem = {(s, r): es.enter_context(nc.semaphore('e_%s%d' % (s, r))) for s in COMPUTE for r in range(NRING)}
            dsem = {}
            for s in STREAMS:
                for k in range(min(NDSEM, self.dma_rr[s])):
                    dsem[(s, k)] = es.enter_context(nc.semaphore('d_%s%d' % (s, k)))
            block = es.enter_context(nc.Block())

            def replay(stream, eng):
                for o in self.streams[stream]:
                    for d in o.waits:
                        if d.dma:
                            eng.wait_ge(dsem[d.dsem], d.val)
                        else:
                            eng.wait_ge(esem[(d.stream, d.ring)], d.val)
                    if o.fn is None:
                        continue
                    ins = o.fn(eng)
                    if o.dma:
                        ins.then_inc(dsem[o.dsem], 16)
                    elif o.inc:
                        ins.then_inc(esem[(o.stream, o.ring)], 1)

            @block.sync
            def _(e):
                replay('sp', e)

            @block.tensor
            def _(e):
                replay('pe', e)

            @block.scalar
            def _(e):
                replay('act', e)

            @block.vector
            def _(e):
                replay('dve', e)

            @block.gpsimd
            def _(e):
                replay('pool', e)


class Arena:
    def __init__(self, nc, nbytes=212480):
        self.n32 = nbytes // 4
        self.t = nc.alloc_sbuf_tensor('arena', [128, self.n32], F32)
        self.off = 0
        self.marks = []
        self.peak = 0

    def alloc(self, free_shape, dtype=F32):
        n = 1
        for s in free_shape:
            n *= s
        nb = n * (2 if dtype == BF16 else 4)
        nb = (nb + 63) // 64 * 64
        o32 = self.off // 4
        assert self.off + nb <= self.n32 * 4, ('SBUF arena overflow', self.off, nb)
        self.off += nb
        self.peak = max(self.peak, self.off)
        ap = self.t[:, o32:o32 + nb // 4]
        if dtype == BF16:
            ap = ap.bitcast(BF16)[:, 0:n]
        else:
            ap = ap[:, 0:n]
        if len(free_shape) == 2:
            ap = ap.rearrange('p (a b) -> p a b', b=free_shape[1])
        elif len(free_shape) == 3:
            ap = ap.rearrange('p (a b c) -> p a b c', b=free_shape[1], c=free_shape[2])
        return ap

    def mark(self):
        self.marks.append(self.off)

    def release(self):
        self.off = self.marks.pop()


def fm(v):
    v = np.asarray(v, np.float32).reshape(-1, 128)
    return np.ascontiguousarray(v.T)


class Pack:
    def __init__(self):
        self.cols = {}
        self.parts = []
        self.n = 0

    def add(self, name, arr):
        arr = np.asarray(arr, np.float32)
        assert arr.shape[0] == 128
        arr = arr.reshape(128, -1)
        self.cols[name] = (self.n, arr.shape[1])
        self.parts.append(arr)
        self.n += arr.shape[1]

    def array(self):
        return np.ascontiguousarray(np.concatenate(self.parts, axis=1))


def pack_layout():
    L = []
    for l in range(2):
        L += [('gmix%d' % l, 16), ('gffn%d' % l, 16), ('modb%d' % l, 96), ('fcb%d' % l, 44), ('fcw%d' % l, 132)]
    L += [('qg', 1), ('qgs', 1), ('kg', 1), ('kgs', 1), ('acw', 248), ('acb', 8), ('ang', 8), ('anb', 8)]
    L += [('mu', 54), ('w0', 16), ('a0', 16), ('kk', 8), ('ka', 8), ('rk', 8), ('lng', 8), ('lnb', 8),
          ('gng', 16), ('gnb', 16), ('dlog', 16), ('gfin', 16)]
    return L


def pack_params(inp):
    P = Pack()
    for l in range(2):
        P.add('gmix%d' % l, fm(inp['norm_mix_g'][l]))
        P.add('gffn%d' % l, fm(inp['norm_ffn_g'][l]))
        P.add('modb%d' % l, fm(inp['mod_b'][l]))
        P.add('fcb%d' % l, fm(inp['ffn_conv_b'][l]))
        w = np.asarray(inp['ffn_conv_w'][l], np.float32)
        P.add('fcw%d' % l, np.stack([fm(w[j]) for j in range(3)], axis=2))
    sw = np.arange(128) ^ 32
    qg = np.asarray(inp['ab_q_norm'][0], np.float32)
    kg = np.asarray(inp['ab_k_norm'][0], np.float32)
    P.add('qg', qg.reshape(128, 1))
    P.add('qgs', qg[sw].reshape(128, 1))
    P.add('kg', kg.reshape(128, 1))
    P.add('kgs', kg[sw].reshape(128, 1))
    w = np.asarray(inp['ab_conv_w'][0], np.float32)
    P.add('acw', np.stack([fm(w[j]) for j in range(31)], axis=2))
    P.add('acb', fm(inp['ab_conv_b'][0]))
    P.add('ang', fm(inp['ab_conv_norm_g'][0]))
    P.add('anb', fm(inp['ab_conv_norm_b'][0]))
    mu = np.zeros((2, 27 * 128), np.float32)
    mu[:, :3360] = np.asarray(inp['cd_shift_mu'][0], np.float32)
    P.add('mu', np.stack([fm(mu[j]) for j in range(2)], axis=2))
    P.add('w0', np.stack([fm(inp['rwkv_w0'][0][j]) for j in range(2)], axis=1))
    P.add('a0', np.stack([fm(inp['rwkv_a0'][0][j]) for j in range(2)], axis=1))
    P.add('kk', fm(inp['rwkv_k_k'][0]))
    P.add('ka', fm(inp['rwkv_k_a'][0]))
    P.add('rk', fm(np.asarray(inp['rwkv_r_k'][0]).reshape(-1)))
    P.add('lng', fm(inp['rwkv_ln_g'][0]))
    P.add('lnb', fm(inp['rwkv_ln_b'][0]))
    P.add('gng', fm(inp['ret_gn_g'][0]))
    P.add('gnb', fm(inp['ret_gn_b'][0]))
    P.add('dlog', np.broadcast_to(np.asarray(inp['ret_decay_logit'][0], np.float32).reshape(1, 16), (128, 16)))
    P.add('gfin', fm(inp['final_norm_g']))
    assert [(k, v[1]) for k, v in P.cols.items()] == pack_layout()
    return P


CST_LAYOUT = [('ident', 128), ('ones', 128), ('bones', 128), ('eps6', 1), ('eps12', 1), ('epsgn', 1), ('one', 1),
              ('U', 128), ('L', 128), ('IU', 64), ('IL', 64), ('diffT', 128), ('maskF', 128), ('maskB', 128), ('irow', 128), ('irowb', 128),
              ('jcol', 1), ('jcolb', 1), ('c128', 1)]


def const_pack():
    P = Pack()
    P.add('ident', np.eye(128, dtype=np.float32))
    P.add('ones', np.ones((128, 128), np.float32))
    bo = np.zeros((128, 128), np.float32)
    bo[:64, :64] = 1
    bo[64:, 64:] = 1
    P.add('bones', bo)
    P.add('eps6', np.full((128, 1), 1e-6, np.float32))
    P.add('eps12', np.full((128, 1), 1e-12, np.float32))
    P.add('epsgn', np.full((128, 1), 64e-5, np.float32))
    P.add('one', np.ones((128, 1), np.float32))
    p = np.arange(128)
    hp, sp = p // 64, p % 64
    U = ((hp[:, None] == hp[None, :]) & (sp[None, :] > sp[:, None])).astype(np.float32)
    P.add('U', U)
    P.add('L', np.ascontiguousarray(U.T))
    t64 = np.arange(64)
    P.add('IU', (sp[:, None] <= t64[None, :]).astype(np.float32))
    P.add('IL', (sp[:, None] >= t64[None, :]).astype(np.float32))
    diffT = (p[None, :] - p[:, None]).astype(np.float32)
    P.add('diffT', diffT)
    P.add('maskF', (diffT >= 0).astype(np.float32))
    P.add('maskB', (diffT <= 0).astype(np.float32))
    P.add('irow', np.broadcast_to((p + 1.0).astype(np.float32)[None, :], (128, 128)))
    P.add('irowb', np.broadcast_to((128.0 - p).astype(np.float32)[None, :], (128, 128)))
    P.add('jcol', (127.0 - p).astype(np.float32).reshape(128, 1))
    P.add('jcolb', p.astype(np.float32).reshape(128, 1))
    P.add('c128', np.full((128, 1), 128.0, np.float32))
    assert [(k, v[1]) for k, v in P.cols.items()] == CST_LAYOUT
    return P


def rope_tables():
    rows = NLAT // 64
    row = np.repeat(np.arange(rows, dtype=np.float32), 64)
    col = np.tile(np.arange(64, dtype=np.float32), rows)
    inv = (10000.0 ** (-np.arange(0, 64, 2, dtype=np.float32) / 64)).astype(np.float32)
    ang = np.concatenate([row[:, None] * inv, col[:, None] * inv], axis=-1).astype(np.float32)
    cos, sin = np.cos(ang).astype(np.float32), np.sin(ang).astype(np.float32)
    C = np.zeros((128, NLAT), np.float32)
    S = np.zeros((128, NLAT), np.float32)
    for d in range(128):
        axis, r = d // 64, d % 64
        f = axis * 32 + (r % 32)
        C[d] = cos[:, f]
        S[d] = -sin[:, f] if r < 32 else sin[:, f]
    return np.stack([C, S])


WEIGHT_NAMES = ['mod_w', 'ffn_w_up', 'ffn_w_down', 'ab_w_in', 'ab_w_out', 'cd_w_in', 'rwkv_w2', 'rwkv_a2', 'rwkv_g2',
                'cd_w_out']
WEIGHT_SHAPES = {'mod_w': [2, D, 6 * D], 'ffn_w_up': [2, D, 2 * FFN], 'ffn_w_down': [2, FFN, D], 'ab_w_in': [1, D, 3584],
                 'ab_w_out': [1, D, D], 'cd_w_in': [1, D, 9504], 'rwkv_w2': [1, 2, 64, 1024], 'rwkv_a2': [1, 2, 64, 1024],
                 'rwkv_g2': [1, 160, 1024], 'cd_w_out': [1, 3072, D]}


KDEC = 0.6065306597126334


class MK1:
    def qbank(self):
        i = self.q_rr % 8
        self.q_rr += 1
        return self.banks[i][:, 0:128], self.qbufs[i]

    def phase_cd_proj(self, uT, buT):
        A = self.A
        w_in = self.W['cd_w_in'][0]
        zT = self.scratch('zT', [27, 128, 2306], F32)
        self.bz = [Buf('z%d' % i) for i in range(27)]
        rqk = self.scratch('rqk', [16, 128, TT], BF16)
        self.brqk = [Buf('rqk%d' % i) for i in range(16)]
        rvt = self.scratch('rvt', [18, 128, 2048], BF16)
        self.brv = Buf('rvt')
        rgs = self.scratch('rgs', [16, 128, NLAT], BF16)
        self.brg = [Buf('rg%d' % i) for i in range(16)]
        A.mark()
        self.wpool(2, 16 * 512)
        zpad = [A.alloc([2308], F32) for _ in range(2)]
        bzp = [Buf(), Buf()]
        zo = [A.alloc([2306], F32) for _ in range(2)]
        bzo = [Buf(), Buf()]
        c0 = A.alloc([27], F32)
        bc0 = Buf()
        mu = self.P('mu').rearrange('p (c j) -> p c j', j=2)
        for i in range(2):
            self.pool(lambda e, i=i: e.memset(zpad[i], 0.0), [], [bzp[i]])
        self.dve(lambda e: e.tensor_tensor(out=c0, in0=mu[:, :, 0], in1=mu[:, :, 1], op=ALU.add), [self.bpk], [bc0])
        self.dve(lambda e: e.tensor_scalar(out=c0, in0=c0, scalar1=-1.0, scalar2=1.0, op0=ALU.mult, op1=ALU.add), [bc0], [bc0])
        for g in range(7):
            ncol = 512 if g < 6 else 288
            wt, wb = self.wload(w_in[:, g * 512:g * 512 + ncol], 16, ncol)
            for c4 in range(4 if g < 6 else 3):
                c = g * 4 + c4
                m = 128 if c < 26 else 32
                zp, bzp_ = zpad[c % 2], bzp[c % 2]
                zo_, bzo_ = zo[c % 2], bzo[c % 2]
                for ti, (t0, tl) in enumerate(TILES):
                    off = 1 + t0 if ti < 4 else 2051
                    bk, bb = self.bank()
                    self.mm(bk[0:m, 0:tl], [(wt[:, kc, c4 * 128:c4 * 128 + m], uT[:, kc, t0:t0 + tl]) for kc in range(16)], [wb, buT], bb)
                    self.act(lambda e, bk=bk, zp=zp, m=m, off=off, tl=tl: e.activation(out=zp[0:m, off:off + tl], in_=bk[0:m, 0:tl], func=AF.Copy), [bb], [bzp_])
                self.dve(lambda e, zo_=zo_, zp=zp, m=m, c=c: e.tensor_scalar(out=zo_[0:m, :], in0=zp[0:m, 1:2307], scalar1=c0[0:m, c:c + 1],
                                                                           scalar2=None, op0=ALU.mult), [bzp_, bc0], [bzo_])
                self.dve(lambda e, zo_=zo_, zp=zp, m=m, c=c: e.scalar_tensor_tensor(out=zo_[0:m, :], in0=zp[0:m, 0:2306], scalar=mu[0:m, c, 0:1],
                                                                                  in1=zo_[0:m, :], op0=ALU.mult, op1=ALU.add), [bzp_, self.bpk, bzo_], [bzo_])
                self.dve(lambda e, zo_=zo_, zp=zp, m=m, c=c: e.scalar_tensor_tensor(out=zo_[0:m, :], in0=zp[0:m, 2:2308], scalar=mu[0:m, c, 1:2],
                                                                                  in1=zo_[0:m, :], op0=ALU.mult, op1=ALU.add), [bzp_, self.bpk, bzo_], [bzo_])
                self.dma_sp(zT[c][0:m], zo_[0:m, :], [bzo_], [self.bz[c]])
        A.release()
        self.S.barrier()
        A.mark()
        rope = A.alloc([2, NLAT], F32)
        brope = Buf()
        self.dma_sp(rope, self.roped.rearrange('a p t -> p a t'), [], [brope])
        self.wpool(2, 16 * 256)
        ws_ap = [A.alloc([16 * 256], BF16) for _ in range(2)]
        bws = [Buf(), Buf()]
        t1 = [A.alloc([512], F32) for _ in range(2)]
        bt1 = [Buf(), Buf()]
        t2 = [A.alloc([512], F32) for _ in range(2)]
        bt2 = [Buf(), Buf()]
        row = [A.alloc([TT], BF16) for _ in range(2)]
        brow = [Buf(), Buf()]
        k = 0
        for g in range(8):
            col0 = 3360 + g * 256
            wt, wb = self.wload(w_in[:, col0:col0 + 256], 16, 256)
            ws = ws_ap[g % 2].rearrange('p (c n) -> p c n', n=256)
            bw_ = bws[g % 2]
            wtv = wt.rearrange('p c (g b e) -> p (c g) b e', b=2, e=32)
            wsv = ws.rearrange('p c (g b e) -> p (c g) b e', b=2, e=32)
            for b in range(2):
                self.pool(lambda e, wsv=wsv, wtv=wtv, b=b: e.tensor_copy(out=wsv[:, :, b, :], in_=wtv[:, :, 1 - b, :]), [wb], [bw_])
            for hh in range(2):
                hd = g * 2 + hh
                rw, brw = row[hd % 2], brow[hd % 2]
                tiles = TILES[:4] if hd < 8 else TILES
                for ti, (t0, tl) in enumerate(tiles):
                    i2 = k % 2
                    k += 1
                    bkq, bbq = self.bank()
                    self.mm(bkq[:, 0:tl], [(wt[:, c, hh * 128:(hh + 1) * 128], uT[:, c, t0:t0 + tl]) for c in range(16)], [wb, buT], bbq)
                    if ti < 4:
                        bkw, bbw = self.bank()
                        self.mm(bkw[:, 0:tl], [(ws[:, c, hh * 128:(hh + 1) * 128], uT[:, c, t0:t0 + tl]) for c in range(16)], [bw_, buT], bbw)
                        self.dve(lambda e, bkq=bkq, i2=i2, t0=t0, tl=tl: e.tensor_tensor(out=t1[i2][:, 0:tl], in0=bkq[:, 0:tl], in1=rope[:, 0, t0:t0 + tl],
                                                                                      op=ALU.mult), [bbq, brope], [bt1[i2]])
                        self.dve(lambda e, bkw=bkw, i2=i2, t0=t0, tl=tl: e.tensor_tensor(out=t2[i2][:, 0:tl], in0=bkw[:, 0:tl], in1=rope[:, 1, t0:t0 + tl],
                                                                                      op=ALU.mult), [bbw, brope], [bt2[i2]])
                        self.pool(lambda e, i2=i2, rw=rw, t0=t0, tl=tl: e.tensor_tensor(out=rw[:, t0:t0 + tl], in0=t1[i2][:, 0:tl], in1=t2[i2][:, 0:tl],
                                                                                     op=ALU.add), [bt1[i2], bt2[i2]], [brw])
                    else:
                        self.act(lambda e, bkq=bkq, rw=rw, t0=t0, tl=tl: e.activation(out=rw[:, t0:t0 + tl], in_=bkq[:, 0:tl], func=AF.Copy), [bbq], [brw])
                self.dma_sp(rqk[hd], rw, [brw], [self.brqk[hd]])
        A.release()
        self.S.barrier()
        A.mark()
        wv = A.alloc([16, 2048], BF16)
        bwv = Buf()
        for n4 in range(4):
            self.dma_cast(wv[:, :, n4 * 512:(n4 + 1) * 512], w_in[:, 5408 + n4 * 512:5408 + (n4 + 1) * 512].rearrange('(c p) n -> p c n', p=128), [], [bwv])
        vt = [A.alloc([2048], BF16) for _ in range(2)]
        bvt = [Buf(), Buf()]
        for tb in range(18):
            v_, bv_ = vt[tb % 2], bvt[tb % 2]
            for n4 in range(4):
                bk, bb = self.bank()
                self.mm(bk, [(uT[:, c, tb * 128:(tb + 1) * 128], wv[:, c, n4 * 512:(n4 + 1) * 512]) for c in range(16)], [bwv, buT], bb)
                if n4 % 2 == 0:
                    self.act(lambda e, bk=bk, v_=v_, n4=n4: e.activation(out=v_[:, n4 * 512:(n4 + 1) * 512], in_=bk, func=AF.Copy), [bb], [bv_])
                else:
                    self.dve(lambda e, bk=bk, v_=v_, n4=n4: e.tensor_copy(out=v_[:, n4 * 512:(n4 + 1) * 512], in_=bk), [bb], [bv_])
            self.dma_sp(rvt[tb], v_, [bv_], [self.brv])
        A.release()
        self.S.barrier()
        A.mark()
        self.wpool(2, 16 * 512)
        row = [A.alloc([NLAT], BF16) for _ in range(2)]
        brow = [Buf(), Buf()]
        for g in range(4):
            wt, wb = self.wload(w_in[:, 7456 + g * 512:7456 + (g + 1) * 512], 16, 512)
            for c4 in range(4):
                c = g * 4 + c4
                rw, brw = row[c % 2], brow[c % 2]
                for ti, (t0, tl) in enumerate(TILES[:4]):
                    bk, bb = self.bank()
                    self.mm(bk, [(wt[:, kc, c4 * 128:(c4 + 1) * 128], uT[:, kc, t0:t0 + tl]) for kc in range(16)], [wb, buT], bb)
                    self.act(lambda e, bk=bk, rw=rw, t0=t0, tl=tl: e.activation(out=rw[:, t0:t0 + tl], in_=bk, func=AF.Silu), [bb], [brw])
                self.dma_sp(rgs[c], rw, [brw], [self.brg[c]])
        A.release()
        self.S.barrier()

    def phase_rwkv(self):
        A = self.A
        A.mark()
        zT = self.scr['zT']
        yin_s = self.scratch('yin', [24, 128, NLAT], BF16)
        self.byin_s = [Buf('yin%d' % i) for i in range(24)]
        self.q_rr = 0
        self.qbufs = [Buf('q%d' % i) for i in range(32)]
        cU, cL, cIU, cIL = self.C('U'), self.C('L'), self.C('IU'), self.C('IL')
        bd3 = self.C('bones', True).rearrange('p (h c) -> p h c', c=64)
        bonesb, bonesf = self.C('bones', True), self.C('bones')
        identb = self.C('ident', True)
        identf = self.C('ident')
        F = lambda: A.alloc([TT], F32)
        H = lambda: A.alloc([TT], BF16)
        dwa, sg25, sg26 = H(), H(), H()
        bsh = Buf('shared')
        A.mark()
        z32 = A.alloc([2306], F32)
        bz32 = Buf()
        for (c, dst, m0, m1, fn) in ((24, dwa, 0, 64, AF.Tanh), (24, dwa, 64, 128, AF.Copy), (25, sg25, 0, 128, AF.Sigmoid), (26, sg26, 0, 32, AF.Sigmoid)):
            if m0 == 0:
                self.dma_sp(z32[0:(32 if c == 26 else 128), :], zT[c][0:(32 if c == 26 else 128)], [self.bz[c]], [bz32])
            self.act(lambda e, dst=dst, m0=m0, m1=m1, fn=fn: e.activation(out=dst[m0:m1, 0:NLAT], in_=z32[m0:m1, 0:NLAT], func=fn), [bz32], [bsh])
            self.act(lambda e, dst=dst, m0=m0, m1=m1, fn=fn: e.activation(out=dst[m0:m1, NLAT:TT], in_=z32[m0:m1, 2050:2306], func=fn), [bz32], [bsh])
        A.release()
        self.S.barrier()
        w2a2 = A.alloc([2, 1024], BF16)
        g2w = A.alloc([2, 1024], BF16)
        self.dma_cast(w2a2[0:64], self.W['rwkv_w2'][0].rearrange('d k n -> k d n'), [], [bsh])
        self.dma_cast(w2a2[64:128], self.W['rwkv_a2'][0].rearrange('d k n -> k d n'), [], [bsh])
        self.dma_cast(g2w[:, 0, :], self.W['rwkv_g2'][0][0:128, :], [], [bsh])
        self.dma_cast(g2w[0:32, 1, :], self.W['rwkv_g2'][0][128:160, :], [], [bsh])
        seg = F()
        self.pool(lambda e: e.memset(seg, 1.0), [], [bsh])
        self.pool(lambda e: e.memset(seg.rearrange('p (n c) -> p n c', c=64)[:, :, 0:1], 0.0), [bsh], [bsh])
        r_, k_, v_, kk_ = F(), F(), F(), F()
        T = [F() for _ in range(6)]
        vb, rkr = H(), H()
        KKg, Bg, Kg, Rg = H(), H(), H(), H()
        wkv = A.alloc([NLAT], F32)
        bonus = A.alloc([NLAT], F32)
        yrow = [A.alloc([NLAT], BF16)] * 2
        byrow = [Buf()] * 2
        bpair, bdir, bwkv, bbon = Buf('pair'), Buf('dir'), Buf('wkv'), Buf('bonus')
        NSET = 3
        def tset():
            d = {}
            for nm in ('KKb', 'Bgb', 'Kgb', 'vTb', 'X', 'XT', 'BT', 'TTa', 'TTb', 'Ya', 'Yb', 'YTa', 'YTb', 'BgTb', 'KgTb', 'vb', 'ub'):
                d[nm] = (A.alloc([128], BF16), Buf(nm))
            for nm in ('MrbT', 'MrkT', 'vst', 'nZ', 'ust'):
                d[nm] = (A.alloc([64], BF16), Buf(nm))
            return d
        sets = [tset() for _ in range(NSET)]
        for d_ in sets:
            for nm in ('KKb', 'Bgb', 'Kgb', 'vTb', 'ub'):
                ap_, bf_ = d_[nm]
                self.pool(lambda e, ap_=ap_: e.memset(ap_, 0.0), [], [bf_])
        Pst = [(A.alloc([64], F32), A.alloc([64], BF16), A.alloc([128], BF16), Buf('P%d' % i)) for i in range(2)]
        ptmp = A.alloc([64], F32)
        bptmp = Buf()
        sm = [A.alloc([512], F32) for _ in range(4)]
        bsm = [Buf() for _ in range(4)]
        w0 = self.P('w0').rearrange('p (d c) -> p d c', c=8)
        a0 = self.P('a0').rearrange('p (d c) -> p d c', c=8)
        kkp, kap, rkp, lng, lnb = self.P('kk'), self.P('ka'), self.P('rk'), self.P('lng'), self.P('lnb')
        LT = TILES[:4]
        un = 0
        for pc in range(getattr(self, 'rw_pairs', 8)):
            for (dst, c) in ((r_, pc), (k_, 8 + pc), (v_, 16 + pc)):
                self.dma_sp(dst[:, 0:NLAT], zT[c][:, 0:NLAT], [self.bz[c]], [bpair])
                self.dma_sp(dst[:, NLAT:TT], zT[c][:, 2050:2306], [self.bz[c]], [bpair])
            self.act(lambda e: e.activation(out=vb, in_=v_, func=AF.Copy), [bpair], [bpair])
            self.dve(lambda e, pc=pc: e.tensor_scalar(out=kk_, in0=k_, scalar1=kkp[:, pc:pc + 1], scalar2=None, op0=ALU.mult), [bpair, self.bpk], [bpair])
            self.act(lambda e: e.activation(out=T[0], in_=kk_, func=AF.Square), [bpair], [bdir])
            for ti, (t0, tl) in enumerate(TILES):
                bk, bb = self.qbank4()
                self.mm(bk[:, 0:tl], [(bonesf, T[0][:, t0:t0 + tl])], [bdir, self.bcst], bb)
                self.act(lambda e, bk=bk, t0=t0, tl=tl: e.activation(out=T[1][:, t0:t0 + tl], in_=bk[:, 0:tl], func=AF.Sqrt, bias=self.C('eps12'), scale=1.0),
                         [bb, self.bcst], [bdir])
            self.dve(lambda e: e.reciprocal(out=T[1], in_=T[1]), [bdir], [bdir])
            self.pool(lambda e: e.tensor_tensor(out=kk_, in0=kk_, in1=T[1], op=ALU.mult), [bdir, bpair], [bpair])
            for d in range(getattr(self, 'rw_dirs', 2)):
                sig, a_, cs, alt, gi, gneg = T[0], T[1], T[2], T[3], T[4], T[5]
                for ti, (t0, tl) in enumerate(TILES):
                    bk, bb = self.qbank4()
                    self.mm(bk[:, 0:tl], [(w2a2[0:64, d, pc * 128:(pc + 1) * 128], dwa[0:64, t0:t0 + tl])], [bsh], bb)
                    self.act(lambda e, bk=bk, t0=t0, tl=tl, d=d, pc=pc: e.activation(out=sig[:, t0:t0 + tl], in_=bk[:, 0:tl], func=AF.Sigmoid,
                                                                               bias=w0[:, d, pc:pc + 1], scale=1.0), [bb, self.bpk], [bdir])
                    bk2, bb2 = self.qbank4()
                    self.mm(bk2[:, 0:tl], [(w2a2[64:128, d, pc * 128:(pc + 1) * 128], dwa[64:128, t0:t0 + tl])], [bsh], bb2)
                    self.act(lambda e, bk2=bk2, t0=t0, tl=tl, d=d, pc=pc: e.activation(out=a_[:, t0:t0 + tl], in_=bk2[:, 0:tl], func=AF.Sigmoid,
                                                                                 bias=a0[:, d, pc:pc + 1], scale=1.0), [bb2, self.bpk], [bdir])
                self.dve(lambda e, cs=cs, sig=sig: e.tensor_tensor_scan(out=cs, data0=seg, data1=sig, initial=0.0, op0=ALU.mult, op1=ALU.add), [bdir, bsh], [bdir])
                if d == 1:
                    cs3 = cs.rearrange('p (n c) -> p n c', c=64)
                    self.dve(lambda e, alt=alt, sig=sig, cs=cs: e.tensor_tensor(out=alt, in0=sig, in1=cs, op=ALU.subtract), [bdir], [bdir])
                    self.dve(lambda e, cs3=cs3, alt=alt: e.tensor_tensor(out=alt.rearrange('p (n c) -> p n c', c=64), in0=alt.rearrange('p (n c) -> p n c', c=64),
                                                               in1=cs3[:, :, 63:64].broadcast_to([128, 36, 64]), op=ALU.add), [bdir], [bdir])
                    cs, alt = alt, cs
                self.act(lambda e, cs=cs: e.activation(out=gi, in_=cs, func=AF.Exp, scale=-KDEC), [bdir], [bdir])
                self.act(lambda e, cs=cs: e.activation(out=gneg, in_=cs, func=AF.Exp, scale=KDEC), [bdir], [bdir])
                self.dve(lambda e, cs=cs: e.tensor_tensor(out=sig, in0=cs, in1=sig, op=ALU.subtract), [bdir], [bdir])
                self.act(lambda e: e.activation(out=sig, in_=sig, func=AF.Exp, scale=-KDEC), [bdir], [bdir])
                self.pool(lambda e: e.tensor_tensor(out=KKg, in0=kk_, in1=sig, op=ALU.mult), [bdir, bpair], [bdir])
                self.pool(lambda e, alt=alt: e.tensor_tensor(out=alt, in0=kk_, in1=a_, op=ALU.mult), [bdir, bpair], [bdir])
                self.pool(lambda e, alt=alt: e.tensor_tensor(out=Bg, in0=alt, in1=gneg, op=ALU.mult), [bdir], [bdir])
                self.dve(lambda e, pc=pc: e.tensor_scalar(out=a_, in0=a_, scalar1=-1.0, scalar2=kap[:, pc:pc + 1], op0=ALU.add, op1=ALU.mult),
                         [bdir, self.bpk], [bdir])
                self.dve(lambda e: e.scalar_tensor_tensor(out=a_, in0=a_, scalar=1.0, in1=k_, op0=ALU.add, op1=ALU.mult), [bdir, bpair], [bdir])
                self.pool(lambda e: e.tensor_tensor(out=Kg, in0=a_, in1=gneg, op=ALU.mult), [bdir], [bdir])
                self.pool(lambda e: e.tensor_tensor(out=Rg, in0=r_, in1=gi, op=ALU.mult), [bdir, bpair], [bdir])
                self.dve(lambda e, pc=pc: e.scalar_tensor_tensor(out=rkr, in0=r_, scalar=rkp[:, pc:pc + 1], in1=a_, op0=ALU.mult, op1=ALU.mult),
                         [bdir, bpair, self.bpk], [bdir])
                for ti, (t0, tl) in enumerate(LT):
                    bk, bb = self.qbank4()
                    self.mm(bk[:, 0:tl], [(bonesb, rkr[:, t0:t0 + tl])], [bdir, self.bcst], bb)
                    if d == 0:
                        self.dve(lambda e, bk=bk, t0=t0, tl=tl: e.tensor_tensor(out=bonus[:, t0:t0 + tl], in0=bk[:, 0:tl], in1=v_[:, t0:t0 + tl], op=ALU.mult),
                                 [bb, bpair], [bbon])
                    else:
                        s_, bs_ = sm[ti % 4], bsm[ti % 4]
                        self.dve(lambda e, bk=bk, s_=s_, t0=t0, tl=tl: e.tensor_tensor(out=s_[:, 0:tl], in0=bk[:, 0:tl], in1=v_[:, t0:t0 + tl], op=ALU.mult),
                                 [bb, bpair], [bs_])
                        self.pool(lambda e, s_=s_, t0=t0, tl=tl: e.tensor_tensor(out=bonus[:, t0:t0 + tl], in0=bonus[:, t0:t0 + tl], in1=s_[:, 0:tl], op=ALU.add),
                                  [bs_, bbon], [bbon])
                Pf, Pb, Pbd, bP = Pst[d]
                self.pool(lambda e, Pf=Pf: e.memset(Pf, 0.0), [], [bP])
                self.pool(lambda e, Pb=Pb: e.memset(Pb, 0.0), [bP], [bP])
                self.pool(lambda e, Pbd=Pbd: e.memset(Pbd, 0.0), [bP], [bP])
                order = [32, 33, 34, 35] + list(range(32)) if d == 0 else [35, 34, 33, 32] + list(range(31, -1, -1))
                mS, mSn, mI = (cU, cL, cIU) if d == 0 else (cL, cU, cIL)
                order = order[:getattr(self, 'rw_chunks', 36)]
                self.rw_level = getattr(self, 'rw_level', 9)
                for n in order:
                    ts_ = sets[un % NSET]
                    un += 1
                    cs_ = slice(n * 64, (n + 1) * 64)
                    lat = n < 32
                    gcol = gi[:, n * 64 + 63:n * 64 + 64] if d == 0 else gi[:, n * 64:n * 64 + 1]
                    def bdexp(name, src):
                        ap, bf = ts_[name]
                        for h2 in range(2):
                            self.pool(lambda e, ap=ap, src=src, h2=h2: e.tensor_copy(out=ap[h2 * 64:(h2 + 1) * 64, h2 * 64:(h2 + 1) * 64],
                                                                                   in_=src[h2 * 64:(h2 + 1) * 64, :]), [bdir, bpair], [bf])
                        return ap, bf
                    KKb, bKKb = bdexp('KKb', KKg[:, cs_])
                    Bgb, bBgb = bdexp('Bgb', Bg[:, cs_])
                    Kgb, bKgb = bdexp('Kgb', Kg[:, cs_])
                    vTb, bvTb = bdexp('vTb', vb[:, cs_])
                    if self.rw_level <= 1:
                        continue
                    def mmq(pairs, reads):
                        q, bq_ = self.qbank()
                        self.mm(q, pairs, reads, bq_)
                        return q, bq_
                    def evac_mask(name, q, bq_, mask, neg):
                        ap, bf = ts_[name]
                        if neg:
                            self.dve(lambda e, ap=ap, q=q, mask=mask: e.scalar_tensor_tensor(out=ap, in0=q, scalar=-1.0, in1=mask, op0=ALU.mult, op1=ALU.mult),
                                     [bq_, self.bcst], [bf])
                        else:
                            self.dve(lambda e, ap=ap, q=q, mask=mask: e.tensor_tensor(out=ap, in0=q, in1=mask, op=ALU.mult), [bq_, self.bcst], [bf])
                        return ap, bf
                    q, bq_ = mmq([(Bgb, KKb)], [bBgb, bKKb])
                    XT, bXT = evac_mask('XT', q, bq_, mS, True)
                    TTc, bTT = ts_['TTa']
                    self.dve(lambda e, TTc=TTc, XT=XT: e.tensor_tensor(out=TTc, in0=XT, in1=identb, op=ALU.add), [bXT, self.bcst], [bTT])
                    q, bq_ = mmq([(KKb, Bgb)], [bBgb, bKKb])
                    X, bX = evac_mask('X', q, bq_, mSn, True)
                    q, bq_ = mmq([(Kgb, KKb)], [bKgb, bKKb])
                    BT, bBT = evac_mask('BT', q, bq_, mS, False)
                    q, bq_ = self.qbank()
                    self.mm(q[:, 0:64], [(Bgb, Rg[:, cs_])], [bBgb, bdir], bq_)
                    ap, bMrb = ts_['MrbT']
                    self.dve(lambda e, ap=ap, q=q, mI=mI: e.tensor_tensor(out=ap, in0=q[:, 0:64], in1=mI, op=ALU.mult), [bq_, self.bcst], [bMrb])
                    MrbT = ap
                    q, bq_ = self.qbank()
                    self.mm(q[:, 0:64], [(Kgb, Rg[:, cs_])], [bKgb, bdir], bq_)
                    ap, bMrk = ts_['MrkT']
                    self.dve(lambda e, ap=ap, q=q, mI=mI: e.tensor_tensor(out=ap, in0=q[:, 0:64], in1=mI, op=ALU.mult), [bq_, self.bcst], [bMrk])
                    MrkT = ap
                    if self.rw_level <= 2:
                        continue
                    def tr(name, src, bsrc):
                        q, bq_ = self.qbank()
                        qb = q.bitcast(BF16)[:, 0:128]
                        self.pe(lambda e, qb=qb, src=src: e.transpose(out=qb, in_=src, identity=identb), [bsrc, self.bcst], [bq_])
                        ap, bf = ts_[name]
                        self.act(lambda e, ap=ap, qb=qb: e.activation(out=ap, in_=qb, func=AF.Copy), [bq_], [bf])
                        return ap, bf, qb, bq_
                    BgTb, bBgT, _, _ = tr('BgTb', Bgb, bBgb)
                    KgTb, bKgT, _, _ = tr('KgTb', Kgb, bKgb)
                    vbd, bvbd, _, _ = tr('vb', vTb, bvTb)
                    vst, bvst = ts_['vst']
                    self.pool(lambda e, vst=vst, vbd=vbd: e.tensor_tensor(out=vst, in0=vbd[:, 0:64], in1=vbd[:, 64:128], op=ALU.add), [bvbd], [bvst])
                    if self.rw_level <= 3:
                        continue
                    Y, bY, YT, bYT = X, bX, XT, bXT
                    for lev in range(5):
                        nm = 'a' if lev % 2 == 0 else 'b'
                        q, bq_ = mmq([(YT, Y)], [bYT, bY])
                        Y2, bY2 = ts_['Y' + nm]
                        self.act(lambda e, Y2=Y2, q=q: e.activation(out=Y2, in_=q, func=AF.Copy), [bq_], [bY2])
                        if lev < 4:
                            q, bq_ = mmq([(Y, YT)], [bYT, bY])
                            YT2, bYT2 = ts_['YT' + nm]
                            self.act(lambda e, YT2=YT2, q=q: e.activation(out=YT2, in_=q, func=AF.Copy), [bq_], [bYT2])
                        q, bq_ = mmq([(Y2, TTc)], [bY2, bTT])
                        TTn, bTTn = ts_['TTb' if lev % 2 == 0 else 'TTa']
                        self.dve(lambda e, TTn=TTn, q=q, TTc=TTc: e.tensor_tensor(out=TTn, in0=q, in1=TTc, op=ALU.add), [bq_, bTT], [bTTn])
                        TTc, bTT = TTn, bTTn
                        Y, bY = Y2, bY2
                        if lev < 4:
                            YT, bYT = YT2, bYT2
                    if self.rw_level <= 4:
                        continue
                    q, bq_ = self.qbank()
                    self.mm(q[:, 0:64], [(KKb, Pb), (BT, vst)], [bKKb, bP, bBT, bvst], bq_)
                    nZ, bnZ = ts_['nZ']
                    self.act(lambda e, nZ=nZ, q=q: e.activation(out=nZ, in_=q[:, 0:64], func=AF.Copy, scale=-1.0), [bq_], [bnZ])
                    qu, bqu = self.qbank()
                    self.mm(qu[:, 0:64], [(TTc, nZ)], [bTT, bnZ], bqu)
                    ust, bust = ts_['ust']
                    self.act(lambda e, ust=ust, qu=qu: e.activation(out=ust, in_=qu[:, 0:64], func=AF.Copy), [bqu], [bust])
                    if lat:
                        ub, bub = ts_['ub']
                        self.act(lambda e, ub=ub, qu=qu: e.activation(out=ub[0:64, 0:64], in_=qu[0:64, 0:64], func=AF.Copy), [bqu], [bub])
                        self.act(lambda e, ub=ub, qu=qu: e.activation(out=ub[64:128, 64:128], in_=qu[64:128, 0:64], func=AF.Copy), [bqu], [bub])
                        qy, bqy = self.qbank()
                        self.mm(qy[:, 0:64], [(Pbd, Rg[:, cs_]), (ub, MrbT), (vbd, MrkT)], [bP, bdir, bub, bMrb, bvbd, bMrk], bqy)
                        if d == 0:
                            self.act(lambda e, qy=qy, cs_=cs_: e.activation(out=wkv[:, cs_], in_=qy[:, 0:64], func=AF.Copy), [bqy], [bwkv])
                        else:
                            self.dve(lambda e, qy=qy, cs_=cs_: e.tensor_tensor(out=wkv[:, cs_], in0=qy[:, 0:64], in1=wkv[:, cs_], op=ALU.add), [bqy, bwkv], [bwkv])
                    qd, bqd = self.qbank()
                    self.mm(qd[:, 0:64], [(BgTb, ust), (KgTb, vst)], [bBgT, bust, bKgT, bvst], bqd)
                    self.dve(lambda e, qd=qd, Pf=Pf: e.tensor_tensor(out=ptmp, in0=qd[:, 0:64], in1=Pf, op=ALU.add), [bqd, bP], [bptmp])
                    self.dve(lambda e, Pf=Pf, gcol=gcol: e.tensor_scalar(out=Pf, in0=ptmp, scalar1=gcol, scalar2=None, op0=ALU.mult), [bptmp, bdir], [bP])
                    self.act(lambda e, Pf=Pf, Pb=Pb: e.activation(out=Pb, in_=Pf, func=AF.Copy), [bP], [bP])
                    for h2 in range(2):
                        self.pool(lambda e, Pf=Pf, Pbd=Pbd, h2=h2: e.tensor_copy(out=Pbd[h2 * 64:(h2 + 1) * 64, h2 * 64:(h2 + 1) * 64],
                                                                               in_=Pf[h2 * 64:(h2 + 1) * 64, :]), [bP], [bP])
            if not getattr(self, 'rw_gn', True):
                continue
            yr, byr = yrow[pc % 2], byrow[pc % 2]
            for ti, (t0, tl) in enumerate(LT):
                s0, s1, s2, s3 = sm
                b0_, b1_, b2_, b3_ = bsm
                bk1, bb1 = self.qbank4()
                self.mm(bk1[:, 0:tl], [(bonesf, wkv[:, t0:t0 + tl])], [bwkv, self.bcst], bb1)
                self.act(lambda e, t0=t0, tl=tl: e.activation(out=s0[:, 0:tl], in_=wkv[:, t0:t0 + tl], func=AF.Square), [bwkv], [b0_])
                bk2, bb2 = self.qbank4()
                self.mm(bk2[:, 0:tl], [(bonesf, s0[:, 0:tl])], [b0_, self.bcst], bb2)
                self.act(lambda e, bk1=bk1, tl=tl: e.activation(out=s1[:, 0:tl], in_=bk1[:, 0:tl], func=AF.Copy, scale=1.0 / 64), [bb1], [b1_])
                self.dve(lambda e, tl=tl: e.tensor_tensor(out=s2[:, 0:tl], in0=s1[:, 0:tl], in1=s1[:, 0:tl], op=ALU.mult), [b1_], [b2_])
                self.dve(lambda e, bk2=bk2, tl=tl: e.scalar_tensor_tensor(out=s2[:, 0:tl], in0=bk2[:, 0:tl], scalar=1.0 / 64, in1=s2[:, 0:tl],
                                                                          op0=ALU.mult, op1=ALU.subtract), [bb2, b2_], [b2_])
                self.act(lambda e, tl=tl: e.activation(out=s2[:, 0:tl], in_=s2[:, 0:tl], func=AF.Sqrt, bias=self.C('epsgn'), scale=1.0), [b2_, self.bcst], [b2_])
                self.dve(lambda e, tl=tl: e.reciprocal(out=s2[:, 0:tl], in_=s2[:, 0:tl]), [b2_], [b2_])
                self.dve(lambda e, t0=t0, tl=tl: e.tensor_tensor(out=s3[:, 0:tl], in0=wkv[:, t0:t0 + tl], in1=s1[:, 0:tl], op=ALU.subtract), [bwkv, b1_], [b3_])
                self.pool(lambda e, tl=tl: e.tensor_tensor(out=s3[:, 0:tl], in0=s3[:, 0:tl], in1=s2[:, 0:tl], op=ALU.mult), [b3_, b2_], [b3_])
                self.act(lambda e, tl=tl, pc=pc: e.activation(out=s3[:, 0:tl], in_=s3[:, 0:tl], func=AF.Identity, bias=lnb[:, pc:pc + 1], scale=lng[:, pc:pc + 1]),
                         [b3_, self.bpk], [b3_])
                self.pool(lambda e, t0=t0, tl=tl: e.tensor_tensor(out=s3[:, 0:tl], in0=s3[:, 0:tl], in1=bonus[:, t0:t0 + tl], op=ALU.add), [b3_, bbon], [b3_])
                bkg, bbg = self.qbank4()
                self.mm(bkg[:, 0:tl], [(g2w[:, 0, pc * 128:(pc + 1) * 128], sg25[:, t0:t0 + tl]), (g2w[0:32, 1, pc * 128:(pc + 1) * 128], sg26[0:32, t0:t0 + tl])],
                        [bsh], bbg)
                self.dve(lambda e, bkg=bkg, yr=yr, t0=t0, tl=tl: e.tensor_tensor(out=yr[:, t0:t0 + tl], in0=bkg[:, 0:tl], in1=s3[:, 0:tl], op=ALU.mult),
                         [bbg, b3_], [byr])
            self.dma_sp(yin_s[pc], yr, [byr], [self.byin_s[pc]])
        A.release()
        self.S.barrier()

    def qbank4(self):
        i = self.q_rr % 8
        self.q_rr += 1
        return self.banks[i][:, :], self.qbufs[i]

    def phase_ret(self):
        A = self.A
        A.mark()
        rqk, rvt, rgs, yin_s = self.scr['rqk'], self.scr['rvt'], self.scr['rgs'], self.scr['yin']
        identb = self.C('ident', True)
        onesf = self.C('ones')
        diffT, maskF, maskB = self.C('diffT'), self.C('maskF'), self.C('maskB')
        irow, irowb, jcol, jcolb, c128 = self.C('irow'), self.C('irowb'), self.C('jcol'), self.C('jcolb'), self.C('c128')
        gng, gnb = self.P('gng'), self.P('gnb')
        SC = 128.0 ** -0.5
        lg = A.alloc([16], F32)
        nlg = A.alloc([16], F32)
        blg = Buf()
        self.act(lambda e: e.activation(out=lg, in_=self.P('dlog'), func=AF.Exp, scale=-1.0), [self.bpk], [blg])
        self.act(lambda e: e.activation(out=lg, in_=lg, func=AF.Ln, bias=self.C('one'), scale=1.0), [blg, self.bcst], [blg])
        self.dve(lambda e: e.tensor_scalar(out=nlg, in0=lg, scalar1=1.0, scalar2=None, op0=ALU.mult), [blg], [blg])
        self.dve(lambda e: e.tensor_scalar(out=lg, in0=nlg, scalar1=-1.0, scalar2=None, op0=ALU.mult), [blg], [blg])
        qT, kT = A.alloc([TT], BF16), A.alloc([TT], BF16)
        vtok = A.alloc([18, 256], BF16)
        rg2 = A.alloc([2, NLAT], BF16)
        oacc = A.alloc([2, NLAT], F32)
        yrow = [A.alloc([NLAT], BF16) for _ in range(2)]
        byrow = [Buf(), Buf()]
        Dc = A.alloc([128], F32)
        e2 = A.alloc([128], F32)
        qdt = [A.alloc([128], F32) for _ in range(2)]
        kdc = A.alloc([4], F32)
        bhd, btab, boacc = Buf('head'), Buf('tab'), Buf('oacc')
        sm_ = [A.alloc([128], BF16) for _ in range(3)]
        bsm_ = [Buf() for _ in range(3)]
        qd_ = [A.alloc([128], BF16) for _ in range(3)]
        bqd_ = [Buf() for _ in range(3)]
        ktk = [[A.alloc([128], BF16) for _ in range(2)] for _ in range(18)]
        bktk = Buf('ktk')
        R32 = [A.alloc([256], F32) for _ in range(2)]
        Rbf = [A.alloc([256], BF16) for _ in range(2)]
        bR = [Buf('R0'), Buf('R1')]
        st = [A.alloc([512], F32) for _ in range(4)]
        bst = [Buf() for _ in range(4)]
        k3 = 0
        for h in range(8):
            self.dma_sp(qT[:, 0:NLAT], rqk[h][:, 0:NLAT], [self.brqk[h]], [bhd])
            self.dma_sp(kT, rqk[8 + h], [self.brqk[8 + h]], [bhd])
            self.dma_sp(vtok, rvt[:, :, h * 256:(h + 1) * 256].rearrange('b p f -> p b f'), [self.brv], [bhd])
            self.dma_sp(rg2, rgs[2 * h:2 * h + 2].rearrange('c p t -> p c t'), self.brg[2 * h:2 * h + 2], [bhd])
            lf, lb, nlb = lg[:, h:h + 1], lg[:, 8 + h:9 + h], nlg[:, 8 + h:9 + h]
            self.act(lambda e, lf=lf: e.activation(out=Dc, in_=diffT, func=AF.Exp, scale=lf), [blg, self.bcst], [btab])
            self.dve(lambda e: e.tensor_tensor(out=Dc, in0=Dc, in1=maskF, op=ALU.mult), [btab, self.bcst], [btab])
            self.act(lambda e, nlb=nlb: e.activation(out=e2, in_=diffT, func=AF.Exp, scale=nlb), [blg, self.bcst], [btab])
            self.dve(lambda e: e.tensor_tensor(out=e2, in0=e2, in1=maskB, op=ALU.mult), [btab, self.bcst], [btab])
            self.dve(lambda e: e.tensor_tensor(out=Dc, in0=Dc, in1=e2, op=ALU.add), [btab], [btab])
            self.dve(lambda e: e.tensor_scalar(out=Dc, in0=Dc, scalar1=SC, scalar2=None, op0=ALU.mult), [btab], [btab])
            self.act(lambda e, lf=lf: e.activation(out=qdt[0], in_=irow, func=AF.Exp, scale=lf), [blg, self.bcst], [btab])
            self.act(lambda e, lb=lb: e.activation(out=qdt[1], in_=irowb, func=AF.Exp, scale=lb), [blg, self.bcst], [btab])
            self.act(lambda e, lf=lf: e.activation(out=kdc[:, 0:1], in_=jcol, func=AF.Exp, scale=lf), [blg, self.bcst], [btab])
            self.act(lambda e, lb=lb: e.activation(out=kdc[:, 1:2], in_=jcolb, func=AF.Exp, scale=lb), [blg, self.bcst], [btab])
            self.act(lambda e, lf=lf: e.activation(out=kdc[:, 2:3], in_=c128, func=AF.Exp, scale=lf), [blg, self.bcst], [btab])
            self.act(lambda e, lb=lb: e.activation(out=kdc[:, 3:4], in_=c128, func=AF.Exp, scale=lb), [blg, self.bcst], [btab])
            self.dve(lambda e: e.tensor_scalar(out=kdc[:, 0:2], in0=kdc[:, 0:2], scalar1=SC, scalar2=None, op0=ALU.mult), [btab], [btab])
            for tb in range(18):
                q, bq_ = self.qbank()
                qb = q.bitcast(BF16)[:, 0:128]
                self.pe(lambda e, qb=qb, tb=tb: e.transpose(out=qb, in_=kT[:, tb * 128:(tb + 1) * 128], identity=identb), [bhd, self.bcst], [bq_])
                for d in range(2):
                    self.act(lambda e, qb=qb, tb=tb, d=d: e.activation(out=ktk[tb][d], in_=qb, func=AF.Copy, scale=kdc[:, d:d + 1]), [bq_, btab], [bktk])
            for tb in range(16):
                tsl = slice(tb * 128, (tb + 1) * 128)
                q, bq_ = self.qbank()
                self.mm(q, [(kT[:, tsl], qT[:, tsl])], [bhd], bq_)
                s_, bs_ = sm_[k3 % 3], bsm_[k3 % 3]
                k3 += 1
                self.dve(lambda e, s_=s_, q=q: e.tensor_tensor(out=s_, in0=q, in1=Dc, op=ALU.mult), [bq_, btab], [bs_])
                for vc in range(2):
                    q2, bq2 = self.qbank()
                    self.mm(q2, [(vtok[:, tb, vc * 128:(vc + 1) * 128], s_)], [bhd, bs_], bq2)
                    self.act(lambda e, q2=q2, vc=vc, tsl=tsl: e.activation(out=oacc[:, vc, tsl], in_=q2, func=AF.Copy), [bq2], [boacc])
            for d in range(2):
                self.pool(lambda e, d=d: e.memset(R32[d], 0.0), [], [bR[d]])
                self.pool(lambda e, d=d: e.memset(Rbf[d], 0.0), [bR[d]], [bR[d]])
                order = [16, 17] + list(range(16)) if d == 0 else [17, 16] + list(range(15, -1, -1))
                for tb in order:
                    tsl = slice(tb * 128, (tb + 1) * 128)
                    if tb < 16:
                        qd, bqd = qd_[k3 % 3], bqd_[k3 % 3]
                        k3 += 1
                        self.pool(lambda e, qd=qd, tsl=tsl, d=d: e.tensor_tensor(out=qd, in0=qT[:, tsl], in1=qdt[d], op=ALU.mult), [bhd, btab], [bqd])
                        for vc in range(2):
                            q2, bq2 = self.qbank()
                            self.mm(q2, [(Rbf[d][:, vc * 128:(vc + 1) * 128], qd)], [bR[d], bqd], bq2)
                            self.dve(lambda e, q2=q2, vc=vc, tsl=tsl: e.tensor_tensor(out=oacc[:, vc, tsl], in0=q2, in1=oacc[:, vc, tsl], op=ALU.add),
                                     [bq2, boacc], [boacc])
                    qr, bqr = self.qbank4()
                    self.mm(qr[:, 0:256], [(ktk[tb][d], vtok[:, tb, :])], [bktk, bhd], bqr)
                    self.dve(lambda e, qr=qr, d=d: e.scalar_tensor_tensor(out=R32[d], in0=R32[d], scalar=kdc[:, 2 + d:3 + d], in1=qr[:, 0:256],
                                                                          op0=ALU.mult, op1=ALU.add), [bqr, bR[d], btab], [bR[d]])
                    self.act(lambda e, d=d: e.activation(out=Rbf[d], in_=R32[d], func=AF.Copy), [bR[d]], [bR[d]])
            for ti, (t0, tl) in enumerate(TILES[:4]):
                s0, s1, s2, s3 = st
                b0_, b1_, b2_, b3_ = bst
                bk1, bb1 = self.qbank4()
                self.mm(bk1, [(onesf, oacc[:, vc, t0:t0 + tl]) for vc in range(2)], [boacc, self.bcst], bb1)
                bk2, bb2 = self.qbank4()
                for vc in range(2):
                    self.act(lambda e, vc=vc, t0=t0, tl=tl: e.activation(out=(s0 if vc == 0 else s3)[:, 0:tl], in_=oacc[:, vc, t0:t0 + tl], func=AF.Square),
                             [boacc], [b0_ if vc == 0 else b3_])
                self.mm(bk2, [(onesf, s0), (onesf, s3)], [b0_, b3_, self.bcst], bb2)
                self.act(lambda e, bk1=bk1: e.activation(out=s1, in_=bk1, func=AF.Copy, scale=1.0 / 256), [bb1], [b1_])
                self.dve(lambda e: e.tensor_tensor(out=s2, in0=s1, in1=s1, op=ALU.mult), [b1_], [b2_])
                self.dve(lambda e, bk2=bk2: e.scalar_tensor_tensor(out=s2, in0=bk2, scalar=1.0 / 256, in1=s2, op0=ALU.mult, op1=ALU.subtract), [bb2, b2_], [b2_])
                self.act(lambda e: e.activation(out=s2, in_=s2, func=AF.Sqrt, bias=self.C('eps6'), scale=1.0), [b2_, self.bcst], [b2_])
                self.dve(lambda e: e.reciprocal(out=s2, in_=s2), [b2_], [b2_])
                for vc in range(2):
                    c = 2 * h + vc
                    yr, byr = yrow[c % 2], byrow[c % 2]
                    self.dve(lambda e, vc=vc, t0=t0, tl=tl: e.tensor_tensor(out=s0, in0=oacc[:, vc, t0:t0 + tl], in1=s1, op=ALU.subtract), [boacc, b1_, b3_], [b0_])
                    self.pool(lambda e: e.tensor_tensor(out=s0, in0=s0, in1=s2, op=ALU.mult), [b0_, b2_], [b0_])
                    self.act(lambda e, c=c: e.activation(out=s0, in_=s0, func=AF.Identity, bias=gnb[:, c:c + 1], scale=gng[:, c:c + 1]), [b0_, self.bpk], [b0_])
                    self.pool(lambda e, yr=yr, vc=vc, t0=t0, tl=tl: e.tensor_tensor(out=yr[:, t0:t0 + tl], in0=s0, in1=rg2[:, vc, t0:t0 + tl], op=ALU.mult),
                              [b0_, bhd], [byr])
            for vc in range(2):
                c = 2 * h + vc
                self.dma_sp(yin_s[8 + c], yrow[c % 2], [byrow[c % 2]], [self.byin_s[8 + c]])
        A.release()
        self.S.barrier()


class MK(MK1):
    def __init__(self, debug=(), only=None):
        self.debug = set(debug)
        self.only = only
        nc = self.nc = bass.Bass("TRN2", target_bir_lowering=False)
        self.S = Sched()
        self.A = Arena(nc)
        din = lambda name, shape, dt=F32: nc.dram_tensor(name, shape, dt, kind="ExternalInput").ap()
        self.x = din('x', [NLAT, D])
        self.ctx = din('ctx', [NCTX, D])
        self.cc = din('cc', [128, 32])
        npk = sum(w for _, w in pack_layout())
        self.pkd = din('pk', [128, npk])
        ncst = sum(w for _, w in CST_LAYOUT)
        self.cstd = din('cst', [128, ncst])
        self.roped = din('rope', [2, 128, NLAT])
        self.W = {k: din(k, WEIGHT_SHAPES[k] if (only is None or k in only) else [1, 1, 1]) for k in WEIGHT_NAMES}
        if only is not None:
            self.u1T = din('u1T', [16, 128, TT])
        self.out = nc.dram_tensor('out', [NLAT, D], F32, kind="ExternalOutput").ap()
        self.bout = Buf('out')
        self.scr = {}
        self.banks = [nc.alloc_psum_tensor('ps%d' % i, [128, 512], F32) for i in range(8)]
        self.bbufs = [Buf('bank%d' % i) for i in range(8)]
        self.bank_rr = 0
        A = self.A
        self.pk = A.alloc([npk], F32)
        self.cst = A.alloc([ncst], F32)
        self.cstb = A.alloc([ncst], BF16)
        self.modv = A.alloc([2, 96, 2], F32)
        self.bpk, self.bcst, self.bmod = Buf('pk'), Buf('cst'), Buf('mod')
        self.pkc = {}
        o = 0
        for name, w in pack_layout():
            self.pkc[name] = (o, w)
            o += w
        self.cc_ = {}
        o = 0
        for name, w in CST_LAYOUT:
            self.cc_[name] = (o, w)
            o += w
        self.dma_sp(self.pk, self.pkd, [], [self.bpk])
        self.dma_sp(self.cst, self.cstd, [], [self.bcst])
        self.dve(lambda e: e.tensor_copy(out=self.cstb, in_=self.cst), [self.bcst], [self.bcst])

    def scratch(self, name, shape, dt):
        kind = "ExternalOutput" if name in self.debug else "Internal"
        t = self.nc.dram_tensor(name, shape, dt, kind=kind).ap()
        self.scr[name] = t
        return t

    def P(self, name):
        o, w = self.pkc[name]
        return self.pk[:, o:o + w]

    def C(self, name, bf=False):
        o, w = self.cc_[name]
        return (self.cstb if bf else self.cst)[:, o:o + w]

    def pe(self, fn, r, w):
        return self.S.op('pe', fn, r, w)

    def act(self, fn, r, w):
        return self.S.op('act', fn, r, w)

    def dve(self, fn, r, w):
        return self.S.op('dve', fn, r, w)

    def pool(self, fn, r, w):
        return self.S.op('pool', fn, r, w)

    def dma_sp(self, out, in_, r, w):
        return self.S.op('sp', lambda e: e.dma_start(out=out, in_=in_), r, w, dma=True)

    def dma_cast(self, out, in_, r, w):
        return self.S.op('pool', lambda e: e.dma_start(out=out, in_=in_), r, w, dma=True)

    def bank(self):
        i = self.bank_rr % 8
        self.bank_rr += 1
        return self.banks[i][:, :], self.bbufs[i]

    def mm(self, out, pairs, r, wbuf):
        n = len(pairs)
        for i, (l, rr) in enumerate(pairs):
            self.pe(lambda e, l=l, rr=rr, i=i: e.matmul(out, lhsT=l, rhs=rr, start=(i == 0), stop=(i == n - 1)), r, [wbuf])

    def wpool(self, nbuf, nelem):
        self.wb_aps = [self.A.alloc([nelem], BF16) for _ in range(nbuf)]
        self.wb_bufs = [Buf('w%d' % i) for i in range(nbuf)]
        self.wb_rr = 0

    def wload(self, src, kc, n, rows=128):
        i = self.wb_rr % len(self.wb_aps)
        self.wb_rr += 1
        ap = self.wb_aps[i][0:rows, 0:kc * n].rearrange('p (c n) -> p c n', n=n)
        b = self.wb_bufs[i]
        self.dma_cast(ap, src.rearrange('(c p) n -> p c n', p=rows), [], [b])
        return ap, b

    def phase_mod(self):
        A = self.A
        A.mark()
        c32 = A.alloc([32], F32)
        sT = A.alloc([16, 2], BF16)
        bc, bs = Buf(), Buf()
        self.dma_sp(c32, self.cc, [], [bc])
        self.act(lambda e: e.activation(out=sT, in_=c32.rearrange('p (c t) -> p c t', t=2), func=AF.Silu), [bc], [bs])
        self.wpool(3, 16 * 512)
        for L in range(2):
            bk, bb = self.bank()
            for g in range(24):
                wt, wb = self.wload(self.W['mod_w'][L][:, g * 512:(g + 1) * 512], 16, 512)
                for oc in range(4):
                    j = g * 4 + oc
                    self.mm(bk[:, 2 * j:2 * j + 2], [(wt[:, c, oc * 128:(oc + 1) * 128], sT[:, c, :]) for c in range(16)],
                            [wb, bs], bb)
            mv = self.modv[:, L]
            self.dve(lambda e, bk=bk, mv=mv, L=L: e.tensor_tensor(
                out=mv, in0=bk[:, 0:192].rearrange('p (a b) -> p a b', b=2),
                in1=self.P('modb%d' % L).unsqueeze(2).broadcast_to([128, 96, 2]), op=ALU.add), [bb, self.bpk], [self.bmod])
            for (lo, gname) in ((16, 'gmix%d' % L), (64, 'gffn%d' % L)):
                self.dve(lambda e, mv=mv, lo=lo, gname=gname: e.scalar_tensor_tensor(
                    out=mv[:, lo:lo + 16, :], in0=mv[:, lo:lo + 16, :], scalar=1.0,
                    in1=self.P(gname).unsqueeze(2).broadcast_to([128, 16, 2]), op0=ALU.add, op1=ALU.mult),
                    [self.bmod, self.bpk], [self.bmod])
        A.release()
        self.S.barrier()

    def mvec(self, L, idx, c, col):
        return self.modv[:, L, idx * 16 + c, col:col + 1]

    def phase_in(self):
        A = self.A
        A.mark()
        hT = self.scratch('hT', [16, 128, TT], F32)
        self.bhT = [Buf('hT%d' % c) for c in range(16)]
        xt = [A.alloc([4, D], F32) for _ in range(2)]
        bx = [Buf(), Buf()]
        ht = [A.alloc([16, 512], F32) for _ in range(2)]
        bh = [Buf(), Buf()]
        identf = self.C('ident')
        for ti, (t0, tl) in enumerate(TILES):
            nb = tl // 128
            xx, bxx = xt[ti % 2], bx[ti % 2]
            hh, bhh = ht[ti % 2], bh[ti % 2]
            src = self.x[t0:t0 + tl, :] if ti < 4 else self.ctx[:, :]
            self.dma_sp(xx[:, 0:nb, :], src.rearrange('(b p) f -> p b f', p=128), [], [bxx])
            for c in range(16):
                bk, bb = self.bank()
                for b in range(nb):
                    self.pe(lambda e, bk=bk, xx=xx, b=b, c=c: e.transpose(out=bk[:, b * 128:(b + 1) * 128],
                                                                           in_=xx[:, b, c * 128:(c + 1) * 128], identity=identf),
                            [bxx, self.bcst], [bb])
                if c % 2 == 0:
                    self.act(lambda e, bk=bk, hh=hh, c=c, tl=tl: e.activation(out=hh[:, c, 0:tl], in_=bk[:, 0:tl], func=AF.Copy), [bb], [bhh])
                else:
                    self.dve(lambda e, bk=bk, hh=hh, c=c, tl=tl: e.tensor_copy(out=hh[:, c, 0:tl], in_=bk[:, 0:tl]), [bb], [bhh])
            self.dma_sp(hT[:, :, t0:t0 + tl].rearrange('c p t -> p c t'), hh[:, :, 0:tl], [bhh], self.bhT)
        A.release()
        self.S.barrier()

    def phase_norm(self, L, which, uT, buT, tiles=TILES):
        A = self.A
        A.mark()
        hT = self.scr['hT']
        ht = [A.alloc([16, 512], F32) for _ in range(2)]
        bh = [Buf(), Buf()]
        sq = A.alloc([16, 512], BF16)
        bsq = Buf()
        rstd = A.alloc([512], F32)
        brs = Buf()
        tmp = [A.alloc([512], F32) for _ in range(3)]
        btmp = [Buf() for _ in range(3)]
        ones = self.C('ones', True)
        ia, ib = (1, 0) if which == 0 else (4, 3)
        k = 0
        for ti, (t0, tl) in enumerate(tiles):
            col = 0 if ti < 4 else 1
            hh, bhh = ht[ti % 2], bh[ti % 2]
            self.dma_sp(hh[:, :, 0:tl], hT[:, :, t0:t0 + tl].rearrange('c p t -> p c t'), self.bhT, [bhh])
            self.act(lambda e, hh=hh, tl=tl: e.activation(out=sq[:, :, 0:tl], in_=hh[:, :, 0:tl], func=AF.Square), [bhh], [bsq])
            bk, bb = self.bank()
            self.mm(bk[:, 0:tl], [(ones, sq[:, c, 0:tl]) for c in range(16)], [bsq, self.bcst], bb)
            self.act(lambda e, bk=bk, tl=tl: e.activation(out=rstd[:, 0:tl], in_=bk[:, 0:tl], func=AF.Sqrt,
                                                          bias=self.C('eps6'), scale=1.0 / D), [bb, self.bcst], [brs])
            self.dve(lambda e, tl=tl: e.reciprocal(out=rstd[:, 0:tl], in_=rstd[:, 0:tl]), [brs], [brs])
            for c in range(16):
                tm, btm = tmp[k % 3], btmp[k % 3]
                k += 1
                self.dve(lambda e, tm=tm, hh=hh, c=c, tl=tl, col=col: e.scalar_tensor_tensor(
                    out=tm[:, 0:tl], in0=hh[:, c, 0:tl], scalar=self.mvec(L, ia, c, col), in1=rstd[:, 0:tl],
                    op0=ALU.mult, op1=ALU.mult), [bhh, brs, self.bmod], [btm])
                self.act(lambda e, tm=tm, c=c, t0=t0, tl=tl, col=col: e.activation(
                    out=uT[:, c, t0:t0 + tl], in_=tm[:, 0:tl], func=AF.Identity, bias=self.mvec(L, ib, c, col), scale=1.0),
                    [btm, self.bmod], [buT])
        A.release()
        self.S.barrier()

    def phase_resid_proj(self, L, gidx, actT, bact, KC, w_dram, tiles=TILES):
        A = self.A
        A.mark()
        hT = self.scr['hT']
        self.wpool(2, KC * 512)
        hrow = [A.alloc([TT], F32) for _ in range(3)]
        bhr = [Buf() for _ in range(3)]
        for og in range(4):
            wt, wb = self.wload(w_dram[:, og * 512:(og + 1) * 512], KC, 512)
            for o4 in range(4):
                oc = og * 4 + o4
                hr, bh = hrow[oc % 3], bhr[oc % 3]
                self.dma_sp(hr, hT[oc], [self.bhT[oc]], [bh])
                for ti, (t0, tl) in enumerate(tiles):
                    col = 0 if ti < 4 else 1
                    bk, bb = self.bank()
                    self.mm(bk[:, 0:tl], [(wt[:, c, o4 * 128:(o4 + 1) * 128], actT[:, c, t0:t0 + tl]) for c in range(KC)],
                            [wb, bact], bb)
                    self.dve(lambda e, bk=bk, hr=hr, t0=t0, tl=tl, oc=oc, col=col: e.scalar_tensor_tensor(
                        out=hr[:, t0:t0 + tl], in0=bk[:, 0:tl], scalar=self.mvec(L, gidx, oc, col), in1=hr[:, t0:t0 + tl],
                        op0=ALU.mult, op1=ALU.add), [bb, bh, self.bmod], [bh])
                self.dma_sp(hT[oc], hr, [bh], [self.bhT[oc]])
        A.release()
        self.S.barrier()

    def phase_ffn_up(self, L, fT, bfT, tiles=TILES):
        A = self.A
        A.mark()
        hid = self.scr.get('hid')
        if hid is None:
            hid = self.scratch('hid', [44, 128, TT], BF16)
            self.bhid = [Buf('hid%d' % c) for c in range(44)]
        GP = 2308
        self.wpool(4, 16 * 512)
        gpad = [A.alloc([GP], BF16) for _ in range(2)]
        bgp = [Buf(), Buf()]
        vsb = [A.alloc([TT], BF16) for _ in range(2)]
        bvs = [Buf(), Buf()]
        dg = [A.alloc([3, 128], BF16) for _ in range(2)]
        bdg = [Buf(), Buf()]
        sg = [A.alloc([512], BF16) for _ in range(3)]
        bsg = [Buf() for _ in range(3)]
        hrow = [A.alloc([TT], BF16) for _ in range(3)]
        bhr = [Buf() for _ in range(3)]
        for i in range(2):
            self.pool(lambda e, i=i: e.memset(gpad[i], 0.0), [], [bgp[i]])
        wup = self.W['ffn_w_up'][L]
        fcw = self.P('fcw%d' % L).rearrange('p (c j) -> p c j', j=3)
        fcb = self.P('fcb%d' % L)
        identb = self.C('ident', True)
        k = 0
        def ld(g):
            return (self.wload(wup[:, g * 512:(g + 1) * 512], 16, 512),
                    self.wload(wup[:, FFN + g * 512:FFN + (g + 1) * 512], 16, 512))
        nxt = ld(0)
        for g in range(11):
            (wg, bwg), (wv, bwv) = nxt
            if g + 1 < 11:
                nxt = ld(g + 1)
            for c4 in range(4):
                c = g * 4 + c4
                gp, bg = gpad[c % 2], bgp[c % 2]
                vs, bv = vsb[c % 2], bvs[c % 2]
                dd, bd = dg[c % 2], bdg[c % 2]
                hr, bh = hrow[c % 3], bhr[c % 3]
                for j in range(3):
                    self.pool(lambda e, dd=dd, j=j, c=c: e.tensor_scalar(out=dd[:, j, :], in0=identb, scalar1=fcw[:, c, j:j + 1],
                                                                        scalar2=None, op0=ALU.mult), [self.bcst, self.bpk], [bd])
                for ti, (t0, tl) in enumerate(tiles):
                    off = 1 + t0 if ti < 4 else 2051
                    bk, bb = self.bank()
                    self.mm(bk[:, 0:tl], [(wg[:, kc, c4 * 128:(c4 + 1) * 128], fT[:, kc, t0:t0 + tl]) for kc in range(16)], [bwg, bfT], bb)
                    self.act(lambda e, bk=bk, gp=gp, off=off, tl=tl: e.activation(out=gp[:, off:off + tl], in_=bk[:, 0:tl], func=AF.Copy), [bb], [bg])
                    bk2, bb2 = self.bank()
                    self.mm(bk2[:, 0:tl], [(wv[:, kc, c4 * 128:(c4 + 1) * 128], fT[:, kc, t0:t0 + tl]) for kc in range(16)], [bwv, bfT], bb2)
                    self.dve(lambda e, bk2=bk2, vs=vs, t0=t0, tl=tl: e.tensor_copy(out=vs[:, t0:t0 + tl], in_=bk2[:, 0:tl]), [bb2], [bv])
                for ti, (t0, tl) in enumerate(tiles):
                    base = t0 if ti < 4 else 2050
                    bk, bb = self.bank()
                    self.mm(bk[:, 0:tl], [(dd[:, j, :], gp[:, base + j:base + j + tl]) for j in range(3)], [bd, bg], bb)
                    s_, bs_ = sg[k % 3], bsg[k % 3]
                    k += 1
                    self.act(lambda e, bk=bk, s_=s_, tl=tl, c=c: e.activation(out=s_[:, 0:tl], in_=bk[:, 0:tl], func=AF.Silu,
                                                                             bias=fcb[:, c:c + 1], scale=1.0), [bb, self.bpk], [bs_])
                    self.pool(lambda e, s_=s_, hr=hr, vs=vs, t0=t0, tl=tl: e.tensor_tensor(out=hr[:, t0:t0 + tl], in0=s_[:, 0:tl],
                                                                                         in1=vs[:, t0:t0 + tl], op=ALU.mult), [bs_, bv], [bh])
                self.dma_sp(hid[c], hr, [bh], [self.bhid[c]])
        A.release()
        self.S.barrier()

    def phase_ffn_down(self, L, KG=4, tiles=TILES):
        A = self.A
        A.mark()
        hid = self.scr['hid']
        hT = self.scr['hT']
        wdn = self.W['ffn_w_down'][L]
        ng = 44 // KG
        self.wpool(2, KG * 1024)
        hg = [A.alloc([KG, TT], BF16) for _ in range(2)]
        bhg = [Buf(), Buf()]
        acc = A.alloc([8, TT], F32)
        bacc = [Buf('acc%d' % i) for i in range(8)]
        hrow = [A.alloc([TT], F32) for _ in range(2)]
        bhr = [Buf(), Buf()]
        n = 0
        for half in range(2):
            for kg in range(ng):
                h_, bh_ = hg[n % 2], bhg[n % 2]
                n += 1
                self.dma_sp(h_, hid[kg * KG:(kg + 1) * KG].rearrange('c p t -> p c t'), self.bhid[kg * KG:(kg + 1) * KG], [bh_])
                wt, wb = self.wload(wdn[kg * KG * 128:(kg + 1) * KG * 128, half * 1024:(half + 1) * 1024], KG, 1024)
                for o in range(8):
                    for ti, (t0, tl) in enumerate(tiles):
                        bk, bb = self.bank()
                        self.mm(bk[:, 0:tl], [(wt[:, kc, o * 128:(o + 1) * 128], h_[:, kc, t0:t0 + tl]) for kc in range(KG)], [wb, bh_], bb)
                        if kg == 0:
                            self.act(lambda e, bk=bk, o=o, t0=t0, tl=tl: e.activation(out=acc[:, o, t0:t0 + tl], in_=bk[:, 0:tl], func=AF.Copy),
                                     [bb], [bacc[o]])
                        else:
                            self.dve(lambda e, bk=bk, o=o, t0=t0, tl=tl: e.tensor_tensor(out=acc[:, o, t0:t0 + tl], in0=bk[:, 0:tl],
                                                                                      in1=acc[:, o, t0:t0 + tl], op=ALU.add), [bb, bacc[o]], [bacc[o]])
            for o in range(8):
                oc = half * 8 + o
                hr, bh = hrow[o % 2], bhr[o % 2]
                self.dma_sp(hr, hT[oc], [self.bhT[oc]], [bh])
                for (lo, hi, col) in (((0, NLAT, 0), (NLAT, TT, 1)) if len(tiles) == 5 else ((0, NLAT, 0),)):
                    self.dve(lambda e, hr=hr, o=o, oc=oc, lo=lo, hi=hi, col=col: e.scalar_tensor_tensor(
                        out=hr[:, lo:hi], in0=acc[:, o, lo:hi], scalar=self.mvec(L, 5, oc, col), in1=hr[:, lo:hi],
                        op0=ALU.mult, op1=ALU.add), [bacc[o], bh, self.bmod], [bh])
                self.dma_sp(hT[oc], hr, [bh], [self.bhT[oc]])
        A.release()
        self.S.barrier()

    def phase_final(self):
        A = self.A
        A.mark()
        hT = self.scr['hT']
        ht = [A.alloc([16, 512], F32) for _ in range(2)]
        bh = [Buf(), Buf()]
        sq = A.alloc([16, 512], BF16)
        bsq = Buf()
        rstd = A.alloc([512], F32)
        brs = Buf()
        yn = [A.alloc([16, 512], F32) for _ in range(2)]
        byn = [Buf(), Buf()]
        ot = [A.alloc([D], F32) for _ in range(3)]
        bot = [Buf() for _ in range(3)]
        ones = self.C('ones', True)
        identf = self.C('ident')
        gfin = self.P('gfin')
        k = 0
        for ti, (t0, tl) in enumerate(TILES[:4]):
            hh, bhh = ht[ti % 2], bh[ti % 2]
            y_, by_ = yn[ti % 2], byn[ti % 2]
            self.dma_sp(hh, hT[:, :, t0:t0 + tl].rearrange('c p t -> p c t'), self.bhT, [bhh])
            self.act(lambda e, hh=hh: e.activation(out=sq, in_=hh, func=AF.Square), [bhh], [bsq])
            bk, bb = self.bank()
            self.mm(bk, [(ones, sq[:, c, :]) for c in range(16)], [bsq, self.bcst], bb)
            self.act(lambda e, bk=bk: e.activation(out=rstd, in_=bk, func=AF.Sqrt, bias=self.C('eps6'), scale=1.0 / D), [bb, self.bcst], [brs])
            self.dve(lambda e: e.reciprocal(out=rstd, in_=rstd), [brs], [brs])
            for c in range(16):
                self.dve(lambda e, hh=hh, y_=y_, c=c: e.scalar_tensor_tensor(out=y_[:, c, :], in0=hh[:, c, :], scalar=gfin[:, c:c + 1],
                                                                       in1=rstd, op0=ALU.mult, op1=ALU.mult), [bhh, brs, self.bpk], [by_])
            for b in range(4):
                o_, bo_ = ot[k % 3], bot[k % 3]
                k += 1
                for fg in range(4):
                    bk, bb = self.bank()
                    for f4 in range(4):
                        c = fg * 4 + f4
                        self.pe(lambda e, bk=bk, y_=y_, c=c, b=b, f4=f4: e.transpose(out=bk[:, f4 * 128:(f4 + 1) * 128],
                                                                                      in_=y_[:, c, b * 128:(b + 1) * 128], identity=identf),
                                [by_, self.bcst], [bb])
                    if fg % 2 == 0:
                        self.act(lambda e, bk=bk, o_=o_, fg=fg: e.activation(out=o_[:, fg * 512:(fg + 1) * 512], in_=bk, func=AF.Copy), [bb], [bo_])
                    else:
                        self.dve(lambda e, bk=bk, o_=o_, fg=fg: e.tensor_copy(out=o_[:, fg * 512:(fg + 1) * 512], in_=bk), [bb], [bo_])
                r0 = t0 + b * 128
                self.dma_sp(self.out[r0:r0 + 128, :], o_, [bo_], [self.bout])
        A.release()
        self.S.barrier()

    def phase_ab_qkv(self, uT, buT, qT, kT, vtok, bq, bkk, bv):
        A = self.A
        A.mark()
        w_in = self.W['ab_w_in'][0]
        rope = A.alloc([2, NLAT], F32)
        brope = Buf()
        self.dma_sp(rope, self.roped.rearrange('a p t -> p a t'), [], [brope])
        self.wpool(2, 16 * 256)
        ws_ap = [A.alloc([16 * 256], BF16) for _ in range(2)]
        bws = [Buf(), Buf()]
        sq = [A.alloc([512], BF16) for _ in range(2)]
        bsq = [Buf(), Buf()]
        rstd = [A.alloc([512], F32) for _ in range(2)]
        brs = [Buf(), Buf()]
        t1 = [A.alloc([512], F32) for _ in range(2)]
        bt1 = [Buf(), Buf()]
        t2 = [A.alloc([512], F32) for _ in range(2)]
        bt2 = [Buf(), Buf()]
        ones = self.C('ones', True)
        k = 0
        groups = [(g * 256, 2, qT, bq, 'qg', 'qgs', 2 * g) for g in range(4)] + [(1024, 2, kT, bkk, 'kg', 'kgs', 0)]
        for gi, (col0, nh, dst, bdst, gn, gsn, h0) in enumerate(groups):
            n = nh * 128
            wt, wb = self.wload(w_in[:, col0:col0 + n], 16, n)
            ws = ws_ap[gi % 2][:, 0:16 * n].rearrange('p (c n) -> p c n', n=n)
            bw_ = bws[gi % 2]
            wtv = wt.rearrange('p c (g b e) -> p (c g) b e', b=2, e=32)
            wsv = ws.rearrange('p c (g b e) -> p (c g) b e', b=2, e=32)
            for b in range(2):
                self.pool(lambda e, wsv=wsv, wtv=wtv, b=b: e.tensor_copy(out=wsv[:, :, b, :], in_=wtv[:, :, 1 - b, :]), [wb], [bw_])
            g_ap, gs_ap = self.P(gn), self.P(gsn)
            for hh in range(nh):
                hd = h0 + hh
                for ti, (t0, tl) in enumerate(TILES):
                    i2 = k % 2
                    k += 1
                    bkq, bbq = self.bank()
                    self.mm(bkq[:, 0:tl], [(wt[:, c, hh * 128:(hh + 1) * 128], uT[:, c, t0:t0 + tl]) for c in range(16)], [wb, buT], bbq)
                    self.act(lambda e, bkq=bkq, i2=i2, tl=tl: e.activation(out=sq[i2][:, 0:tl], in_=bkq[:, 0:tl], func=AF.Square), [bbq], [bsq[i2]])
                    bks, bbs = self.bank()
                    self.mm(bks[:, 0:tl], [(ones, sq[i2][:, 0:tl])], [bsq[i2], self.bcst], bbs)
                    self.act(lambda e, bks=bks, i2=i2, tl=tl: e.activation(out=rstd[i2][:, 0:tl], in_=bks[:, 0:tl], func=AF.Sqrt,
                                                                          bias=self.C('eps6'), scale=1.0 / 128), [bbs, self.bcst], [brs[i2]])
                    self.dve(lambda e, i2=i2, tl=tl: e.reciprocal(out=rstd[i2][:, 0:tl], in_=rstd[i2][:, 0:tl]), [brs[i2]], [brs[i2]])
                    if ti < 4:
                        bkw, bbw = self.bank()
                        self.mm(bkw[:, 0:tl], [(ws[:, c, hh * 128:(hh + 1) * 128], uT[:, c, t0:t0 + tl]) for c in range(16)], [bw_, buT], bbw)
                        self.dve(lambda e, bkq=bkq, i2=i2, t0=t0, tl=tl, g_ap=g_ap: e.scalar_tensor_tensor(
                            out=t1[i2][:, 0:tl], in0=bkq[:, 0:tl], scalar=g_ap, in1=rope[:, 0, t0:t0 + tl], op0=ALU.mult, op1=ALU.mult),
                            [bbq, brope, self.bpk], [bt1[i2]])
                        self.dve(lambda e, bkw=bkw, i2=i2, t0=t0, tl=tl, gs_ap=gs_ap: e.scalar_tensor_tensor(
                            out=t2[i2][:, 0:tl], in0=bkw[:, 0:tl], scalar=gs_ap, in1=rope[:, 1, t0:t0 + tl], op0=ALU.mult, op1=ALU.mult),
                            [bbw, brope, self.bpk], [bt2[i2]])
                        self.pool(lambda e, i2=i2, tl=tl: e.tensor_tensor(out=t1[i2][:, 0:tl], in0=t1[i2][:, 0:tl], in1=t2[i2][:, 0:tl], op=ALU.add),
                                  [bt1[i2], bt2[i2]], [bt1[i2]])
                        self.pool(lambda e, i2=i2, dst=dst, hd=hd, t0=t0, tl=tl: e.tensor_tensor(
                            out=dst[:, hd, t0:t0 + tl], in0=t1[i2][:, 0:tl], in1=rstd[i2][:, 0:tl], op=ALU.mult), [bt1[i2], brs[i2]], [bdst])
                    else:
                        self.dve(lambda e, bkq=bkq, i2=i2, dst=dst, hd=hd, t0=t0, tl=tl, g_ap=g_ap: e.scalar_tensor_tensor(
                            out=dst[:, hd, t0:t0 + tl], in0=bkq[:, 0:tl], scalar=g_ap, in1=rstd[i2][:, 0:tl], op0=ALU.mult, op1=ALU.mult),
                            [bbq, brs[i2], self.bpk], [bdst])
        wv, bwv = self.wload(w_in[:, 1280:1536], 16, 256)
        for tb in range(18):
            bk, bb = self.bank()
            self.mm(bk[:, 0:256], [(uT[:, c, tb * 128:(tb + 1) * 128], wv[:, c, :]) for c in range(16)], [bwv, buT], bb)
            self.act(lambda e, bk=bk, tb=tb: e.activation(out=vtok[:, tb, :], in_=bk[:, 0:256], func=AF.Copy), [bb], [bv])
        A.release()
        self.S.barrier()

    def phase_ab_conv(self, uT, buT):
        A = self.A
        A.mark()
        w_in = self.W['ab_w_in'][0]
        cv = self.scratch('cv', [8, 128, TT], F32)
        self.bcv = [Buf('cv%d' % c) for c in range(8)]
        GP = 2364
        self.wpool(4, 16 * 512)
        gpad = [A.alloc([GP], BF16) for _ in range(2)]
        bgp = [Buf(), Buf()]
        dg = [A.alloc([31, 128], BF16) for _ in range(2)]
        bdg = [Buf(), Buf()]
        sig = [A.alloc([512], F32) for _ in range(2)]
        bsig = [Buf(), Buf()]
        cvr = [A.alloc([TT], F32) for _ in range(2)]
        bcr = [Buf(), Buf()]
        for i in range(2):
            self.pool(lambda e, i=i: e.memset(gpad[i], 0.0), [], [bgp[i]])
        acw = self.P('acw').rearrange('p (c j) -> p c j', j=31)
        acb = self.P('acb')
        identb = self.C('ident', True)
        k = 0
        for g in range(2):
            wa, bwa = self.wload(w_in[:, 1536 + g * 512:1536 + (g + 1) * 512], 16, 512)
            wb_, bwb = self.wload(w_in[:, 2560 + g * 512:2560 + (g + 1) * 512], 16, 512)
            for c4 in range(4):
                cc = g * 4 + c4
                gp, bg = gpad[cc % 2], bgp[cc % 2]
                dd, bd = dg[cc % 2], bdg[cc % 2]
                cr, bc = cvr[cc % 2], bcr[cc % 2]
                for j in range(31):
                    self.pool(lambda e, dd=dd, j=j, cc=cc: e.tensor_scalar(out=dd[:, j, :], in0=identb, scalar1=acw[:, cc, j:j + 1],
                                                                          scalar2=None, op0=ALU.mult), [self.bcst, self.bpk], [bd])
                for ti, (t0, tl) in enumerate(TILES):
                    off = 15 + t0 if ti < 4 else 2093
                    i2 = k % 2
                    k += 1
                    bka, bba = self.bank()
                    self.mm(bka[:, 0:tl], [(wa[:, kc, c4 * 128:(c4 + 1) * 128], uT[:, kc, t0:t0 + tl]) for kc in range(16)], [bwa, buT], bba)
                    bkb, bbb = self.bank()
                    self.mm(bkb[:, 0:tl], [(wb_[:, kc, c4 * 128:(c4 + 1) * 128], uT[:, kc, t0:t0 + tl]) for kc in range(16)], [bwb, buT], bbb)
                    self.act(lambda e, bkb=bkb, i2=i2, tl=tl: e.activation(out=sig[i2][:, 0:tl], in_=bkb[:, 0:tl], func=AF.Sigmoid), [bbb], [bsig[i2]])
                    self.dve(lambda e, bka=bka, i2=i2, gp=gp, off=off, tl=tl: e.tensor_tensor(out=gp[:, off:off + tl], in0=bka[:, 0:tl],
                                                                                           in1=sig[i2][:, 0:tl], op=ALU.mult), [bba, bsig[i2]], [bg])
                for ti, (t0, tl) in enumerate(TILES):
                    base = t0 if ti < 4 else 2078
                    bk, bb = self.bank()
                    self.mm(bk[:, 0:tl], [(dd[:, j, :], gp[:, base + j:base + j + tl]) for j in range(31)], [bd, bg], bb)
                    self.act(lambda e, bk=bk, cr=cr, t0=t0, tl=tl, cc=cc: e.activation(out=cr[:, t0:t0 + tl], in_=bk[:, 0:tl], func=AF.Identity,
                                                                                    bias=acb[:, cc:cc + 1], scale=1.0), [bb, self.bpk], [bc])
                self.dma_sp(cv[cc], cr, [bc], [self.bcv[cc]])
        A.release()
        self.S.barrier()

    def phase_att(self, qT, kT, vtok, bq, bkk, bv, yin, byin):
        A = self.A
        A.mark()
        pT = [A.alloc([512], BF16) for _ in range(4)]
        bp = [Buf() for _ in range(4)]
        rinv = [A.alloc([512], F32) for _ in range(2)]
        bri = [Buf(), Buf()]
        ones = self.C('ones', True)
        scale = 128.0 ** -0.5
        k = 0
        n = 0
        for h in range(8):
            hk = h // 4
            for ti, (t0, tl) in enumerate(TILES):
                chunks = list(range(18)) if ti < 4 else [16, 17]
                ai = 2 * (n % 2)
                bko, bbo = self.banks[ai], self.bbufs[ai]
                bkr, bbr = self.banks[ai + 1], self.bbufs[ai + 1]
                nck = len(chunks)
                for ci, kc in enumerate(chunks):
                    bks, bbs = self.banks[4 + k % 4], self.bbufs[4 + k % 4]
                    self.mm(bks[:, 0:tl], [(kT[:, hk, kc * 128:(kc + 1) * 128], qT[:, h, t0:t0 + tl])], [bkk, bq], bbs)
                    p_, bp_ = pT[k % 4], bp[k % 4]
                    k += 1
                    self.act(lambda e, bks=bks, p_=p_, tl=tl: e.activation(out=p_[:, 0:tl], in_=bks[:, 0:tl], func=AF.Exp, scale=scale), [bbs], [bp_])
                    self.pe(lambda e, bko=bko, p_=p_, kc=kc, hk=hk, tl=tl, ci=ci, nck=nck: e.matmul(
                        bko[:, 0:tl], lhsT=vtok[:, kc, hk * 128:(hk + 1) * 128], rhs=p_[:, 0:tl], start=(ci == 0), stop=(ci == nck - 1)),
                        [bv, bp_], [bbo])
                    self.pe(lambda e, bkr=bkr, p_=p_, tl=tl, ci=ci, nck=nck: e.matmul(
                        bkr[:, 0:tl], lhsT=ones, rhs=p_[:, 0:tl], start=(ci == 0), stop=(ci == nck - 1)), [self.bcst, bp_], [bbr])
                ri, bri_ = rinv[n % 2], bri[n % 2]
                n += 1
                self.dve(lambda e, bkr=bkr, ri=ri, tl=tl: e.reciprocal(out=ri[:, 0:tl], in_=bkr[:, 0:tl]), [bbr], [bri_])
                self.dve(lambda e, bko=bko, ri=ri, h=h, t0=t0, tl=tl: e.tensor_tensor(out=yin[:, h, t0:t0 + tl], in0=bko[:, 0:tl],
                                                                                  in1=ri[:, 0:tl], op=ALU.mult), [bbo, bri_], [byin])
        A.release()
        self.S.barrier()

    def phase_conv_ln(self, yin, byin):
        A = self.A
        A.mark()
        cv = self.scr['cv']
        ct = [A.alloc([8, 512], F32) for _ in range(2)]
        bct = [Buf(), Buf()]
        sq = A.alloc([8, 512], F32)
        bsq = Buf()
        mean = A.alloc([512], F32)
        msq = A.alloc([512], F32)
        rstd = A.alloc([512], F32)
        bst = Buf()
        xc = [A.alloc([512], F32) for _ in range(2)]
        bxc = [Buf(), Buf()]
        onesf = self.C('ones')
        ang, anb = self.P('ang'), self.P('anb')
        k = 0
        for ti, (t0, tl) in enumerate(TILES):
            c_, bc_ = ct[ti % 2], bct[ti % 2]
            self.dma_sp(c_[:, :, 0:tl], cv[:, :, t0:t0 + tl].rearrange('c p t -> p c t'), self.bcv, [bc_])
            self.act(lambda e, c_=c_, tl=tl: e.activation(out=sq[:, :, 0:tl], in_=c_[:, :, 0:tl], func=AF.Square), [bc_], [bsq])
            bk1, bb1 = self.bank()
            self.mm(bk1[:, 0:tl], [(onesf, c_[:, c, 0:tl]) for c in range(8)], [bc_, self.bcst], bb1)
            bk2, bb2 = self.bank()
            self.mm(bk2[:, 0:tl], [(onesf, sq[:, c, 0:tl]) for c in range(8)], [bsq, self.bcst], bb2)
            self.act(lambda e, bk1=bk1, tl=tl: e.activation(out=mean[:, 0:tl], in_=bk1[:, 0:tl], func=AF.Copy, scale=1.0 / 1024), [bb1], [bst])
            self.dve(lambda e, tl=tl: e.tensor_tensor(out=msq[:, 0:tl], in0=mean[:, 0:tl], in1=mean[:, 0:tl], op=ALU.mult), [bst], [bst])
            self.dve(lambda e, bk2=bk2, tl=tl: e.scalar_tensor_tensor(out=rstd[:, 0:tl], in0=bk2[:, 0:tl], scalar=1.0 / 1024, in1=msq[:, 0:tl],
                                                                      op0=ALU.mult, op1=ALU.subtract), [bb2, bst], [bst])
            self.act(lambda e, tl=tl: e.activation(out=rstd[:, 0:tl], in_=rstd[:, 0:tl], func=AF.Sqrt, bias=self.C('eps6'), scale=1.0), [bst, self.bcst], [bst])
            self.dve(lambda e, tl=tl: e.reciprocal(out=rstd[:, 0:tl], in_=rstd[:, 0:tl]), [bst], [bst])
            for c in range(8):
                x_, bx_ = xc[k % 2], bxc[k % 2]
                k += 1
                self.dve(lambda e, x_=x_, c_=c_, c=c, tl=tl: e.tensor_tensor(out=x_[:, 0:tl], in0=c_[:, c, 0:tl], in1=mean[:, 0:tl], op=ALU.subtract),
                         [bc_, bst], [bx_])
                self.pool(lambda e, x_=x_, tl=tl: e.tensor_tensor(out=x_[:, 0:tl], in0=x_[:, 0:tl], in1=rstd[:, 0:tl], op=ALU.mult), [bx_, bst], [bx_])
                self.act(lambda e, x_=x_, c=c, t0=t0, tl=tl: e.activation(out=yin[:, 8 + c, t0:t0 + tl], in_=x_[:, 0:tl], func=AF.Silu,
                                                                        bias=anb[:, c:c + 1], scale=ang[:, c:c + 1]), [bx_, self.bpk], [byin])
        A.release()
        self.S.barrier()

    def build(self, stop_after=None, start_layer=0):
        A = self.A
        self.phase_mod()
        self.phase_in()
        if start_layer == 1:
            return self.build_l1()
        A.mark()
        uT = A.alloc([16, TT], BF16)
        buT = Buf('uT')
        self.phase_norm(0, 0, uT, buT)
        self.phase_ab_conv(uT, buT)
        qT = A.alloc([8, TT], BF16)
        kT = A.alloc([2, TT], BF16)
        vtok = A.alloc([18, 256], BF16)
        bq, bkk, bv = Buf('q'), Buf('k'), Buf('v')
        self.phase_ab_qkv(uT, buT, qT, kT, vtok, bq, bkk, bv)
        yin, byin = uT, buT
        self.phase_att(qT, kT, vtok, bq, bkk, bv, yin, byin)
        A.release()
        A.mark()
        yin = A.alloc([16, TT], BF16)
        self.phase_conv_ln(yin, byin)
        self.phase_resid_proj(0, 2, yin, byin, 16, self.W['ab_w_out'][0])
        if stop_after == 'mix0':
            A.release()
            return self.finish()
        self.phase_norm(0, 1, yin, byin)
        self.phase_ffn_up(0, yin, byin)
        A.release()
        self.phase_ffn_down(0)
        if stop_after == 'l0':
            return self.finish()
        return self.build_l1()

    def build_dbg(self, stop_after):
        A = self.A
        A.mark()
        uT = A.alloc([16, TT], BF16)
        buT = Buf('uT1')
        A.mark()
        tmp = [A.alloc([16, 512], F32) for _ in range(2)]
        bt = [Buf(), Buf()]
        for ti, (t0, tl) in enumerate(TILES):
            self.dma_sp(tmp[ti % 2][:, :, 0:tl], self.u1T[:, :, t0:t0 + tl].rearrange('c p t -> p c t'), [], [bt[ti % 2]])
            self.dve(lambda e, ti=ti, t0=t0, tl=tl: e.tensor_copy(out=uT[:, :, t0:t0 + tl], in_=tmp[ti % 2][:, :, 0:tl]), [bt[ti % 2]], [buT])
        A.release()
        self.S.barrier()
        self.phase_cd_proj(uT, buT)
        A.release()
        if stop_after == 'cdproj':
            return self.finish()
        self.phase_rwkv()
        if stop_after == 'rwkv':
            return self.finish()
        self.phase_ret()
        return self.finish()

    def build_l1(self):
        A = self.A
        LT = TILES[:4]
        A.mark()
        uT = A.alloc([16, TT], BF16)
        buT = Buf('uT1')
        self.phase_norm(1, 0, uT, buT)
        self.phase_cd_proj(uT, buT)
        A.release()
        self.phase_rwkv()
        self.phase_ret()
        A.mark()
        yin = A.alloc([24, NLAT], BF16)
        byin = Buf('yin1')
        self.dma_sp(yin, self.scr['yin'].rearrange('c p t -> p c t'), self.byin_s, [byin])
        self.phase_resid_proj(1, 2, yin, byin, 24, self.W['cd_w_out'][0], tiles=LT)
        A.release()
        A.mark()
        fT = A.alloc([16, TT], BF16)
        bfT = Buf('fT1')
        self.phase_norm(1, 1, fT, bfT, tiles=LT)
        self.phase_ffn_up(1, fT, bfT, tiles=LT)
        A.release()
        self.phase_ffn_down(1, tiles=LT)
        self.phase_final()
        return self.finish()

    def finish(self):
        self.S.barrier()
        self.S.emit(self.nc)
        return self.nc


def make_in_maps(inp):
    pk = pack_params(inp).array()
    cst = const_pack().array()
    rope = rope_tables()
    maps = []
    for b in range(8):
        cc = np.stack([fm(inp['c'][b]), fm(inp['c_ctx'])], axis=2).reshape(128, 32)
        m = {'x': np.ascontiguousarray(inp['x'][b]), 'ctx': np.ascontiguousarray(inp['ctx'][b]), 'cc': np.ascontiguousarray(cc),
             'pk': pk, 'cst': cst, 'rope': rope}
        for k in WEIGHT_NAMES:
            m[k] = np.ascontiguousarray(np.asarray(inp[k], np.float32))
        maps.append(m)
    return maps


def kernel(**inputs):
    inp = {k: np.asarray(v) for k, v in inputs.items()}
    mk = MK()
    nc = mk.build()
    res = run_bass_kernel_spmd(nc, make_in_maps(inp), core_ids=list(range(8)))
    return np.stack([r['out'] for r in res.results], axis=0).astype(np.float32)
```
